# Optimizing a Trainium2 kernel written in Bass

```python
import math
import jax, jax.numpy as jnp
from jax import lax
import numpy as np

D_MODEL = 1024
BATCH = 8
SEQ = 8192
DEPTH = 2
DEC_BATCH = 8
DEC_SEQ = 64
PAST_LEN = 2048

CHUNK = 64
Q_BLOCK = 128
N_EVEN = (DEPTH + 1) // 2
N_ODD = DEPTH // 2
D_FF = 2816
EPS = 1e-6

M_HEADS = 4
M_DH = 128
M_W = M_HEADS * M_DH
A_HEADS = 4
A_NOPE = 128
A_ROPE = 64
A_VDIM = 128
Q_LORA = 384
KV_LORA = 256
ROPE_THETA = 10000.0
MLA_SCALE = (A_NOPE + A_ROPE) ** -0.5
S5_W = 512
S5_GSIZE = 16
S5_GROUPS = S5_W // S5_GSIZE
S5_P = 64
DT_MIN = 0.001
DT_MAX = 0.1
SB_HEADS = 8
SB_DH = 64
SB_W = SB_HEADS * SB_DH

EVEN_IN = 4 * M_W + 2 * M_HEADS + Q_LORA + KV_LORA + A_ROPE
EVEN_OUT = M_W + A_HEADS * A_VDIM
ODD_IN = S5_W + 3 * SB_W
ODD_OUT = S5_W + SB_W

kernel_name = 'hybrid_stream_encoder_step'


def rmsnorm(x, g):
    xf = x.astype(jnp.float32)
    y = xf * lax.rsqrt(jnp.mean(xf * xf, axis=-1, keepdims=True) + EPS)
    return (y * g.astype(jnp.float32)).astype(x.dtype)


def swiglu(x, w_gate, w_up, w_down):
    return (jax.nn.silu(x @ w_gate) * (x @ w_up)) @ w_down


def split_cols(z, sizes):
    parts, start = [], 0
    for s in sizes:
        parts.append(z[..., start:start + s])
        start += s
    return parts


def rope(x, pos):
    half = A_ROPE // 2
    inv_freq = ROPE_THETA ** (-jnp.arange(half, dtype=jnp.float32) / half)
    ang = pos.astype(jnp.float32)[:, None] * inv_freq[None, :]
    shape = (pos.shape[0],) + (1,) * (x.ndim - 3) + (half,)
    cos, sin = jnp.cos(ang).reshape(shape), jnp.sin(ang).reshape(shape)
    xf = x.astype(jnp.float32)
    x1, x2 = xf[..., :half], xf[..., half:]
    return jnp.concatenate([x1 * cos - x2 * sin, x1 * sin + x2 * cos], axis=-1).astype(x.dtype)


def blockwise(attend_fn, q_parts, q_pos, kv_parts, k_pos):
    L = q_pos.shape[0]
    nb = L // Q_BLOCK

    def to_blocks(a):
        return jnp.moveaxis(a.reshape((a.shape[0], nb, Q_BLOCK) + a.shape[2:]), 1, 0)

    def one(args):
        qp = args[-1]
        return attend_fn(*args[:-1], qp, *kv_parts, k_pos)

    out = lax.map(one, tuple(to_blocks(a) for a in q_parts) + (q_pos.reshape(nb, Q_BLOCK),))
    out = jnp.moveaxis(out, 0, 1)
    return out.reshape((out.shape[0], L) + out.shape[3:])


def mlstm_chunk(carry, inp):
    C, n, m = carry
    q, k, v, ig, lf = inp
    L = q.shape[2]
    b = jnp.cumsum(lf, axis=-1)
    causal = jnp.tril(jnp.ones((L, L), dtype=bool))
    log_d = jnp.where(causal, b[..., :, None] - b[..., None, :] + ig[..., None, :], -jnp.inf)
    m_inter = b + m[..., None]
    m_t = jnp.maximum(m_inter, jnp.max(log_d, axis=-1))
    w_intra = jnp.exp(log_d - m_t[..., None])
    w_inter = jnp.exp(m_inter - m_t)
    s = jnp.einsum('bhtd,bhsd->bhts', q, k) * w_intra
    num = w_inter[..., None] * jnp.einsum('bhvk,bhtk->bhtv', C, q) + jnp.einsum('bhts,bhsv->bhtv', s, v)
    den = w_inter * jnp.einsum('bhk,bhtk->bht', n, q) + jnp.sum(s, axis=-1)
    h = num / jnp.maximum(jnp.abs(den), jnp.exp(-m_t))[..., None]
    m_new = m_t[..., -1]
    w_s = jnp.exp(b[..., -1:] - b + ig - m_new[..., None])
    decay = jnp.exp(b[..., -1] + m - m_new)
    C_new = decay[..., None, None] * C + jnp.einsum('bhs,bhsv,bhsk->bhvk', w_s, v, k)
    n_new = decay[..., None] * n + jnp.einsum('bhs,bhsk->bhk', w_s, k)
    return (C_new, n_new, m_new), h


def mlstm_mix(q, k, v, ig, lf, state):
    f32 = jnp.float32
    Bn, L = q.shape[:2]
    q = q.astype(f32).transpose(0, 2, 1, 3)
    k = (k.astype(f32) * M_DH ** -0.5).transpose(0, 2, 1, 3)
    v = v.astype(f32).transpose(0, 2, 1, 3)
    ig = ig.astype(f32).transpose(0, 2, 1)
    lf = lf.astype(f32).transpose(0, 2, 1)
    if state is None:
        nc = L // CHUNK

        def split(a):
            return jnp.moveaxis(a.reshape(a.shape[:2] + (nc, CHUNK) + a.shape[3:]), 2, 0)

        init = (jnp.zeros((Bn, M_HEADS, M_DH, M_DH), f32), jnp.zeros((Bn, M_HEADS, M_DH), f32),
                jnp.zeros((Bn, M_HEADS), f32))
        final, h = lax.scan(mlstm_chunk, init, tuple(split(a) for a in (q, k, v, ig, lf)))
        h = jnp.moveaxis(h, 0, 2).reshape(Bn, M_HEADS, L, M_DH)
    else:
        final, h = mlstm_chunk(tuple(s.astype(f32) for s in state), (q, k, v, ig, lf))
    return h.transpose(0, 2, 1, 3), final


def expand_latent(latent, w_ukv):
    Bn, L, _ = latent.shape
    kv = (latent @ w_ukv).reshape(Bn, L, A_HEADS, A_NOPE + A_VDIM)
    return kv[..., :A_NOPE], kv[..., A_NOPE:]


def mla_attend(q_nope, q_rope, q_pos, k_nope, k_rope, v, k_pos):
    s = (jnp.einsum('bqhd,bkhd->bhqk', q_nope, k_nope)
         + jnp.einsum('bqhd,bkd->bhqk', q_rope, k_rope)).astype(jnp.float32) * MLA_SCALE
    visible = (k_pos // CHUNK)[None, :] <= (q_pos // CHUNK)[:, None]
    p = jax.nn.softmax(jnp.where(visible, s, -jnp.inf), axis=-1)
    return jnp.einsum('bhqk,bkhd->bqhd', p, v.astype(jnp.float32))


def s5_discretize(A_re, A_im, B_re, B_im, log_dt):
    f32 = jnp.float32
    dt = jnp.exp(log_dt.astype(f32))[:, None]
    ar, ai = A_re.astype(f32), A_im.astype(f32)
    mag = jnp.exp(dt * ar)
    ab_r, ab_i = mag * jnp.cos(dt * ai), mag * jnp.sin(dt * ai)
    den = ar * ar + ai * ai
    c_r = ((ab_r - 1.0) * ar + ab_i * ai) / den
    c_i = (ab_i * ar - (ab_r - 1.0) * ai) / den
    br, bi = B_re.astype(f32), B_im.astype(f32)
    bb_r = c_r[..., None] * br - c_i[..., None] * bi
    bb_i = c_r[..., None] * bi + c_i[..., None] * br
    return ab_r, ab_i, bb_r, bb_i


def ssm_combine(e1, e2):
    a1r, a1i, b1r, b1i = e1
    a2r, a2i, b2r, b2i = e2
    return (a2r * a1r - a2i * a1i, a2r * a1i + a2i * a1r,
            a2r * b1r - a2i * b1i + b2r, a2r * b1i + a2i * b1r + b2i)


def s5_mix(u, A_re, A_im, B_re, B_im, C_re, C_im, D, log_dt, w_glu, state):
    f32 = jnp.float32
    Bn, L, _ = u.shape
    uf = u.astype(f32)
    ab_r, ab_i, bb_r, bb_i = s5_discretize(A_re, A_im, B_re, B_im, log_dt)
    ug = uf.reshape(Bn, L, S5_GROUPS, S5_GSIZE)
    bu_r = jnp.einsum('blgc,gpc->blgp', ug, bb_r)
    bu_i = jnp.einsum('blgc,gpc->blgp', ug, bb_i)
    a_r = jnp.broadcast_to(ab_r, (1, L, S5_GROUPS, S5_P))
    a_i = jnp.broadcast_to(ab_i, (1, L, S5_GROUPS, S5_P))
    pa_r, pa_i, x_r, x_i = lax.associative_scan(ssm_combine, (a_r, a_i, bu_r, bu_i), axis=1)
    if state is not None:
        x0r, x0i = state[0].astype(f32)[:, None], state[1].astype(f32)[:, None]
        x_r, x_i = x_r + pa_r * x0r - pa_i * x0i, x_i + pa_r * x0i + pa_i * x0r
    y = (jnp.einsum('blgp,gcp->blgc', x_r, C_re.astype(f32))
         - jnp.einsum('blgp,gcp->blgc', x_i, C_im.astype(f32)))
    y = y.reshape(Bn, L, S5_W) + D.astype(f32) * uf
    g = jax.nn.gelu(y)
    out = g * jax.nn.sigmoid(g @ w_glu.astype(f32))
    return out, (x_r[:, -1], x_i[:, -1])


def sb_attend(q, q_pos, k, v, k_pos):
    z = jnp.einsum('bqhd,bkhd->bhqk', q, k).astype(jnp.float32) * SB_DH ** -0.5
    causal = k_pos[None, :] < q_pos[:, None]
    log_1mb = jnp.where(causal, jax.nn.log_sigmoid(-z), 0.0)
    log_after = lax.cumsum(log_1mb, axis=3, reverse=True) - log_1mb
    log_w = jnp.where(causal, jax.nn.log_sigmoid(z) + log_after, -jnp.inf)
    return jnp.einsum('bhqk,bkhd->bqhd', jnp.exp(log_w), v.astype(jnp.float32))


def even_mixer(h, pos, w_in, w_out, b_igate, b_fgate, out_norm, q_norm, kv_norm, w_uq, w_ukv,
               mem_state, kv_cache):
    f32 = jnp.float32
    Bn, L, _ = h.shape
    q_m, k_m, v_m, o_m, i_pre, f_pre, c_q, c_kv, k_r = split_cols(
        h @ w_in, [M_W, M_W, M_W, M_W, M_HEADS, M_HEADS, Q_LORA, KV_LORA, A_ROPE])
    heads = (Bn, L, M_HEADS, M_DH)
    h_m, new_mem = mlstm_mix(q_m.reshape(heads), k_m.reshape(heads), v_m.reshape(heads),
                             i_pre.astype(f32) + b_igate.astype(f32),
                             jax.nn.log_sigmoid(f_pre.astype(f32) + b_fgate.astype(f32)), mem_state)
    h_m = rmsnorm(h_m, out_norm.reshape(M_HEADS, M_DH)).reshape(Bn, L, M_W)
    h_m = jax.nn.sigmoid(o_m.astype(f32)) * h_m
    q = (rmsnorm(c_q, q_norm) @ w_uq).reshape(Bn, L, A_HEADS, A_NOPE + A_ROPE)
    q_nope, q_rope = q[..., :A_NOPE], rope(q[..., A_NOPE:], pos)
    latent, k_rope = rmsnorm(c_kv, kv_norm), rope(k_r, pos)
    if kv_cache is None:
        k_nope, v = expand_latent(latent, w_ukv)
        h_a = blockwise(mla_attend, (q_nope, q_rope), pos, (k_nope, k_rope, v), pos)
    else:
        lat_all = jnp.concatenate([kv_cache[0].astype(latent.dtype), latent], axis=1)
        kr_all = jnp.concatenate([kv_cache[1].astype(k_rope.dtype), k_rope], axis=1)
        k_nope, v = expand_latent(lat_all, w_ukv)
        h_a = mla_attend(q_nope, q_rope, pos, k_nope, kr_all, v, jnp.arange(lat_all.shape[1]))
    mixed = jnp.concatenate([h_m, h_a.reshape(Bn, L, A_HEADS * A_VDIM)], axis=-1).astype(h.dtype)
    return mixed @ w_out, new_mem, (latent, k_rope)


def odd_mixer(h, pos, w_in, w_out, A_re, A_im, B_re, B_im, C_re, C_im, D, log_dt, w_glu,
              ssm_state, kv_cache):
    Bn, L, _ = h.shape
    u, q, k, v = split_cols(h @ w_in, [S5_W, SB_W, SB_W, SB_W])
    h_s, new_ssm = s5_mix(u, A_re, A_im, B_re, B_im, C_re, C_im, D, log_dt, w_glu, ssm_state)
    heads = (Bn, L, SB_HEADS, SB_DH)
    q, k, v = q.reshape(heads), k.reshape(heads), v.reshape(heads)
    if kv_cache is None:
        h_b = blockwise(sb_attend, (q,), pos, (k, v), pos)
    else:
        k_all = jnp.concatenate([kv_cache[0].astype(k.dtype), k], axis=1)
        v_all = jnp.concatenate([kv_cache[1].astype(v.dtype), v], axis=1)
        h_b = sb_attend(q, pos, k_all, v_all, jnp.arange(k_all.shape[1]))
    mixed = jnp.concatenate([h_s, h_b.reshape(Bn, L, SB_W)], axis=-1).astype(h.dtype)
    return mixed @ w_out, new_ssm, (k, v)


def setup_inputs(seed: int = 0) -> dict:
    key = jax.random.key(seed)
    ks = jax.random.split(key, 48)
    keys = iter([ks[i] for i in range(48)])
    f32 = jnp.float32

    def nrm(shape, scale):
        return jax.random.normal(next(keys), shape, f32) * scale

    def gain(shape):
        return 1.0 + 0.01 * jax.random.normal(next(keys), shape, f32)

    s5_log_dt = jax.random.uniform(next(keys), (N_ODD, S5_GROUPS), f32,
                                   math.log(DT_MIN), math.log(DT_MAX))
    return dict(
        x_prompt=nrm((BATCH, SEQ, D_MODEL), 1.0),
        x_sample=nrm((DEC_BATCH, DEC_SEQ, D_MODEL), 1.0),
        cache_mla_latent=nrm((N_EVEN, DEC_BATCH, PAST_LEN, KV_LORA), 1.0),
        cache_mla_krope=nrm((N_EVEN, DEC_BATCH, PAST_LEN, A_ROPE), 1.0),
        state_mlstm_C=nrm((N_EVEN, DEC_BATCH, M_HEADS, M_DH, M_DH), 0.3),
        state_mlstm_n=nrm((N_EVEN, DEC_BATCH, M_HEADS, M_DH), 0.3),
        state_mlstm_m=nrm((N_EVEN, DEC_BATCH, M_HEADS), 1.0),
        state_s5_re=nrm((N_ODD, DEC_BATCH, S5_GROUPS, S5_P), 0.3),
        state_s5_im=nrm((N_ODD, DEC_BATCH, S5_GROUPS, S5_P), 0.3),
        cache_sb_k=nrm((N_ODD, DEC_BATCH, PAST_LEN, SB_HEADS, SB_DH), 1.0),
        cache_sb_v=nrm((N_ODD, DEC_BATCH, PAST_LEN, SB_HEADS, SB_DH), 1.0),
        norm_ffn=gain((DEPTH, 2, D_MODEL)),
        norm_mix=gain((DEPTH, D_MODEL)),
        norm_final=gain((D_MODEL,)),
        ffn_w_gate=nrm((DEPTH, 2, D_MODEL, D_FF), D_MODEL ** -0.5),
        ffn_w_up=nrm((DEPTH, 2, D_MODEL, D_FF), D_MODEL ** -0.5),
        ffn_w_down=nrm((DEPTH, 2, D_FF, D_MODEL), D_FF ** -0.5),
        even_w_in=nrm((N_EVEN, D_MODEL, EVEN_IN), D_MODEL ** -0.5),
        even_w_out=nrm((N_EVEN, EVEN_OUT, D_MODEL), EVEN_OUT ** -0.5),
        mlstm_b_igate=nrm((N_EVEN, M_HEADS), 0.1),
        mlstm_b_fgate=jnp.linspace(3.0, 6.0, M_HEADS, dtype=f32)[None, :] + nrm((N_EVEN, M_HEADS), 0.1),
        mlstm_out_norm=gain((N_EVEN, M_W)),
        mla_q_norm=gain((N_EVEN, Q_LORA)),
        mla_kv_norm=gain((N_EVEN, KV_LORA)),
        mla_w_uq=nrm((N_EVEN, Q_LORA, A_HEADS * (A_NOPE + A_ROPE)), Q_LORA ** -0.5),
        mla_w_ukv=nrm((N_EVEN, KV_LORA, A_HEADS * (A_NOPE + A_VDIM)), KV_LORA ** -0.5),
        odd_w_in=nrm((N_ODD, D_MODEL, ODD_IN), D_MODEL ** -0.5),
        odd_w_out=nrm((N_ODD, ODD_OUT, D_MODEL), ODD_OUT ** -0.5),
        s5_A_re=-0.5 + nrm((N_ODD, S5_GROUPS, S5_P), 0.01),
        s5_A_im=math.pi * jnp.arange(S5_P, dtype=f32) + nrm((N_ODD, S5_GROUPS, S5_P), 0.01),
        s5_B_re=nrm((N_ODD, S5_GROUPS, S5_P, S5_GSIZE), (2 * S5_GSIZE) ** -0.5),
        s5_B_im=nrm((N_ODD, S5_GROUPS, S5_P, S5_GSIZE), (2 * S5_GSIZE) ** -0.5),
        s5_C_re=nrm((N_ODD, S5_GROUPS, S5_GSIZE, S5_P), S5_P ** -0.5),
        s5_C_im=nrm((N_ODD, S5_GROUPS, S5_GSIZE, S5_P), S5_P ** -0.5),
        s5_D=nrm((N_ODD, S5_W), 1.0),
        s5_log_dt=s5_log_dt,
        s5_w_glu=nrm((N_ODD, S5_W, S5_W), S5_W ** -0.5),
    )


def reference(x_prompt, x_sample, cache_mla_latent, cache_mla_krope, state_mlstm_C, state_mlstm_n,
              state_mlstm_m, state_s5_re, state_s5_im, cache_sb_k, cache_sb_v, norm_ffn, norm_mix,
              norm_final, ffn_w_gate, ffn_w_up, ffn_w_down, even_w_in, even_w_out, mlstm_b_igate,
              mlstm_b_fgate, mlstm_out_norm, mla_q_norm, mla_kv_norm, mla_w_uq, mla_w_ukv, odd_w_in,
              odd_w_out, s5_A_re, s5_A_im, s5_B_re, s5_B_im, s5_C_re, s5_C_im, s5_D, s5_log_dt,
              s5_w_glu):
    past = cache_sb_k.shape[2]
    pos_p = jnp.arange(x_prompt.shape[1])
    pos_s = past + jnp.arange(x_sample.shape[1])

    def trunk(x, pos, use_cache):
        lat, krp, mem_c, mem_n, mem_m, ssm_re, ssm_im, sb_k, sb_v = ([] for _ in range(9))
        for l in range(DEPTH):
            j = l // 2
            h = rmsnorm(x, norm_ffn[l, 0])
            x = x + (0.5 * swiglu(h, ffn_w_gate[l, 0], ffn_w_up[l, 0], ffn_w_down[l, 0])).astype(x.dtype)
            h = rmsnorm(x, norm_mix[l])
            if l % 2 == 0:
                mem = (state_mlstm_C[j], state_mlstm_n[j], state_mlstm_m[j]) if use_cache else None
                kv = (cache_mla_latent[j], cache_mla_krope[j]) if use_cache else None
                out, (c_new, n_new, m_new), (lat_new, kr_new) = even_mixer(
                    h, pos, even_w_in[j], even_w_out[j], mlstm_b_igate[j], mlstm_b_fgate[j],
                    mlstm_out_norm[j], mla_q_norm[j], mla_kv_norm[j], mla_w_uq[j], mla_w_ukv[j], mem, kv)
                mem_c.append(c_new)
                mem_n.append(n_new)
                mem_m.append(m_new)
                lat.append(lat_new)
                krp.append(kr_new)
            else:
                ssm = (state_s5_re[j], state_s5_im[j]) if use_cache else None
                kv = (cache_sb_k[j], cache_sb_v[j]) if use_cache else None
                out, (re_new, im_new), (k_new, v_new) = odd_mixer(
                    h, pos, odd_w_in[j], odd_w_out[j], s5_A_re[j], s5_A_im[j], s5_B_re[j], s5_B_im[j],
                    s5_C_re[j], s5_C_im[j], s5_D[j], s5_log_dt[j], s5_w_glu[j], ssm, kv)
                ssm_re.append(re_new)
                ssm_im.append(im_new)
                sb_k.append(k_new)
                sb_v.append(v_new)
            x = x + out.astype(x.dtype)
            h = rmsnorm(x, norm_ffn[l, 1])
            x = x + (0.5 * swiglu(h, ffn_w_gate[l, 1], ffn_w_up[l, 1], ffn_w_down[l, 1])).astype(x.dtype)
        return (rmsnorm(x, norm_final), jnp.stack(lat), jnp.stack(krp), jnp.stack(mem_c),
                jnp.stack(mem_n), jnp.stack(mem_m), jnp.stack(ssm_re), jnp.stack(ssm_im),
                jnp.stack(sb_k), jnp.stack(sb_v))

    (y_prompt, lat_p, kr_p, c_p, n_p, m_p, re_p, im_p, k_p, v_p) = trunk(x_prompt, pos_p, False)
    (y_sample, lat_s, kr_s, c_s, n_s, m_s, re_s, im_s, k_s, v_s) = trunk(x_sample, pos_s, True)
    return (y_prompt, y_sample, lat_p, kr_p, c_p, n_p, m_p, re_p, im_p, k_p, v_p,
            lat_s, kr_s, c_s, n_s, m_s, re_s, im_s, k_s, v_s)
```

```python
import os
import numpy as np
import concourse.bass as bass
import concourse.mybir as mybir
from concourse.bass_utils import run_bass_kernel_spmd

F32 = mybir.dt.float32
BF16 = mybir.dt.bfloat16
I32 = mybir.dt.int32
ALU = mybir.AluOpType
AF = mybir.ActivationFunctionType
AX = mybir.AxisListType

CENG = ['pe', 'act', 'dve', 'pool']
ALLENG = CENG + ['sp']
EIDX = {e: i for i, e in enumerate(ALLENG)}
BLK = 32
DBLK = 2048
N_DSEM = 48
SEM_ROLL = 30000


def _esize(dt):
    return {F32: 4, BF16: 2, I32: 4}.get(dt, None) or mybir.dt.size(dt)


class Op:
    __slots__ = ('eng', 'fn', 'deps', 'id', 'pos', 'signal', 'semgen', 'semval', 'waits',
                 'is_dma', 'dk', 'sk', 'prev_dma')

    def __init__(self, eng, fn, is_dma):
        self.eng = eng
        self.fn = fn
        self.is_dma = is_dma
        self.deps = set()
        self.signal = False
        self.waits = []
        self.pos = -1
        self.sk = None
        self.prev_dma = None


class Space:
    def __init__(self, nblocks):
        self.last_w = np.full(nblocks, -1, dtype=np.int64)
        self.last_r = np.full((nblocks, len(CENG)), -1, dtype=np.int64)
        self.dma_readers = []


class KB:
    def __init__(self):
        self.nc = bass.Bass("TRN2", target_bir_lowering=False)
        self.ops = []
        self.spaces = {}
        self.untracked = set()
        self.n_dma = 0
        self.dma_ops = []
        self.dry = False
        self.keyw = {}
        self.last_compute = None

    def _range(self, ap):
        sp = str(ap.space)
        t = ap.tensor
        name = ap.name
        es = _esize(ap.dtype)
        if 'DRAM' in sp:
            return None
            lo = ap.offset
            hi = ap.offset
            for st, cnt in ap.ap:
                if st >= 0:
                    hi += st * (cnt - 1)
                else:
                    lo += st * (cnt - 1)
            key = 'D:' + name
            if key not in self.spaces:
                n = 1
                for s in t.shape:
                    n *= s
                self.spaces[key] = Space((n * es + DBLK - 1) // DBLK)
            return key, (lo * es) // DBLK, (hi * es + es - 1) // DBLK + 1
        rowlen = ap.ap[0][0]
        if rowlen == 0:
            rowlen = t.shape[1] if len(t.shape) == 2 else int(np.prod(t.shape[1:]))
        off = ap.offset % rowlen
        lo = off
        hi = off
        for st, cnt in ap.ap[1:]:
            if st >= 0:
                hi += st * (cnt - 1)
            else:
                lo += st * (cnt - 1)
        key = ('P:' if 'PSUM' in sp else 'S:') + name
        if key not in self.spaces:
            self.spaces[key] = Space((rowlen * es + BLK - 1) // BLK)
        return key, (lo * es) // BLK, (hi * es + es - 1) // BLK + 1

    def _hazards(self, op, reads, writes):
        deps = op.deps
        slot = EIDX[op.eng] if (not op.is_dma and op.eng in CENG) else None
        for ap in reads:
            r = self._range(ap)
            if r is None:
                continue
            key, lo, hi = r
            s = self.spaces[key]
            deps.update(np.unique(s.last_w[lo:hi]).tolist())
            if slot is not None:
                s.last_r[lo:hi, slot] = op.id
            else:
                s.dma_readers.append((op.id, lo, hi))
        for ap in writes:
            r = self._range(ap)
            if r is None:
                continue
            key, lo, hi = r
            s = self.spaces[key]
            deps.update(np.unique(s.last_w[lo:hi]).tolist())
            deps.update(np.unique(s.last_r[lo:hi]).tolist())
            if s.dma_readers:
                keep = []
                for (oid, l2, h2) in s.dma_readers:
                    if l2 < hi and lo < h2:
                        deps.add(oid)
                    else:
                        keep.append((oid, l2, h2))
                s.dma_readers = keep
            s.last_w[lo:hi] = op.id
            s.last_r[lo:hi] = -1
        deps.discard(-1)
        deps.discard(op.id)

    def op(self, eng, fn, reads, writes, is_dma=False, rkeys=(), wkeys=()):
        if self.dry:
            return None
        o = Op(eng, fn, is_dma)
        o.id = len(self.ops)
        self.ops.append(o)
        self._hazards(o, reads, writes)
        for k in rkeys:
            o.deps.update(self.keyw.get(k, ()))
        for k in wkeys:
            self.keyw.setdefault(k, []).append(o.id)
        if eng == 'pe':
            o.deps = {d for d in o.deps if self.ops[d].eng != 'pe' or self.ops[d].is_dma}
        if os.environ.get("FW_SERIAL") and o.id > 0:
            o.deps.add(o.id - 1)
        if not is_dma and not os.environ.get("FW_NOCSERIAL"):
            pc = self.last_compute
            if pc is not None and not (eng == 'pe' and pc.eng == 'pe'):
                o.deps.add(pc.id)
            self.last_compute = o
        if is_dma:
            o.dk = self.n_dma
            self.n_dma += 1
            self.dma_ops.append(o)
            if o.dk >= N_DSEM:
                o.prev_dma = self.dma_ops[o.dk - N_DSEM]
                o.deps.add(o.prev_dma.id)
        return o

    def dma(self, out, in_, eng='sp', rkeys=(), wkeys=(), **kw):
        if True:
            eng = 'sp'
        return self.op(eng, lambda e: e.dma_start(out=out, in_=in_, **kw), [in_], [out], is_dma=True,
                       rkeys=rkeys, wkeys=wkeys)

    def mm(self, out, lhsT, rhs, start=True, stop=True, **kw):
        return self.op('pe', lambda e: e.matmul(out, lhsT, rhs, start=start, stop=stop, **kw),
                       [lhsT, rhs], [out])

    def tr(self, out, in_, ident):
        return self.op('pe', lambda e: e.transpose(out, in_, ident), [in_, ident], [out])

    def act(self, out, in_, func, bias=None, scale=1.0, accum_out=None, eng='act'):
        rd = [in_]
        kw = {}
        if bias is not None:
            kw['bias'] = bias
            if not isinstance(bias, (int, float)):
                rd.append(bias)
        if not isinstance(scale, (int, float)):
            rd.append(scale)
        wr = [out]
        if accum_out is not None:
            kw['accum_out'] = accum_out
            wr.append(accum_out)
        return self.op(eng, lambda e: e.activation(out=out, in_=in_, func=func, scale=scale, **kw), rd, wr)

    def tt(self, out, in0, in1, op, eng='dve'):
        return self.op(eng, lambda e: e.tensor_tensor(out=out, in0=in0, in1=in1, op=op), [in0, in1], [out])

    def ts(self, out, in0, s1, op0, s2=None, op1=None, eng='dve', accum_out=None):
        rd = [in0]
        if not isinstance(s1, (int, float)):
            rd.append(s1)
        if s2 is not None and not isinstance(s2, (int, float)):
            rd.append(s2)
        kw = {}
        if op1 is not None:
            kw['op1'] = op1
        wr = [out]
        if accum_out is not None:
            kw['accum_out'] = accum_out
            wr.append(accum_out)
        return self.op(eng, lambda e: e.tensor_scalar(out=out, in0=in0, scalar1=s1, scalar2=s2, op0=op0, **kw), rd, wr)

    def stt(self, out, in0, scalar, in1, op0, op1, eng='dve'):
        rd = [in0, in1]
        if not isinstance(scalar, (int, float)):
            rd.append(scalar)
        return self.op(eng, lambda e: e.scalar_tensor_tensor(out=out, in0=in0, scalar=scalar, in1=in1, op0=op0, op1=op1), rd, [out])

    def copy(self, out, in_, eng='dve'):
        if eng == 'act':
            return self.act(out, in_, AF.Copy)
        return self.op(eng, lambda e: e.tensor_copy(out=out, in_=in_), [in_], [out])

    def memset(self, out, val, eng='dve'):
        return self.op(eng, lambda e: e.memset(out, val), [], [out])

    def scan(self, out, d0, d1, initial, op0, op1):
        rd = [d0, d1]
        if not isinstance(initial, (int, float)):
            rd.append(initial)
        return self.op('dve', lambda e: e.tensor_tensor_scan(out=out, data0=d0, data1=d1, initial=initial, op0=op0, op1=op1), rd, [out])

    def reduce(self, out, in_, op, axis=AX.X, eng='dve'):
        return self.op(eng, lambda e: e.tensor_reduce(out=out, in_=in_, axis=axis, op=op), [in_], [out])

    def recip(self, out, in_):
        return self.op('dve', lambda e: e.reciprocal(out=out, in_=in_), [in_], [out])

    def analyze(self):
        nce = len(CENG)
        cur = {e: np.zeros(nce, dtype=np.int64) for e in ALLENG}
        dknown = {e: set() for e in ALLENG}
        count = {e: 0 for e in CENG}
        byeng = {e: [] for e in CENG}
        for o in self.ops:
            e = o.eng
            c = cur[e]
            dk = dknown[e]
            need = {}
            for d in o.deps:
                dop = self.ops[d]
                if dop.is_dma:
                    if d not in dk or os.environ.get("FW_NOPRUNE"):
                        dk.add(d)
                        o.waits.append(dop)
                        np.maximum(c, dop.sk, out=c)
                else:
                    b = EIDX[dop.eng]
                    if dop.pos + 1 > need.get(b, 0):
                        need[b] = dop.pos + 1
            noprune = bool(os.environ.get("FW_NOPRUNE"))
            for b, p in sorted(need.items(), key=lambda kv: -kv[1]):
                if c[b] >= p and not noprune:
                    continue
                t = byeng[CENG[b]][p - 1]
                t.signal = True
                o.waits.append(t)
                np.maximum(c, t.sk, out=c)
                c[b] = max(c[b], p)
            o.sk = c.copy()
            if not o.is_dma and e in CENG:
                o.pos = count[e]
                count[e] += 1
                byeng[e].append(o)
        self.ngen = {}
        for e in CENG:
            n = 0
            for o in byeng[e]:
                if o.signal:
                    o.semgen = n // SEM_ROLL
                    o.semval = n % SEM_ROLL + 1
                    n += 1
            self.ngen[e] = n // SEM_ROLL + 1
        self.byeng = byeng

    def emit(self, stack):
        nc = self.nc
        esem = {e: [stack.enter_context(nc.semaphore(f"s_{e}{g}")) for g in range(self.ngen[e])] for e in CENG}
        dsem = [stack.enter_context(nc.semaphore(f"d{i}")) for i in range(min(N_DSEM, max(self.n_dma, 1)))]
        block = stack.enter_context(nc.Block())
        ops = self.ops
        n_dma = self.n_dma

        def stream(ename):
            def body(eng):
                for o in ops:
                    if o.eng != ename:
                        continue
                    for w in o.waits:
                        if w.is_dma:
                            eng.wait_ge(dsem[w.dk % N_DSEM], 16 * (w.dk // N_DSEM + 1))
                        else:
                            eng.wait_ge(esem[w.eng][w.semgen], w.semval)
                    ins = o.fn(eng)
                    if o.is_dma:
                        ins.then_inc(dsem[o.dk % N_DSEM], 16)
                    elif o.signal:
                        ins.then_inc(esem[o.eng][o.semgen], 1)
                if ename == 'sp':
                    for i in range(min(N_DSEM, n_dma)):
                        cnt = (n_dma - 1 - i) // N_DSEM + 1
                        eng.wait_ge(dsem[i], 16 * cnt)
            return body

        block.sync(stream('sp'))
        block.tensor(stream('pe'))
        block.scalar(stream('act'))
        block.vector(stream('dve'))
        block.gpsimd(stream('pool'))

    def stats(self):
        from collections import Counter
        c = Counter(o.eng + ('_dma' if o.is_dma else '') for o in self.ops)
        w = sum(len(o.waits) for o in self.ops)
        s = sum(1 for o in self.ops if o.signal)
        return dict(c), w, s

from contextlib import ExitStack

D = 1024
DFF = 2816
NJ = 22
EPS = 1e-6
MAGIC = 12582912.0
TWO_PI = float(2 * np.pi)
ARENA_F = 47104


def bc(ap, shape):
    return ap.to_broadcast(list(shape))


class Arena:
    def __init__(self, base):
        self.base = base
        self.top = 0
        self.marks = []
        self.peak = 0

    def push(self):
        self.marks.append(self.top)

    def pop(self):
        self.top = self.marks.pop()

    def alloc(self, dims, dt=F32, parts=128):
        if isinstance(dims, int):
            dims = (dims,)
        n = int(np.prod(dims))
        es = 4 if dt == F32 else 2
        nb = (n * es + 63) // 64 * 64
        lo = self.top
        self.top += nb
        self.peak = max(self.peak, self.top)
        assert self.top <= ARENA_F * 4, f"arena overflow {self.top}"
        a = self.base[:, lo // 4:(lo + nb) // 4]
        if dt != F32:
            a = a.bitcast(dt)
        a = a[0:parts, 0:n]
        if len(dims) == 2:
            a = a.rearrange("p (a b) -> p a b", a=dims[0])
        elif len(dims) == 3:
            a = a.rearrange("p (a b c) -> p a b c", a=dims[0], b=dims[1])
        return a


class Ring:
    def __init__(self, bld, nbytes):
        self.b = bld
        self.size = nbytes
        self.region = bld.ar.alloc(nbytes // 4)
        self.plan = bld.plan
        self.dry = bld.kb.dry
        self.newplan = []
        self.issued = 0
        self.cons = 0
        self.head = 0
        self.live = []
        self.aps = {}
        self.held = set()

    def _view(self, lo, parts, n):
        a = self.region[:, lo // 4:(lo + (n * 2 + 3) // 4 * 4) // 4].bitcast(BF16)
        return a[0:parts, 0:n]

    def _try_issue(self):
        src, parts, n, rkeys = self.plan[self.issued]
        if any(k not in self.b.done for k in rkeys):
            return False
        nb = (n * 2 + 63) // 64 * 64
        lo = self.head
        if lo + nb > self.size:
            lo = 0
        hi = lo + nb
        for (_, l2, h2) in self.live:
            if l2 < hi and lo < h2:
                return False
        ap = self._view(lo, parts, n)
        dst = ap
        if len(src.shape) == 3:
            dst = ap.rearrange("p (a b) -> p a b", a=src.shape[1])
        self.b.kb.dma(dst, src, rkeys=rkeys)
        self.live.append((self.issued, lo, hi))
        self.aps[self.issued] = ap
        self.head = hi
        self.issued += 1
        return True

    def unhold(self):
        self.held = set()

    def fetch(self, src, rkeys=(), hold=False):
        parts, n = src.shape[0], int(np.prod(src.shape[1:]))
        if self.dry:
            self.newplan.append((src, parts, n, tuple(rkeys)))
            return self._view(0, parts, n)
        k = self.cons
        if hold:
            self.held.add(k)
        self.live = [x for x in self.live if x[0] >= k or x[0] in self.held]
        while self.issued < len(self.plan) and (self.issued - k) < 24:
            if not self._try_issue():
                break
        assert self.issued > k, "ring too small"
        ps, pp, pn, _ = self.plan[k]
        assert (pp, pn) == (parts, n), "plan mismatch"
        self.cons += 1
        return self.aps.pop(k)


class Group:
    pass


class Builder:
    def __init__(self, S, PAST, plan=None):
        self.S, self.PAST = S, PAST
        self.plan = plan
        self.kb = KB()
        self.kb.dry = plan is None
        self.nc = self.kb.nc
        self.st = ExitStack()
        self.din = {}
        self.dout = {}
        self._evac_i = 0
        self.ps_i = 0
        self.ps_res = set()
        self.debug = False

    def inp(self, name, shape, dt=F32):
        t = self.nc.dram_tensor(name, list(shape), dt, kind="ExternalInput").ap()
        self.din[name] = t
        return t

    def outp(self, name, shape):
        t = self.nc.dram_tensor(name, list(shape), F32, kind="ExternalOutput").ap()
        self.dout[name] = t
        return t

    def scratch(self, name, shape, dt=BF16):
        return self.nc.dram_tensor(name, list(shape), dt).ap()

    def dbg(self, name, ap, dims):
        if name not in self.dout:
            self.outp(name, [128] + list(dims))
        self.ar.push()
        t = self.ar.alloc(dims)
        self.kb.copy(t, ap)
        self.kb.dma(self.dout[name], t)
        self.ar.pop()

    def psum(self):
        while True:
            i = self.ps_i % 8
            self.ps_i += 1
            if i not in self.ps_res:
                return self.ps[i]

    def reserve(self, n):
        out = []
        for i in range(8):
            if i not in self.ps_res and len(out) < n:
                self.ps_res.add(i)
                out.append(i)
        return out

    def release(self, idxs):
        for i in idxs:
            self.ps_res.discard(i)

    def evac(self, out, in_, scale=None):
        self._evac_i += 1
        if self._evac_i % 2 == 0:
            if scale is None:
                self.kb.copy(out, in_, eng='dve')
            else:
                self.kb.ts(out, in_, float(scale), ALU.mult)
        else:
            self.kb.act(out, in_, AF.Copy, scale=1.0 if scale is None else float(scale))

    def declare(self):
        S, PAST = self.S, self.PAST
        i = self.inp
        self.xp = i("xp", [S, D]); self.xs = i("xs", [64, D])
        self.c_lat = i("c_lat", [PAST, 256]); self.c_kr = i("c_kr", [PAST, 64])
        self.st_C = i("st_C", [4, 128, 128]); self.st_n = i("st_n", [4, 128]); self.st_m = i("st_m", [1, 4])
        self.st_re = i("st_re", [16, 128]); self.st_im = i("st_im", [16, 128])
        self.c_sbk = i("c_sbk", [PAST, 512]); self.c_sbv = i("c_sbv", [PAST, 512])
        self.w_gate = i("w_gate", [4, D, DFF]); self.w_up = i("w_up", [4, D, DFF]); self.w_down = i("w_down", [4, DFF, D])
        self.e_win = i("e_win", [D, 2760]); self.e_wout = i("e_wout", [D, D])
        self.wuq_d = i("wuq", [384, 768]); self.wukv_d = i("wukv", [256, 1024])
        self.o_win = i("o_win", [D, 2048]); self.o_wout = i("o_wout", [D, D]); self.wglu_d = i("wglu", [512, 512])
        self.grows = i("grows", [69, 128]); self.bi_d = i("bi", [1, 4]); self.bf_d = i("bf", [1, 4])
        self.s5rows = i("s5rows", [48, 128])
        self.s5B = [i("s5B_re", [2048, 16]), i("s5B_im", [2048, 16])]
        self.s5C = [i("s5C_re", [512, 64]), i("s5C_im", [512, 64])]
        self.c_ident = i("c_ident", [128, 128]); self.c_masksb = i("c_masksb", [128, 128]); self.c_maskml = i("c_maskml", [64, 64])
        self.c_uincl = i("c_uincl", [128, 128]); self.c_sel = i("c_sel", [8, 1024]); self.c_cm = i("c_cm", [128, 512])
        self.c_iota = i("c_iota", [128, 128])
        self.c_ropep = [i("c_cosp", [64, S]), i("c_sinp", [64, S])]
        self.c_ropes = [i("c_coss", [64, 64]), i("c_sins", [64, 64])]
        o = self.outp
        self.groups = []
        for g, (n, nm) in enumerate([(S, "p"), (64, "s")]):
            G = Group()
            G.name = nm; G.n = n
            G.T = 512 if nm == "p" else 64
            G.ntile = n // G.T
            G.npast = 0 if nm == "p" else PAST // 512
            G.pos0 = 0 if nm == "p" else PAST
            G.x = self.xp if nm == "p" else self.xs
            G.rope = self.c_ropep if nm == "p" else self.c_ropes
            G.y = o("y_" + nm, [n, D]); G.lat = o("lat_" + nm, [n, 256]); G.kr = o("kr_" + nm, [n, 64])
            G.oC = o("C_" + nm, [4, 128, 128]); G.on = o("n_" + nm, [4, 128]); G.om = o("m_" + nm, [1, 4])
            G.ore = o("re_" + nm, [16, 128]); G.oim = o("im_" + nm, [16, 128])
            G.ok = o("k_" + nm, [n, 512]); G.ov = o("v_" + nm, [n, 512])
            NT = G.npast + G.ntile
            G.KnT = self.scratch("KnT_" + nm, [NT, 128, 2048]); G.KrT = self.scratch("KrT_" + nm, [NT, 64, 512])
            G.V = self.scratch("V_" + nm, [NT, 128, 2048])
            G.sKT = self.scratch("sKT_" + nm, [NT, 128, 2048]); G.sV = self.scratch("sV_" + nm, [NT, 128, 2048])
            self.groups.append(G)
        sc = self.scratch
        self.wgu = sc("wgu", [4, NJ, 128, 2048]); self.wdn = sc("wdn", [4, 8, 128, DFF])
        self.winF_e = sc("winF_e", [19, 128, 1024]); self.winT_e = sc("winT_e", [2, 128, 4096]); self.wout_e = sc("wout_e", [8, 128, 1024])
        self.winF_o = sc("winF_o", [12, 128, 1024]); self.winT_o = sc("winT_o", [2, 128, 4096]); self.wout_o = sc("wout_o", [8, 128, 1024])
        self.s5tab = sc("s5tab", [128, 4096 + 4096], F32)
        self.s5mat = sc("s5mat", [128, 8192], BF16)
        nc = self.nc
        self.arena = self.st.enter_context(nc.sbuf_tensor("arena", [128, ARENA_F], F32))
        self.ps = [self.st.enter_context(nc.psum_tensor(f"ps{k}", [128, 512], F32)) for k in range(8)]

    def load_consts(self):
        kb, ar = self.kb, self.ar
        A = ar.alloc
        self.ident = A(128); kb.dma(self.ident, self.c_ident)
        t = A(128); kb.dma(t, self.c_masksb)
        self.masksb = A(128, BF16); kb.copy(self.masksb, t)
        self.maskml = A(64, parts=64); kb.dma(self.maskml, self.c_maskml)
        t2 = A(128); kb.dma(t2, self.c_uincl)
        self.uincl = A(128, BF16); kb.copy(self.uincl, t2)
        self.ones = A(128, BF16); kb.memset(self.ones, 1.0)
        self.negones = A(128, BF16); kb.memset(self.negones, -1.0)
        self.onef = A(8); kb.memset(self.onef, 1.0)
        self.sel = A((8, 128), parts=8); kb.dma(self.sel, self.c_sel.rearrange("r (g p) -> r g p", g=8))
        self.cm = A(512); kb.dma(self.cm, self.c_cm)
        gr = A(128); kb.memset(gr, 0.0); kb.dma(gr[0:69, :], self.grows)
        p = self.psum()
        kb.tr(p[:, 0:128], gr, self.ident)
        self.gains = A(72); kb.copy(self.gains[:, 0:69], p[:, 0:69])
        self.bi = A(4); kb.dma(self.bi, bc(self.bi_d[0:1, :], [128, 4]))
        bfp = A(4); kb.dma(bfp, bc(self.bf_d[0:1, :], [128, 4]))
        self.nbf = A(4); kb.ts(self.nbf, bfp, -1.0, ALU.mult)
        self.Cst = A((4, 129)); self.mst = A(4); self.wst = A((16, 2))
        self.wuq = A((3, 768), BF16); self.wukv = A((2, 1024), BF16); self.wglu = A((4, 512), BF16)

    def conv_rows(self, src, ncols, stores, wkeys):
        kb, ar = self.kb, self.ar
        st = self.cv_f[self.cv_i % 2]; sb = self.cv_b[self.cv_i % 2]
        self.cv_i += 1
        kb.dma(st[:, 0:ncols], src)
        h = (ncols // 2 + 63) // 64 * 64
        engs = ['act', 'dve', 'pool']
        e1 = engs[self.cv_i % 3]; e2 = engs[(self.cv_i + 1) % 3]
        kb.copy(sb[:, 0:h], st[:, 0:h], eng=e1)
        kb.copy(sb[:, h:ncols], st[:, h:ncols], eng=e2)
        for dst, c0, c1, shape in stores:
            v = sb[:, c0:c1]
            if shape is not None:
                v = v.rearrange("p (a b) -> p a b", a=shape[0])
            kb.dma(dst, v, eng='pool', wkeys=wkeys)

    def prologue_weights(self):
        kb, ar = self.kb, self.ar
        ar.push()
        self.cv_f = [ar.alloc(DFF) for _ in range(2)]
        self.cv_b = [ar.alloc(DFF, BF16) for _ in range(2)]
        self.cv_i = 0
        for f in range(4):
            for kc in range(8):
                for g, W in enumerate((self.w_gate, self.w_up)):
                    dst = self.wgu[f].rearrange("j p x -> p j x")[:, :, kc * 256 + g * 128: kc * 256 + g * 128 + 128]
                    self.conv_rows(W[f, kc * 128:(kc + 1) * 128, :], DFF, [(dst, 0, DFF, (NJ, 128))], [("wgu", f)])
            for jc in range(NJ):
                dst = self.wdn[f].rearrange("m p x -> p m x")[:, :, jc * 128:(jc + 1) * 128]
                self.conv_rows(self.w_down[f, jc * 128:(jc + 1) * 128, :], D, [(dst, 0, D, (8, 128))], [("wdn", f)])
        for kc in range(8):
            F = self.winF_e.rearrange("n p x -> p n x")
            ks = slice(kc * 128, (kc + 1) * 128)
            stores = [(F[:, 0:4, ks], 0, 512, (4, 128)), (F[:, 4:8, ks], 512, 1024, (4, 128)),
                      (F[:, 8:12, ks], 1536, 2048, (4, 128)), (F[:, 12:15, ks], 2056, 2440, (3, 128)),
                      (F[:, 15:17, ks], 2440, 2696, (2, 128)),
                      (self.winF_e[17][:, kc * 128:kc * 128 + 64], 2696, 2760, None),
                      (self.winF_e[18][:, kc * 128:kc * 128 + 8], 2048, 2056, None),
                      (self.winT_e[0][:, kc * 512:(kc + 1) * 512], 512, 1024, None),
                      (self.winT_e[1][:, kc * 512:(kc + 1) * 512], 1024, 1536, None)]
            self.conv_rows(self.e_win[ks, :], 2760, stores, [("we",)])
            Fo = self.winF_o.rearrange("n p x -> p n x")
            stores = [(Fo[:, 0:12, ks], 0, 1536, (12, 128)),
                      (self.winT_o[0][:, kc * 512:(kc + 1) * 512], 1024, 1536, None),
                      (self.winT_o[1][:, kc * 512:(kc + 1) * 512], 1536, 2048, None)]
            self.conv_rows(self.o_win[ks, :], 2048, stores, [("wo",)])
            for W, dstt, key in ((self.e_wout, self.wout_e, "we"), (self.o_wout, self.wout_o, "wo")):
                dst = dstt.rearrange("m p x -> p m x")[:, :, ks]
                self.conv_rows(W[ks, :], D, [(dst, 0, D, (8, 128))], [(key,)])
        for kc in range(3):
            stg = self.cv_f[self.cv_i % 2]; self.cv_i += 1
            kb.dma(stg[:, 0:768], self.wuq_d[kc * 128:(kc + 1) * 128, :])
            kb.copy(self.wuq[:, kc, :], stg[:, 0:768], eng='dve')
        for kc in range(2):
            stg = self.cv_f[self.cv_i % 2]; self.cv_i += 1
            kb.dma(stg[:, 0:1024], self.wukv_d[kc * 128:(kc + 1) * 128, :])
            kb.copy(self.wukv[:, kc, :], stg[:, 0:1024], eng='act')
        for kc in range(4):
            stg = self.cv_f[self.cv_i % 2]; self.cv_i += 1
            kb.dma(stg[:, 0:512], self.wglu_d[kc * 128:(kc + 1) * 128, :])
            kb.copy(self.wglu[:, kc, :], stg[:, 0:512], eng='dve')
        ar.pop()

    def rmsnorm(self, xT, C, T, gcol, out, dim):
        kb, ar = self.kb, self.ar
        ar.push()
        sq = ar.alloc((C, T), BF16)
        kb.act(sq, xT, AF.Square)
        p = self.psum()
        for c in range(C):
            kb.mm(p[:, 0:T], self.ones, sq[:, c, :], start=(c == 0), stop=(c == C - 1))
        rs = ar.alloc(T)
        kb.act(rs, p[:, 0:T], AF.Sqrt, scale=1.0 / dim, bias=self.epsc[:, 0:1])
        kb.recip(rs, rs)
        for c in range(C):
            kb.stt(out[:, c, :], xT[:, c, :], self.gains[:, gcol + c:gcol + c + 1], rs, ALU.mult, ALU.mult)
        ar.pop()

    def transpose_out(self, srcT, rows, T, dst, eng_out='pool'):
        kb, ar = self.kb, self.ar
        tb = min(128, T); nb = T // tb
        tot = sum(r for _, r in srcT)
        ar.push()
        stg = ar.alloc((nb, tot), parts=tb)
        c0 = 0
        for ap, r in srcT:
            p = self.psum()
            if r < 128 and tb < 128:
                tmp = ar.alloc(128, parts=r)
                kb.memset(tmp, 0.0)
                kb.copy(tmp[:, 0:tb], ap[0:r, 0:tb])
                kb.tr(p[:, 0:r], tmp, self.ident[0:r, 0:r])
            else:
                for b in range(nb):
                    kb.tr(p[0:tb, b * r:(b + 1) * r], ap[0:r, b * tb:(b + 1) * tb], self.ident[0:r, 0:r])
            self.evac(stg[:, :, c0:c0 + r], p[0:tb, 0:nb * r].rearrange("p (b r) -> p b r", b=nb))
            c0 += r
        kb.dma(dst.rearrange("(b p) f -> p b f", p=tb), stg, eng=eng_out)
        ar.pop()

    def load_xT(self, G, ti, xT):
        kb, ar = self.kb, self.ar
        T = G.T; tb = min(128, T); nb = T // tb
        ar.push()
        stg = ar.alloc((nb, D), parts=tb)
        kb.dma(stg, G.x[ti * T:(ti + 1) * T, :].rearrange("(b p) f -> p b f", p=tb))
        for c in range(8):
            p = self.psum()
            for b in range(nb):
                kb.tr(p[:, b * tb:(b + 1) * tb], stg[:, b, c * 128:(c + 1) * 128], self.ident[0:tb, 0:tb])
            self.evac(xT[:, c, :], p[:, 0:T])
        ar.pop()

    def ffn(self, xT, T, f):
        kb, ar = self.kb, self.ar
        ar.push()
        hn = ar.alloc((8, T), BF16)
        self.rmsnorm(xT, 8, T, f * 8, hn, D)
        a = ar.alloc((NJ, T), BF16)
        sg = [ar.alloc(T) for _ in range(2)]
        for j in range(NJ):
            w = self.ring.fetch(self.wgu[f, j], rkeys=[("wgu", f)]).rearrange("p (k g c) -> p k g c", k=8, g=2)
            pg = self.psum(); pu = self.psum()
            for kc in range(8):
                kb.mm(pg[:, 0:T], w[:, kc, 0, :], hn[:, kc, :], start=(kc == 0), stop=(kc == 7))
            for kc in range(8):
                kb.mm(pu[:, 0:T], w[:, kc, 1, :], hn[:, kc, :], start=(kc == 0), stop=(kc == 7))
            s = sg[j % 2]
            kb.act(s, pg[:, 0:T], AF.Silu)
            kb.tt(a[:, j, :], s, pu[:, 0:T], ALU.mult)
        for m in range(8):
            w = self.ring.fetch(self.wdn[f, m], rkeys=[("wdn", f)]).rearrange("p (j c) -> p j c", j=NJ)
            p = self.psum()
            for j in range(NJ):
                kb.mm(p[:, 0:T], w[:, j, :], a[:, j, :], start=(j == 0), stop=(j == NJ - 1))
            kb.stt(xT[:, m, :], p[:, 0:T], 0.5, xT[:, m, :], ALU.mult, ALU.add)
        ar.pop()

    def out_proj(self, xT, T, mixed, wout, key):
        kb = self.kb
        for m in range(8):
            w = self.ring.fetch(wout[m], rkeys=[(key,)]).rearrange("p (k c) -> p k c", k=8)
            p = self.psum()
            for kc in range(8):
                kb.mm(p[:, 0:T], w[:, kc, :], mixed[:, kc, :], start=(kc == 0), stop=(kc == 7))
            kb.tt(xT[:, m, :], p[:, 0:T], xT[:, m, :], ALU.add)

    def inF(self, winF, ci, key, hn, T, ncols=128):
        kb = self.kb
        w = self.ring.fetch(winF[ci], rkeys=[(key,)]).rearrange("p (k c) -> p k c", k=8)
        p = self.psum()
        for kc in range(8):
            kb.mm(p[0:ncols, 0:T], w[:, kc, 0:ncols], hn[:, kc, :], start=(kc == 0), stop=(kc == 7))
        return p[0:ncols, 0:T]

    def even_mixer(self, G, ti, xT):
        kb, ar = self.kb, self.ar
        T = G.T; NC = T // 64
        kt = G.npast + ti
        ar.push()
        hn = ar.alloc((8, T), BF16)
        self.rmsnorm(xT, 8, T, 32 + 0, hn, D)
        mixed = ar.alloc((8, T), BF16)
        ar.push()
        pgt = self.inF(self.winF_e, 18, "we", hn, T, ncols=8)
        gT = ar.alloc(T, parts=8); kb.copy(gT, pgt, eng='act')
        ga = ar.alloc((4, T)); gb = ar.alloc((4, T))
        for g in range(8):
            p = self.psum()
            kb.mm(p[:, 0:T], self.sel[0:8, g, :], gT[0:8, :])
            h = g % 4
            if g < 4:
                kb.act(ga[:, h, :], p[:, 0:T], AF.Identity, bias=self.bi[:, h:h + 1])
            else:
                kb.act(gb[:, h, :], p[:, 0:T], AF.Exp, scale=-1.0, bias=self.nbf[:, h:h + 1])
        gl = ar.alloc((4, T))
        kb.act(gl, gb, AF.Ln, bias=self.onec[:, 0:1])
        for h in range(4):
            kb.scan(gb[:, h, :], self.cm[:, 0:T], gl[:, h, :], 0.0, ALU.mult, ALU.add)
        kb.tt(ga, ga, gb, ALU.add)
        amax = ar.alloc((4, NC)); mext = ar.alloc((4, NC + 1)); Ac = ar.alloc((4, NC)); wi = ar.alloc((4, NC))
        kb.reduce(amax.rearrange("p h c -> p (h c)"), ga.rearrange("p h (c t) -> p (h c) t", t=64), ALU.max)
        gbend = gb.rearrange("p h (c t) -> p h c t", t=64)[:, :, :, 63]
        kb.copy(mext[:, :, 0], self.mst)
        for h in range(4):
            kb.scan(mext[:, h, 1:NC + 1], amax[:, h, :], gbend[:, h, :], self.mst[:, h:h + 1], ALU.max, ALU.subtract)
        kb.tt(Ac, mext[:, :, 1:NC + 1], gbend, ALU.add)
        kb.tt(wi, mext[:, :, 0:NC], Ac, ALU.subtract)
        kb.act(wi, wi, AF.Exp)
        kb.copy(self.mst, mext[:, :, NC])
        Ab = bc(Ac.rearrange("p h c -> p (h c)").unsqueeze(2), [128, 4 * NC, 64])
        gav = ga.rearrange("p h (c t) -> p (h c) t", t=64); gbv = gb.rearrange("p h (c t) -> p (h c) t", t=64)
        kb.tt(gav, gav, Ab, ALU.subtract); kb.act(ga, ga, AF.Exp)
        kb.tt(gbv, gbv, Ab, ALU.subtract); kb.act(gb, gb, AF.Exp)
        pw = self.psum()
        for c in range(NC):
            for h in range(4):
                kb.mm(pw[0:64, c * 4 + h:c * 4 + h + 1], ga[0:1, h, c * 64:(c + 1) * 64], self.onef[0:1, 0:1])
        wacol = ar.alloc((NC, 4), parts=64)
        kb.copy(wacol.rearrange("p c h -> p (c h)"), pw[0:64, 0:NC * 4])
        qmT = ar.alloc((4, T), BF16); kmT = ar.alloc((4, T), BF16); osig = ar.alloc((4, T))
        for h in range(4):
            p = self.inF(self.winF_e, h, "we", hn, T)
            kb.copy(qmT[:, h, :], p, eng='act')
        for h in range(4):
            p = self.inF(self.winF_e, 4 + h, "we", hn, T)
            kb.stt(kmT[:, h, :], p, 128 ** -0.5, ga[:, h, :], ALU.mult, ALU.mult)
        for h in range(4):
            p = self.inF(self.winF_e, 8 + h, "we", hn, T)
            kb.act(osig[:, h, :], p, AF.Sigmoid)
        ktok = ar.alloc((NC, 4, 128), BF16, parts=64); vaug = ar.alloc((NC, 4, 129), BF16, parts=64)
        kb.memset(vaug[:, :, :, 128:129], 1.0, eng='pool')
        wk = self.ring.fetch(self.winT_e[0], rkeys=[("we",)]).rearrange("p (k n) -> p k n", k=8)
        for c in range(NC):
            p = self.psum()
            for kc in range(8):
                kb.mm(p[0:64, :], hn[:, kc, c * 64:(c + 1) * 64], wk[:, kc, :], start=(kc == 0), stop=(kc == 7))
            for h in range(4):
                kb.ts(ktok[:, c, h, :], p[0:64, h * 128:(h + 1) * 128], wacol[:, c, h:h + 1], ALU.mult, 128 ** -0.5, ALU.mult)
        wv = self.ring.fetch(self.winT_e[1], rkeys=[("we",)]).rearrange("p (k n) -> p k n", k=8)
        for c in range(NC):
            p = self.psum()
            for kc in range(8):
                kb.mm(p[0:64, :], hn[:, kc, c * 64:(c + 1) * 64], wv[:, kc, :], start=(kc == 0), stop=(kc == 7))
            self.evac(vaug[:, c, :, 0:128], p[0:64, :].rearrange("p (h d) -> p h d", h=4))
        hm = ar.alloc((4, T))
        Cs = [ar.alloc((4, 129)) for _ in range(2)]
        Csb = [ar.alloc((4, 128), BF16) for _ in range(2)]
        nrep = [ar.alloc((4, 128), BF16) for _ in range(2)]
        GT = [ar.alloc((4, 64), BF16, parts=64) for _ in range(2)]
        dab = [ar.alloc((4, 64)) for _ in range(2)]
        for c in range(NC):
            b = c % 2
            cs = slice(c * 64, (c + 1) * 64)
            pS = self.psum(); pN = self.psum(); pD = self.psum(); pC0 = self.psum(); pC1 = self.psum()
            for h in range(4):
                kb.ts(Cs[b][:, h, :], self.Cst[:, h, :], wi[:, h, c:c + 1], ALU.mult)
                kb.copy(Csb[b][:, h, :], Cs[b][:, h, 0:128], eng='act')
                kb.copy(nrep[b][:, h, :], bc(Cs[b][:, h, 128:129], [128, 128]), eng='pool')
                kb.mm(pS[0:64, h * 64:(h + 1) * 64], kmT[:, h, cs], qmT[:, h, cs])
            kb.tt(GT[b], pS[0:64, 0:256].rearrange("p (h t) -> p h t", h=4), bc(self.maskml.unsqueeze(1), [64, 4, 64]), ALU.mult)
            for h in range(4):
                kb.mm(pN[:, h * 64:(h + 1) * 64], Csb[b][:, h, :], qmT[:, h, cs], start=True, stop=False)
                kb.mm(pN[:, h * 64:(h + 1) * 64], vaug[:, c, h, 0:128], GT[b][:, h, :], start=False, stop=True)
            for h in range(4):
                kb.mm(pD[:, h * 64:(h + 1) * 64], nrep[b][:, h, :], qmT[:, h, cs], start=True, stop=False)
                kb.mm(pD[:, h * 64:(h + 1) * 64], self.ones[0:64, :], GT[b][:, h, :], start=False, stop=True)
            for h in range(4):
                pc = pC0 if h < 2 else pC1
                kb.mm(pc[:, (h % 2) * 129:(h % 2) * 129 + 129], ktok[:, c, h, :], vaug[:, c, h, :])
            d = dab[b]
            kb.act(d, pD[:, 0:256].rearrange("p (h t) -> p h t", h=4), AF.Abs)
            kb.tt(d, d, gb[:, :, cs], ALU.max)
            kb.recip(d, d)
            kb.tt(hm[:, :, cs], pN[:, 0:256].rearrange("p (h t) -> p h t", h=4), d, ALU.mult)
            kb.tt(self.Cst[:, 0:2, :], pC0[:, 0:258].rearrange("p (h d) -> p h d", h=2), Cs[b][:, 0:2, :], ALU.add)
            kb.tt(self.Cst[:, 2:4, :], pC1[:, 0:258].rearrange("p (h d) -> p h d", h=2), Cs[b][:, 2:4, :], ALU.add)
        sq = kmT
        kb.act(sq, hm, AF.Square)
        rs = gl
        for h in range(4):
            p = self.psum()
            kb.mm(p[:, 0:T], self.ones, sq[:, h, :])
            kb.act(rs[:, h, :], p[:, 0:T], AF.Sqrt, scale=1.0 / 128, bias=self.epsc[:, 0:1])
        kb.recip(rs, rs)
        for h in range(4):
            kb.stt(hm[:, h, :], hm[:, h, :], self.gains[:, 56 + h:57 + h], rs[:, h, :], ALU.mult, ALU.mult)
        kb.tt(mixed[:, 0:4, :], hm, osig, ALU.mult)
        ar.pop()
        ar.push()
        cqT = ar.alloc((3, T)); ckvT = ar.alloc((2, T)); qr5 = ar.alloc((5, T), parts=64)
        for c in range(3):
            p = self.inF(self.winF_e, 12 + c, "we", hn, T); self.evac(cqT[:, c, :], p)
        for c in range(2):
            p = self.inF(self.winF_e, 15 + c, "we", hn, T); self.evac(ckvT[:, c, :], p)
        p = self.inF(self.winF_e, 17, "we", hn, T, ncols=64); self.evac(qr5[:, 4, :], p)
        cqn = ar.alloc((3, T), BF16)
        self.rmsnorm(cqT, 3, T, 60, cqn, 384)
        qnT = ar.alloc((4, T), BF16)
        for h in range(4):
            p = self.psum()
            for kc in range(3):
                kb.mm(p[:, 0:T], self.wuq[:, kc, h * 192:h * 192 + 128], cqn[:, kc, :], start=(kc == 0), stop=(kc == 2))
            kb.copy(qnT[:, h, :], p[:, 0:T], eng='act')
            p2 = self.psum()
            for kc in range(3):
                kb.mm(p2[0:64, 0:T], self.wuq[:, kc, h * 192 + 128:h * 192 + 192], cqn[:, kc, :], start=(kc == 0), stop=(kc == 2))
            kb.copy(qr5[:, h, :], p2[0:64, 0:T], eng='dve')
        cs2 = ar.alloc(T, parts=64); sn2 = ar.alloc(T, parts=64)
        kb.dma(cs2, G.rope[0][:, ti * T:(ti + 1) * T]); kb.dma(sn2, G.rope[1][:, ti * T:(ti + 1) * T])
        qrf = ar.alloc((5, T), parts=64); qrb = ar.alloc((5, T), BF16, parts=64)
        ar.push()
        t1 = ar.alloc((5, T), parts=64); t2 = ar.alloc((5, T), parts=64)
        kb.tt(t1, qr5, bc(cs2.unsqueeze(1), [64, 5, T]), ALU.mult)
        kb.tt(t2[0:32], qr5[32:64], bc(sn2[32:64].unsqueeze(1), [32, 5, T]), ALU.mult)
        kb.tt(t2[32:64], qr5[0:32], bc(sn2[0:32].unsqueeze(1), [32, 5, T]), ALU.mult, eng='pool')
        kb.tt(qrf[0:32], t1[0:32], t2[0:32], ALU.subtract)
        kb.tt(qrf[32:64], t1[32:64], t2[32:64], ALU.add, eng='pool')
        ar.pop()
        kb.copy(qrb, qrf, eng='act')
        latf = ar.alloc((2, T)); latb = ar.alloc((2, T), BF16)
        self.rmsnorm(ckvT, 2, T, 63, latf, 256)
        kb.copy(latb, latf, eng='act')
        r0 = ti * T
        self.transpose_out([(latf[:, 0, :], 128), (latf[:, 1, :], 128)], 256, T, G.lat[r0:r0 + T, :])
        self.transpose_out([(qrf[:, 4, :], 64)], 64, T, G.kr[r0:r0 + T, :])
        self.kv_expand(G, kt, latb, T)
        kb.dma(G.KrT[kt][:, 0:T], qrb[:, 4, :], eng='pool', wkeys=[("kv" + G.name, kt)])
        self.done.add(("kv" + G.name, kt))
        self.mla_attend(G, ti, T, qnT, qrb, mixed)
        ar.pop()
        if G.name == "p" and ti == 0 and self.debug:
            self.dbg("dbg_mixed", mixed, (8, T))
        self.out_proj(xT, T, mixed, self.wout_e, "we")
        ar.pop()

    def kv_expand(self, G, kt, latb, T):
        kb, ar = self.kb, self.ar
        ar.push()
        kn = ar.alloc((4, T), BF16)
        for h in range(4):
            p = self.psum()
            for kc in range(2):
                kb.mm(p[:, 0:T], self.wukv[:, kc, h * 256:h * 256 + 128], latb[:, kc, :], start=(kc == 0), stop=(kc == 1))
            self.evac(kn[:, h, :], p[:, 0:T])
        kb.dma(G.KnT[kt].rearrange("p (h t) -> p h t", h=4)[:, :, 0:T], kn, eng='pool', wkeys=[("kv" + G.name, kt)])
        if self.debug and G.name == "p" and kt == 0:
            self.dbg("dbg_kn", kn, (4, T))
        tb = min(128, T); nb = T // tb
        vt = ar.alloc((nb, 512), BF16, parts=tb)
        wv = self.wukv.rearrange("p k (h x) -> p k h x", h=4)[:, :, :, 128:256]
        for b in range(nb):
            p = self.psum()
            for kc in range(2):
                kb.mm(p[0:tb, :], latb[:, kc, b * tb:(b + 1) * tb], wv[:, kc, :, :], start=(kc == 0), stop=(kc == 1))
            self.evac(vt[:, b, :], p[0:tb, :])
        kb.dma(G.V[kt].rearrange("p (b x) -> p b x", b=4)[0:tb, 0:nb, :], vt, eng='pool', wkeys=[("kv" + G.name, kt)])
        ar.pop()

    def mla_attend(self, G, ti, T, qnT, qrb, mixed):
        kb, ar = self.kb, self.ar
        scale = 192 ** -0.5
        ktc = G.npast + ti
        ar.push()
        Pb = [ar.alloc(T, BF16) for _ in range(3)]
        rden = ar.alloc(T)
        pi = 0
        for h in range(4):
            res = self.reserve(2)
            pO = self.ps[res[0]]; pDn = self.ps[res[1]]
            first = True
            for kt in range(0, ktc + 1):
                diag = (kt == ktc)
                nk = T if diag else 512
                key = [("kv" + G.name, kt)]
                self.ring.unhold()
                knT = self.ring.fetch(G.KnT[kt][:, h * 512:h * 512 + nk], rkeys=key, hold=True)
                krT = self.ring.fetch(G.KrT[kt][:, 0:nk], rkeys=key, hold=True)
                kbs = min(128, nk); nkb = nk // kbs
                vv = self.ring.fetch(G.V[kt].rearrange("p (b x) -> p b x", b=4)[0:kbs, 0:nkb, h * 128:(h + 1) * 128], rkeys=key)
                vv = vv.rearrange("p (b x) -> p b x", b=nkb)
                if self.debug and G.name == "p" and ti == 0 and h < 2:
                    self.dbg("dbg_knT%d" % h, knT, (512,))
                    if h == 0:
                        self.dbg("dbg_qnT", qnT, (4, T))
                for b in range(nkb):
                    ko = b * kbs
                    qs = ko if diag else 0
                    last = diag and (b == nkb - 1)
                    pS = self.psum()
                    kb.mm(pS[0:kbs, qs:T], knT[:, ko:ko + kbs], qnT[:, h, qs:T], start=True, stop=False)
                    kb.mm(pS[0:kbs, qs:T], krT[0:64, ko:ko + kbs], qrb[:, h, qs:T], start=False, stop=True)
                    P = Pb[pi % 3]; pi += 1
                    kb.act(P[0:kbs, qs:T], pS[0:kbs, qs:T], AF.Exp, scale=scale)
                    if diag and kbs == 128:
                        kb.ts(P[64:128, qs:qs + 64], P[64:128, qs:qs + 64], 0.0, ALU.mult)
                    kb.mm(pO[:, qs:T], vv[:, b, :], P[0:kbs, qs:T], start=first, stop=last)
                    kb.mm(pDn[:, qs:T], self.ones[0:kbs, :], P[0:kbs, qs:T], start=first, stop=last)
                    first = False
            kb.recip(rden, pDn[:, 0:T])
            kb.tt(mixed[:, 4 + h, :], pO[:, 0:T], rden, ALU.mult)
            self.release(res)
            self.ring.unhold()
        ar.pop()

    def odd_mixer(self, G, ti, xT):
        kb, ar = self.kb, self.ar
        T = G.T
        kt = G.npast + ti
        ar.push()
        hn = ar.alloc((8, T), BF16)
        self.rmsnorm(xT, 8, T, 32 + 8, hn, D)
        mixed = ar.alloc((8, T), BF16)
        ar.push()
        Ts = min(128, T); nsc = T // Ts
        Ec = ar.alloc((16, 128)); Es = ar.alloc((16, 128)); BT = ar.alloc((32, 128), BF16); Cm = ar.alloc((32, 128), BF16)
        kb.dma(Ec, self.s5tab[:, 0:2048].rearrange("p (s t) -> p s t", s=16), rkeys=[("s5",)])
        kb.dma(Es, self.s5tab[:, 4096:4096 + 2048].rearrange("p (s t) -> p s t", s=16), rkeys=[("s5",)])
        kb.dma(BT, self.s5mat[:, 0:4096].rearrange("p (s t) -> p s t", s=32), rkeys=[("s5",)])
        kb.dma(Cm, self.s5mat[:, 4096:8192].rearrange("p (s t) -> p s t", s=32), rkeys=[("s5",)])
        uf = ar.alloc((4, T)); ub = ar.alloc((4, T), BF16)
        for j in range(4):
            p = self.inF(self.winF_o, j, "wo", hn, T)
            kb.copy(uf[:, j, :], p, eng='act'); kb.copy(ub[:, j, :], p, eng='dve')
        yT = ar.alloc((4, T))
        tA = [ar.alloc(T)] * 2; tB = [ar.alloc(T)] * 2
        cR = [ar.alloc(T)] * 2; cI = [ar.alloc(T)] * 2
        wR = [ar.alloc(T)] * 2; wI = [ar.alloc(T)] * 2
        xR = [ar.alloc(T, BF16) for _ in range(4)]; xI = [ar.alloc(T, BF16) for _ in range(4)]
        tmp1 = ar.alloc(4)
        v3 = lambda a: a.rearrange("p (c t) -> p c t", t=Ts)
        for j in range(4):
            for sl in range(4):
                s = 4 * j + sl
                b = s % 2
                pr = self.psum(); pim = self.psum()
                kb.mm(pr[:, 0:T], BT[:, 2 * s, :], ub[:, j, :])
                kb.mm(pim[:, 0:T], BT[:, 2 * s + 1, :], ub[:, j, :])
                ecb = bc(Ec[:, s, 0:Ts].unsqueeze(1), [128, nsc, Ts]); esb = bc(Es[:, s, 0:Ts].unsqueeze(1), [128, nsc, Ts])
                kb.tt(v3(tA[b]), v3(pr[:, 0:T]), ecb, ALU.mult)
                kb.tt(v3(tB[b]), v3(pim[:, 0:T]), esb, ALU.mult)
                kb.tt(cR[b], tA[b], tB[b], ALU.add, eng='pool')
                kb.tt(v3(tA[b]), v3(pim[:, 0:T]), ecb, ALU.mult)
                kb.tt(v3(tB[b]), v3(pr[:, 0:T]), esb, ALU.mult)
                kb.tt(cI[b], tA[b], tB[b], ALU.subtract, eng='pool')
                rb = bc(self.rdec[:, s:s + 1], [128, Ts])
                for sc in range(nsc):
                    cs = slice(sc * Ts, (sc + 1) * Ts)
                    kb.scan(wR[b][:, cs], rb, cR[b][:, cs], self.wst[:, s, 0:1], ALU.mult, ALU.add)
                    kb.scan(wI[b][:, cs], rb, cI[b][:, cs], self.wst[:, s, 1:2], ALU.mult, ALU.add)
                    e = sc * Ts + Ts - 1
                    ecl = Ec[:, s, Ts - 1:Ts]; esl = Es[:, s, Ts - 1:Ts]
                    kb.ts(tmp1[:, 0:1], wI[b][:, e:e + 1], esl, ALU.mult)
                    kb.ts(tmp1[:, 1:2], wR[b][:, e:e + 1], esl, ALU.mult)
                    kb.stt(self.wst[:, s, 0:1], wR[b][:, e:e + 1], ecl, tmp1[:, 0:1], ALU.mult, ALU.subtract)
                    kb.stt(self.wst[:, s, 1:2], wI[b][:, e:e + 1], ecl, tmp1[:, 1:2], ALU.mult, ALU.add)
                xi = (s % 4)
                kb.tt(v3(tA[b]), v3(wR[b]), ecb, ALU.mult)
                kb.tt(v3(tB[b]), v3(wI[b]), esb, ALU.mult, eng='pool')
                kb.tt(xR[xi], tA[b], tB[b], ALU.subtract)
                kb.tt(v3(cR[b]), v3(wI[b]), ecb, ALU.mult, eng='pool')
                kb.tt(v3(cI[b]), v3(wR[b]), esb, ALU.mult)
                kb.tt(xI[xi], cR[b], cI[b], ALU.add, eng='pool')
            py = self.psum()
            for sl in range(4):
                s = 4 * j + sl
                kb.mm(py[:, 0:T], Cm[:, 2 * s, :], xR[s % 4], start=(sl == 0), stop=False)
                kb.mm(py[:, 0:T], Cm[:, 2 * s + 1, :], xI[s % 4], start=False, stop=(sl == 3))
            kb.stt(yT[:, j, :], uf[:, j, :], self.gains[:, 65 + j:66 + j], py[:, 0:T], ALU.mult, ALU.add)
        gq = ar.alloc((4, T)); gg = ar.alloc((4, T)); gbf = ar.alloc((4, T), BF16)
        kb.act(gq, yT, AF.Square)
        kb.ts(gq, gq, 0.044715, ALU.mult, 1.0, ALU.add)
        kb.tt(gq, gq, yT, ALU.mult)
        kb.act(gq, gq, AF.Sigmoid, scale=2.0 * 0.7978845608028654)
        kb.tt(gg, gq, yT, ALU.mult)
        kb.copy(gbf, gg, eng='act')
        for m in range(4):
            p = self.psum()
            for kc in range(4):
                kb.mm(p[:, 0:T], self.wglu[:, kc, m * 128:(m + 1) * 128], gbf[:, kc, :], start=(kc == 0), stop=(kc == 3))
            kb.act(gq[:, m, :], p[:, 0:T], AF.Sigmoid)
        kb.tt(mixed[:, 0:4, :], gg, gq, ALU.mult)
        ar.pop()
        ar.push()
        qT = ar.alloc((4, T), BF16); kT = ar.alloc((4, T), BF16)
        for j in range(4):
            p = self.inF(self.winF_o, 4 + j, "wo", hn, T)
            kb.act(qT[:, j, :], p, AF.Copy, scale=0.125)
        for j in range(4):
            p = self.inF(self.winF_o, 8 + j, "wo", hn, T)
            self.evac(kT[:, j, :], p)
        kb.dma(G.sKT[kt].rearrange("p (j t) -> p j t", j=4)[:, :, 0:T], kT, eng='pool', wkeys=[("sb" + G.name, kt)])
        tb = min(128, T); nb = T // tb
        r0 = ti * T
        for which, (dst, oo) in enumerate(((None, G.ok), (G.sV, G.ov))):
            w = self.ring.fetch(self.winT_o[which], rkeys=[("wo",)]).rearrange("p (k n) -> p k n", k=8)
            ar.push()
            stf = ar.alloc((nb, 512), parts=tb)
            stb = ar.alloc((nb, 512), BF16, parts=tb)
            for b in range(nb):
                p = self.psum()
                for kc in range(8):
                    kb.mm(p[0:tb, :], hn[:, kc, b * tb:(b + 1) * tb], w[:, kc, :], start=(kc == 0), stop=(kc == 7))
                kb.copy(stf[:, b, :], p[0:tb, :], eng='act')
                if dst is not None:
                    kb.copy(stb[:, b, :], p[0:tb, :], eng='dve')
            kb.dma(oo[r0:r0 + T, :].rearrange("(b p) f -> p b f", p=tb), stf, eng='pool')
            if dst is not None:
                kb.dma(dst[kt].rearrange("p (b x) -> p b x", b=4)[0:tb, 0:nb, :], stb, eng='pool', wkeys=[("sb" + G.name, kt)])
            ar.pop()
        self.done.add(("sb" + G.name, kt))
        self.sb_attend(G, ti, T, qT, mixed)
        ar.pop()
        self.out_proj(xT, T, mixed, self.wout_o, "wo")
        ar.pop()

    def sb_attend(self, G, ti, T, qT, mixed):
        kb, ar = self.kb, self.ar
        ktc = G.npast + ti
        ar.push()
        Eb = [ar.alloc(T) for _ in range(2)]
        Lb = [ar.alloc(T, BF16) for _ in range(3)]
        Wb = [ar.alloc(T, BF16) for _ in range(3)]
        Racc = [ar.alloc(T) for _ in range(2)]
        Rhi = [ar.alloc(T, BF16) for _ in range(4)]; Rlo = [ar.alloc(T, BF16) for _ in range(4)]
        bi = 0
        for j in range(4):
            res = self.reserve(2)
            pO = [self.ps[res[0]], self.ps[res[1]]]
            for e in range(2):
                kb.memset(Racc[e], 0.0, eng='pool')
            first = [True, True]
            for kt in range(ktc, -1, -1):
                diag = (kt == ktc)
                nk = T if diag else 512
                key = [("sb" + G.name, kt)]
                self.ring.unhold()
                kT = self.ring.fetch(G.sKT[kt][:, j * 512:j * 512 + nk], rkeys=key, hold=True)
                kbs = min(128, nk); nkb = nk // kbs
                vv = self.ring.fetch(G.sV[kt].rearrange("p (b x) -> p b x", b=4)[0:kbs, 0:nkb, j * 128:(j + 1) * 128], rkeys=key)
                vv = vv.rearrange("p (b x) -> p b x", b=nkb)
                for b in range(nkb - 1, -1, -1):
                    ko = b * kbs
                    qs = ko if diag else 0
                    for e in range(2):
                        pb = 64 * e
                        last = (kt == 0 and b == 0)
                        pZ = self.psum()
                        kb.mm(pZ[0:kbs, qs:T], kT[pb:pb + 64, ko:ko + kbs], qT[pb:pb + 64, j, qs:T], start=True, stop=False)
                        E = Eb[bi % 2]; L = Lb[bi % 3]; W = Wb[bi % 3]; rh = Rhi[bi % 4]; rl = Rlo[bi % 4]
                        bi += 1
                        kb.act(E[0:kbs, qs:T], pZ[0:kbs, qs:T], AF.Exp)
                        kb.act(L[0:kbs, qs:T], E[0:kbs, qs:T], AF.Ln, bias=self.onec[0:kbs, 0:1])
                        if diag:
                            kb.tt(L[0:kbs, qs:qs + kbs], L[0:kbs, qs:qs + kbs], self.masksb[0:kbs, 0:kbs], ALU.mult, eng='pool')
                        kb.mm(pZ[0:kbs, qs:T], self.uincl[0:kbs, 0:kbs], L[0:kbs, qs:T], start=False, stop=first[e])
                        if not first[e]:
                            kb.copy(rh[:, qs:T], Racc[e][:, qs:T], eng='pool')
                            kb.tt(rl[:, qs:T], Racc[e][:, qs:T], rh[:, qs:T], ALU.subtract, eng='pool')
                            kb.mm(pZ[0:kbs, qs:T], self.negones[:, 0:kbs], rh[:, qs:T], start=False, stop=False)
                            kb.mm(pZ[0:kbs, qs:T], self.negones[:, 0:kbs], rl[:, qs:T], start=False, stop=True)
                        kb.act(W[0:kbs, qs:T], pZ[0:kbs, qs:T], AF.Exp)
                        if diag:
                            kb.tt(W[0:kbs, qs:qs + kbs], W[0:kbs, qs:qs + kbs], self.masksb[0:kbs, 0:kbs], ALU.mult, eng='pool')
                        kb.mm(pO[e][0:64, qs:T], vv[:, b, pb:pb + 64], W[0:kbs, qs:T], start=first[e], stop=last)
                        if not last:
                            kb.tt(Racc[e][0:kbs, qs:T], Racc[e][0:kbs, qs:T], L[0:kbs, qs:T], ALU.add, eng='pool')
                        first[e] = False
            for e in range(2):
                kb.copy(mixed[64 * e:64 * e + 64, 4 + j, :], pO[e][0:64, 0:T], eng='act')
            self.release(res)
            self.ring.unhold()
        ar.pop()


RING_BYTES = 32768


def _sin_of(self, out, ang, shift, shape):
    kb, ar = self.kb, self.ar
    ar.push()
    t = ar.alloc(shape); k = ar.alloc(shape)
    kb.ts(t, ang, float(shift), ALU.add)
    kb.ts(k, t, 1.0 / TWO_PI, ALU.mult, MAGIC, ALU.add)
    kb.ts(k, k, -MAGIC, ALU.add)
    kb.stt(t, k, -TWO_PI, t, ALU.mult, ALU.add)
    kb.ts(t, t, float(np.pi), ALU.min, -float(np.pi), ALU.max)
    kb.act(out, t, AF.Sin)
    ar.pop()


def prologue_s5(self):
    kb, ar = self.kb, self.ar
    A = ar.alloc
    self.rdec = A(16)
    ar.push()
    rows = A(128); kb.memset(rows, 0.0); kb.dma(rows[0:48, :], self.s5rows)
    p = self.psum(); kb.tr(p[:, 0:128], rows, self.ident)
    prm = A(48); kb.copy(prm, p[:, 0:48])
    a_r = prm[:, 0:16]; a_i = prm[:, 16:32]; ldt = prm[:, 32:48]
    dt = A(16); dar = A(16); dai = A(16)
    kb.act(dt, ldt, AF.Exp)
    kb.tt(dar, dt, a_r, ALU.mult); kb.tt(dai, dt, a_i, ALU.mult)
    kb.act(self.rdec, dar, AF.Exp)
    cs = A(16); sn = A(16)
    _sin_of(self, sn, dai, 0.0, 16); _sin_of(self, cs, dai, np.pi / 2, 16)
    abr = A(16); abi = A(16); den = A(16); t1 = A(16); t2 = A(16); cr = A(16); ci = A(16)
    kb.tt(abr, self.rdec, cs, ALU.mult); kb.tt(abi, self.rdec, sn, ALU.mult)
    kb.tt(t1, a_r, a_r, ALU.mult); kb.tt(t2, a_i, a_i, ALU.mult); kb.tt(den, t1, t2, ALU.add); kb.recip(den, den)
    kb.ts(abr, abr, -1.0, ALU.add)
    kb.tt(t1, abr, a_r, ALU.mult); kb.tt(t2, abi, a_i, ALU.mult); kb.tt(cr, t1, t2, ALU.add); kb.tt(cr, cr, den, ALU.mult)
    kb.tt(t1, abi, a_r, ALU.mult); kb.tt(t2, abr, a_i, ALU.mult); kb.tt(ci, t1, t2, ALU.subtract); kb.tt(ci, ci, den, ALU.mult)
    Bre = A((16, 16)); Bim = A((16, 16))
    kb.dma(Bre, self.s5B[0].rearrange("(s q) c -> q s c", q=128)); kb.dma(Bim, self.s5B[1].rearrange("(s q) c -> q s c", q=128))
    crb = bc(cr.unsqueeze(2), [128, 16, 16]); cib = bc(ci.unsqueeze(2), [128, 16, 16])
    u1 = A((16, 16)); u2 = A((16, 16)); bbr = A((16, 16)); bbi = A((16, 16))
    kb.tt(u1, Bre, crb, ALU.mult); kb.tt(u2, Bim, cib, ALU.mult); kb.tt(bbr, u1, u2, ALU.subtract)
    kb.tt(u1, Bim, crb, ALU.mult); kb.tt(u2, Bre, cib, ALU.mult); kb.tt(bbi, u1, u2, ALU.add)
    bb2 = A((16, 2, 32)); kb.memset(bb2, 0.0)
    for ri, bb in enumerate((bbr, bbi)):
        kb.copy(bb2[0:64, :, ri, 0:16], bb[0:64]); kb.copy(bb2[64:128, :, ri, 16:32], bb[64:128])
    BT = A((32, 128), BF16); kb.memset(BT, 0.0)
    for s in range(16):
        p = self.psum()
        for ri in range(2):
            kb.tr(p[0:32, ri * 128:(ri + 1) * 128], bb2[:, s, ri, :], self.ident)
        a = s % 4
        kb.copy(BT[32 * a:32 * a + 32, 2 * s:2 * s + 2, :], p[0:32, 0:256].rearrange("p (r c) -> p r c", r=2))
    kb.dma(self.s5mat[:, 0:4096].rearrange("p (s t) -> p s t", s=32), BT, eng='pool', wkeys=[("s5",)])
    Cm = A((32, 128), BF16); kb.memset(Cm, 0.0)
    for ri in range(2):
        Cin = A((4, 64)); kb.dma(Cin, self.s5C[ri].rearrange("(j q) p -> q j p", q=128))
        CT = A((4, 128), parts=64)
        for j in range(4):
            p = self.psum()
            kb.tr(p[0:64, 0:128], Cin[:, j, :], self.ident)
            kb.act(CT[:, j, :], p[0:64, 0:128], AF.Copy, scale=(1.0 if ri == 0 else -1.0))
        for e in range(2):
            for sl in range(4):
                c0 = (2 * sl + e) * 16
                kb.copy(Cm[64 * e:64 * e + 64, ri + 2 * sl:32:8, c0:c0 + 16], CT[:, :, c0:c0 + 16])
    kb.dma(self.s5mat[:, 4096:8192].rearrange("p (s t) -> p s t", s=32), Cm, eng='pool', wkeys=[("s5",)])
    io = A(128); kb.dma(io, self.c_iota)
    ang = A((16, 128)); tb = A((16, 128))
    for s in range(16):
        kb.ts(ang[:, s, :], io, dai[:, s:s + 1], ALU.mult)
    _sin_of(self, tb, ang, np.pi / 2, (16, 128))
    kb.dma(self.s5tab[:, 0:2048].rearrange("p (s t) -> p s t", s=16), tb, eng='pool', wkeys=[("s5",)])
    tb2 = A((16, 128))
    _sin_of(self, tb2, ang, 0.0, (16, 128))
    kb.dma(self.s5tab[:, 4096:4096 + 2048].rearrange("p (s t) -> p s t", s=16), tb2, eng='pool', wkeys=[("s5",)])
    ar.pop()


def sample_prep(self, G):
    kb, ar = self.kb, self.ar
    A = ar.alloc
    for kt in range(G.npast):
        ar.push()
        rs = slice(kt * 512, (kt + 1) * 512)
        stg = A((4, 256)); kb.dma(stg, self.c_lat[rs, :].rearrange("(b p) f -> p b f", p=128))
        latb = A((2, 512), BF16)
        for c in range(2):
            p = self.psum()
            for b in range(4):
                kb.tr(p[:, b * 128:(b + 1) * 128], stg[:, b, c * 128:(c + 1) * 128], self.ident)
            self.evac(latb[:, c, :], p[:, 0:512])
        self.kv_expand(G, kt, latb, 512)
        stg2 = A((4, 64)); kb.dma(stg2, self.c_kr[rs, :].rearrange("(b p) f -> p b f", p=128))
        p = self.psum()
        for b in range(4):
            kb.tr(p[0:64, b * 128:(b + 1) * 128], stg2[:, b, :], self.ident)
        krb = A(512, BF16, parts=64); self.evac(krb, p[0:64, 0:512])
        kb.dma(G.KrT[kt], krb, eng='pool', wkeys=[("kv" + G.name, kt)])
        stk = A((4, 512)); kb.dma(stk, self.c_sbk[rs, :].rearrange("(b p) f -> p b f", p=128))
        kT = A((4, 512), BF16)
        for j in range(4):
            p = self.psum()
            for b in range(4):
                kb.tr(p[:, b * 128:(b + 1) * 128], stk[:, b, j * 128:(j + 1) * 128], self.ident)
            self.evac(kT[:, j, :], p[:, 0:512])
        kb.dma(G.sKT[kt].rearrange("p (j t) -> p j t", j=4), kT, eng='pool', wkeys=[("sb" + G.name, kt)])
        stv = A((4, 512)); kb.dma(stv, self.c_sbv[rs, :].rearrange("(b p) f -> p b f", p=128))
        vb = A((4, 512), BF16); kb.copy(vb, stv, eng='pool')
        kb.dma(G.sV[kt].rearrange("p (b x) -> p b x", b=4), vb, eng='pool', wkeys=[("sb" + G.name, kt)])
        self.done.add(("kv" + G.name, kt)); self.done.add(("sb" + G.name, kt))
        ar.pop()
    ar.push()
    stC = A((4, 128)); kb.dma(stC, self.st_C.rearrange("h v k -> v h k"))
    for h in range(4):
        p = self.psum(); kb.tr(p[:, 0:128], stC[:, h, :], self.ident)
        self.evac(self.Cst[:, h, 0:128], p[:, 0:128])
    rows = A(128); kb.memset(rows, 0.0); kb.dma(rows[0:4, :], self.st_n)
    p = self.psum(); kb.tr(p[:, 0:128], rows, self.ident)
    kb.copy(self.Cst[:, :, 128], p[:, 0:4])
    kb.dma(self.mst, bc(self.st_m[0:1, :], [128, 4]))
    rows2 = A(128); kb.memset(rows2, 0.0); kb.dma(rows2[0:16, :], self.st_re); kb.dma(rows2[16:32, :], self.st_im)
    p = self.psum(); kb.tr(p[:, 0:128], rows2, self.ident)
    kb.copy(self.wst[:, :, 0], p[:, 0:16]); kb.copy(self.wst[:, :, 1], p[:, 16:32])
    ar.pop()


def write_states(self, G):
    kb, ar = self.kb, self.ar
    A = ar.alloc
    ar.push()
    stg = A((4, 128))
    for h in range(4):
        p = self.psum(); kb.tr(p[:, 0:128], self.Cst[:, h, 0:128], self.ident)
        self.evac(stg[:, h, :], p[:, 0:128])
    kb.dma(G.oC.rearrange("h v k -> v h k"), stg, eng='pool')
    tmp = A(4); kb.copy(tmp, self.Cst[:, :, 128])
    p = self.psum(); kb.tr(p[0:4, 0:128], tmp, self.ident)
    st2 = A(128, parts=4); kb.copy(st2, p[0:4, 0:128]); kb.dma(G.on, st2, eng='pool')
    kb.dma(G.om, self.mst[0:1, :], eng='pool')
    tmp2 = A(32); kb.copy(tmp2[:, 0:16], self.wst[:, :, 0]); kb.copy(tmp2[:, 16:32], self.wst[:, :, 1])
    p = self.psum(); kb.tr(p[0:32, 0:128], tmp2, self.ident)
    st3 = A(128, parts=32); kb.copy(st3, p[0:32, 0:128])
    kb.dma(G.ore, st3[0:16, :], eng='pool'); kb.dma(G.oim, st3[16:32, :], eng='pool')
    ar.pop()


def final_out(self, G, ti, xT):
    kb, ar = self.kb, self.ar
    T = G.T
    ar.push()
    yT = ar.alloc((8, T))
    self.rmsnorm(xT, 8, T, 48, yT, D)
    r0 = ti * T
    self.transpose_out([(yT[:, c, :], 128) for c in range(8)], D, T, G.y[r0:r0 + T, :])
    ar.pop()


def build(self, stages=("ffn0", "even", "ffn1", "ffn2", "odd", "ffn3")):
    kb = self.kb
    self.declare()
    plan = None
    for pass_ in (0, 1):
        kb.dry = (pass_ == 0)
        self.plan = plan
        self.ar = ar = Arena(self.arena)
        self.ps_i = 0; self.ps_res = set(); self._evac_i = 0
        self.done = set()
        self.load_consts()
        self.epsc = ar.alloc(1); kb.memset(self.epsc, EPS)
        self.onec = ar.alloc(1); kb.memset(self.onec, 1.0)
        self.ring = Ring(self, RING_BYTES)
        xTfull = ar.alloc((8, 512))
        self.prologue_weights()
        prologue_s5(self)
        self.done.update([("wgu", f) for f in range(4)] + [("wdn", f) for f in range(4)] + [("we",), ("wo",), ("s5",)])
        for G in self.groups:
            if G.name == "p":
                kb.memset(self.Cst, 0.0); kb.memset(self.mst, 0.0); kb.memset(self.wst, 0.0)
            else:
                sample_prep(self, G)
            for ti in range(G.ntile):
                xT = xTfull[:, :, 0:G.T]
                self.load_xT(G, ti, xT)
                if "ffn0" in stages: self.ffn(xT, G.T, 0)
                if "even" in stages: self.even_mixer(G, ti, xT)
                if "ffn1" in stages: self.ffn(xT, G.T, 1)
                if "ffn2" in stages: self.ffn(xT, G.T, 2)
                if "odd" in stages: self.odd_mixer(G, ti, xT)
                if "ffn3" in stages: self.ffn(xT, G.T, 3)
                final_out(self, G, ti, xT)
            write_states(self, G)
        if pass_ == 0:
            plan = self.ring.newplan
    print("arena peak", self.ar.peak, "ops", len(kb.ops), "plan", len(plan))
    kb.analyze()
    print(kb.stats())
    kb.emit(self.st)
    self.st.close()
    return self.nc


def rope_tables(pos):
    half = 32
    inv = (np.float32(10000.0) ** (-np.arange(half, dtype=np.float32) / np.float32(half))).astype(np.float32)
    ang = pos.astype(np.float32)[None, :] * inv[:, None]
    c = np.cos(ang).astype(np.float32); s = np.sin(ang).astype(np.float32)
    return np.concatenate([c, c], 0), np.concatenate([s, s], 0)


def host_consts(S, PAST):
    k = np.arange(128)
    c = {}
    c["c_ident"] = np.eye(128, dtype=np.float32)
    c["c_masksb"] = (k[:, None] < k[None, :]).astype(np.float32)
    c["c_maskml"] = (k[:64, None] <= k[None, :64]).astype(np.float32)
    c["c_uincl"] = -(k[:, None] >= k[None, :]).astype(np.float32)
    sel = np.zeros((8, 8, 128), np.float32)
    for g in range(8):
        sel[g, g, :] = 1.0
    c["c_sel"] = sel.reshape(8, 1024)
    cm = np.ones((128, 512), np.float32); cm[:, ::64] = 0.0
    c["c_cm"] = cm
    c["c_iota"] = np.tile(np.arange(1, 129, dtype=np.float32)[None, :], (128, 1))
    c["c_cosp"], c["c_sinp"] = rope_tables(np.arange(S))
    c["c_coss"], c["c_sins"] = rope_tables(PAST + np.arange(64))
    return c


def make_inmaps(inputs, ncores, S, PAST):
    f = lambda a: np.ascontiguousarray(np.asarray(a), dtype=np.float32)
    I = {k: np.asarray(v) for k, v in inputs.items()}
    shared = dict(
        w_gate=f(I["ffn_w_gate"].reshape(4, D, DFF)), w_up=f(I["ffn_w_up"].reshape(4, D, DFF)),
        w_down=f(I["ffn_w_down"].reshape(4, DFF, D)),
        e_win=f(I["even_w_in"][0]), e_wout=f(I["even_w_out"][0]), wuq=f(I["mla_w_uq"][0]), wukv=f(I["mla_w_ukv"][0]),
        o_win=f(I["odd_w_in"][0]), o_wout=f(I["odd_w_out"][0]), wglu=f(I["s5_w_glu"][0]),
        grows=f(np.concatenate([I["norm_ffn"].reshape(32, 128), I["norm_mix"].reshape(16, 128), I["norm_final"].reshape(8, 128),
                                I["mlstm_out_norm"][0].reshape(4, 128), I["mla_q_norm"][0].reshape(3, 128),
                                I["mla_kv_norm"][0].reshape(2, 128), I["s5_D"][0].reshape(4, 128)], 0)),
        bi=f(I["mlstm_b_igate"][0][None]), bf=f(I["mlstm_b_fgate"][0][None]),
        s5rows=f(np.concatenate([I["s5_A_re"][0].reshape(16, 128), I["s5_A_im"][0].reshape(16, 128),
                                 np.repeat(I["s5_log_dt"][0][:, None], 64, axis=1).reshape(16, 128)], 0)),
        s5B_re=f(I["s5_B_re"][0].reshape(2048, 16)), s5B_im=f(I["s5_B_im"][0].reshape(2048, 16)),
        s5C_re=f(I["s5_C_re"][0].reshape(512, 64)), s5C_im=f(I["s5_C_im"][0].reshape(512, 64)),
    )
    shared.update(host_consts(S, PAST))
    maps = []
    for b in range(ncores):
        m = dict(shared)
        m.update(xp=f(I["x_prompt"][b]), xs=f(I["x_sample"][b]), c_lat=f(I["cache_mla_latent"][0, b]),
                 c_kr=f(I["cache_mla_krope"][0, b]), st_C=f(I["state_mlstm_C"][0, b]), st_n=f(I["state_mlstm_n"][0, b]),
                 st_m=f(I["state_mlstm_m"][0, b][None]), st_re=f(I["state_s5_re"][0, b].reshape(16, 128)),
                 st_im=f(I["state_s5_im"][0, b].reshape(16, 128)), c_sbk=f(I["cache_sb_k"][0, b].reshape(PAST, 512)),
                 c_sbv=f(I["cache_sb_v"][0, b].reshape(PAST, 512)))
        maps.append(m)
    return maps


def assemble(results, S):
    def st(name, shape):
        return np.stack([np.asarray(r[name], dtype=np.float32).reshape(shape) for r in results], 0)
    outs = [st("y_p", (S, D)), st("y_s", (64, D))]
    for nm, n in (("p", S), ("s", 64)):
        outs += [st("lat_" + nm, (n, 256))[None], st("kr_" + nm, (n, 64))[None], st("C_" + nm, (4, 128, 128))[None],
                 st("n_" + nm, (4, 128))[None], st("m_" + nm, (4,))[None], st("re_" + nm, (32, 64))[None],
                 st("im_" + nm, (32, 64))[None], st("k_" + nm, (n, 8, 64))[None], st("v_" + nm, (n, 8, 64))[None]]
    return tuple(outs)


_S, _PAST = 8192, 2048


def kernel(**inputs):
    ncores = 8
    b = Builder(_S, _PAST)
    nc = build(b)
    maps = make_inmaps(inputs, ncores, _S, _PAST)
    res = run_bass_kernel_spmd(nc, maps, core_ids=list(range(ncores)))
    return assemble(res.results, _S)
```

```python
import os
import numpy as np
import concourse.bass as bass
import concourse.mybir as mybir
from concourse.bass_utils import run_bass_kernel_spmd

F32 = mybir.dt.float32
BF16 = mybir.dt.bfloat16
I32 = mybir.dt.int32
ALU = mybir.AluOpType
AF = mybir.ActivationFunctionType
AX = mybir.AxisListType

CENG = ['pe', 'act', 'dve', 'pool']
ALLENG = CENG + ['sp']
EIDX = {e: i for i, e in enumerate(ALLENG)}
BLK = 32
DBLK = 2048
N_DSEM = 48
SEM_ROLL = 30000


def _esize(dt):
    return {F32: 4, BF16: 2, I32: 4}.get(dt, None) or mybir.dt.size(dt)


class Op:
    __slots__ = ('eng', 'fn', 'deps', 'id', 'pos', 'signal', 'semgen', 'semval', 'waits',
                 'is_dma', 'dk', 'sk', 'prev_dma')

    def __init__(self, eng, fn, is_dma):
        self.eng = eng
        self.fn = fn
        self.is_dma = is_dma
        self.deps = set()
        self.signal = False
        self.waits = []
        self.pos = -1
        self.sk = None
        self.prev_dma = None


class Space:
    def __init__(self, nblocks):
        self.last_w = np.full(nblocks, -1, dtype=np.int64)
        self.last_r = np.full((nblocks, len(CENG)), -1, dtype=np.int64)
        self.dma_readers = []


class KB:
    def __init__(self):
        self.nc = bass.Bass("TRN2", target_bir_lowering=False)
        self.ops = []
        self.spaces = {}
        self.untracked = set()
        self.n_dma = 0
        self.dma_ops = []
        self.dry = False
        self.keyw = {}
        self.last_compute = None
        self.cserial = True

    def _range(self, ap):
        sp = str(ap.space)
        t = ap.tensor
        name = ap.name
        es = _esize(ap.dtype)
        if 'DRAM' in sp:
            return None
            lo = ap.offset
            hi = ap.offset
            for st, cnt in ap.ap:
                if st >= 0:
                    hi += st * (cnt - 1)
                else:
                    lo += st * (cnt - 1)
            key = 'D:' + name
            if key not in self.spaces:
                n = 1
                for s in t.shape:
                    n *= s
                self.spaces[key] = Space((n * es + DBLK - 1) // DBLK)
            return key, (lo * es) // DBLK, (hi * es + es - 1) // DBLK + 1
        rowlen = ap.ap[0][0]
        if rowlen == 0:
            rowlen = t.shape[1] if len(t.shape) == 2 else int(np.prod(t.shape[1:]))
        off = ap.offset % rowlen
        lo = off
        hi = off
        for st, cnt in ap.ap[1:]:
            if st >= 0:
                hi += st * (cnt - 1)
            else:
                lo += st * (cnt - 1)
        key = ('P:' if 'PSUM' in sp else 'S:') + name
        if key not in self.spaces:
            self.spaces[key] = Space((rowlen * es + BLK - 1) // BLK)
        return key, (lo * es) // BLK, (hi * es + es - 1) // BLK + 1

    def _hazards(self, op, reads, writes):
        deps = op.deps
        slot = EIDX[op.eng] if (not op.is_dma and op.eng in CENG) else None
        for ap in reads:
            r = self._range(ap)
            if r is None:
                continue
            key, lo, hi = r
            s = self.spaces[key]
            deps.update(np.unique(s.last_w[lo:hi]).tolist())
            if slot is not None:
                s.last_r[lo:hi, slot] = op.id
            else:
                s.dma_readers.append((op.id, lo, hi))
        for ap in writes:
            r = self._range(ap)
            if r is None:
                continue
            key, lo, hi = r
            s = self.spaces[key]
            deps.update(np.unique(s.last_w[lo:hi]).tolist())
            deps.update(np.unique(s.last_r[lo:hi]).tolist())
            if s.dma_readers:
                keep = []
                for (oid, l2, h2) in s.dma_readers:
                    if l2 < hi and lo < h2:
                        deps.add(oid)
                        if l2 < lo:
                            keep.append((oid, l2, lo))
                        if hi < h2:
                            keep.append((oid, hi, h2))
                    else:
                        keep.append((oid, l2, h2))
                s.dma_readers = keep
            s.last_w[lo:hi] = op.id
            s.last_r[lo:hi] = -1
        deps.discard(-1)
        deps.discard(op.id)

    def op(self, eng, fn, reads, writes, is_dma=False, rkeys=(), wkeys=()):
        if self.dry:
            return None
        o = Op(eng, fn, is_dma)
        o.id = len(self.ops)
        self.ops.append(o)
        self._hazards(o, reads, writes)
        for k in rkeys:
            o.deps.update(self.keyw.get(k, ()))
        for k in wkeys:
            self.keyw.setdefault(k, []).append(o.id)
        if eng == 'pe':
            o.deps = {d for d in o.deps if self.ops[d].eng != 'pe' or self.ops[d].is_dma}
        if os.environ.get("FW_SERIAL") and o.id > 0:
            o.deps.add(o.id - 1)
        if not is_dma:
            pc = self.last_compute
            if self.cserial and pc is not None and not (eng == 'pe' and pc.eng == 'pe'):
                o.deps.add(pc.id)
            self.last_compute = o
        if is_dma:
            o.dk = self.n_dma
            self.n_dma += 1
            self.dma_ops.append(o)
            if o.dk >= N_DSEM:
                o.prev_dma = self.dma_ops[o.dk - N_DSEM]
                o.deps.add(o.prev_dma.id)
        return o

    def dma(self, out, in_, eng='sp', rkeys=(), wkeys=(), **kw):
        if True:
            eng = 'sp'
        return self.op(eng, lambda e: e.dma_start(out=out, in_=in_, **kw), [in_], [out], is_dma=True,
                       rkeys=rkeys, wkeys=wkeys)

    def mm(self, out, lhsT, rhs, start=True, stop=True, **kw):
        return self.op('pe', lambda e: e.matmul(out, lhsT, rhs, start=start, stop=stop, **kw),
                       [lhsT, rhs], [out])

    def tr(self, out, in_, ident):
        return self.op('pe', lambda e: e.transpose(out, in_, ident), [in_, ident], [out])

    def act(self, out, in_, func, bias=None, scale=1.0, accum_out=None, eng='act'):
        rd = [in_]
        kw = {}
        if bias is not None:
            kw['bias'] = bias
            if not isinstance(bias, (int, float)):
                rd.append(bias)
        if not isinstance(scale, (int, float)):
            rd.append(scale)
        wr = [out]
        if accum_out is not None:
            kw['accum_out'] = accum_out
            wr.append(accum_out)
        return self.op(eng, lambda e: e.activation(out=out, in_=in_, func=func, scale=scale, **kw), rd, wr)

    def tt(self, out, in0, in1, op, eng='dve'):
        return self.op(eng, lambda e: e.tensor_tensor(out=out, in0=in0, in1=in1, op=op), [in0, in1], [out])

    def ts(self, out, in0, s1, op0, s2=None, op1=None, eng='dve', accum_out=None):
        rd = [in0]
        if not isinstance(s1, (int, float)):
            rd.append(s1)
        if s2 is not None and not isinstance(s2, (int, float)):
            rd.append(s2)
        kw = {}
        if op1 is not None:
            kw['op1'] = op1
        wr = [out]
        if accum_out is not None:
            kw['accum_out'] = accum_out
            wr.append(accum_out)
        return self.op(eng, lambda e: e.tensor_scalar(out=out, in0=in0, scalar1=s1, scalar2=s2, op0=op0, **kw), rd, wr)

    def stt(self, out, in0, scalar, in1, op0, op1, eng='dve'):
        rd = [in0, in1]
        if not isinstance(scalar, (int, float)):
            rd.append(scalar)
        return self.op(eng, lambda e: e.scalar_tensor_tensor(out=out, in0=in0, scalar=scalar, in1=in1, op0=op0, op1=op1), rd, [out])

    def copy(self, out, in_, eng='dve'):
        if eng == 'act':
            return self.act(out, in_, AF.Copy)
        return self.op(eng, lambda e: e.tensor_copy(out=out, in_=in_), [in_], [out])

    def memset(self, out, val, eng='dve'):
        return self.op(eng, lambda e: e.memset(out, val), [], [out])

    def scan(self, out, d0, d1, initial, op0, op1):
        rd = [d0, d1]
        if not isinstance(initial, (int, float)):
            rd.append(initial)
        return self.op('dve', lambda e: e.tensor_tensor_scan(out=out, data0=d0, data1=d1, initial=initial, op0=op0, op1=op1), rd, [out])

    def reduce(self, out, in_, op, axis=AX.X, eng='dve'):
        return self.op(eng, lambda e: e.tensor_reduce(out=out, in_=in_, axis=axis, op=op), [in_], [out])

    def recip(self, out, in_):
        return self.op('dve', lambda e: e.reciprocal(out=out, in_=in_), [in_], [out])

    def analyze(self):
        nce = len(CENG)
        cur = {e: np.zeros(nce, dtype=np.int64) for e in ALLENG}
        dknown = {e: set() for e in ALLENG}
        count = {e: 0 for e in CENG}
        byeng = {e: [] for e in CENG}
        for o in self.ops:
            e = o.eng
            c = cur[e]
            dk = dknown[e]
            need = {}
            for d in o.deps:
                dop = self.ops[d]
                if dop.is_dma:
                    if d not in dk or os.environ.get("FW_NOPRUNE"):
                        dk.add(d)
                        o.waits.append(dop)
                        np.maximum(c, dop.sk, out=c)
                else:
                    b = EIDX[dop.eng]
                    if dop.pos + 1 > need.get(b, 0):
                        need[b] = dop.pos + 1
            noprune = bool(os.environ.get("FW_NOPRUNE"))
            for b, p in sorted(need.items(), key=lambda kv: -kv[1]):
                if c[b] >= p and not noprune:
                    continue
                t = byeng[CENG[b]][p - 1]
                t.signal = True
                o.waits.append(t)
                np.maximum(c, t.sk, out=c)
                c[b] = max(c[b], p)
            o.sk = c.copy()
            if not o.is_dma and e in CENG:
                o.pos = count[e]
                count[e] += 1
                byeng[e].append(o)
        self.ngen = {}
        for e in CENG:
            n = 0
            for o in byeng[e]:
                if o.signal:
                    o.semgen = n // SEM_ROLL
                    o.semval = n % SEM_ROLL + 1
                    n += 1
            self.ngen[e] = n // SEM_ROLL + 1
        self.byeng = byeng

    def emit(self, stack):
        nc = self.nc
        esem = {e: [stack.enter_context(nc.semaphore(f"s_{e}{g}")) for g in range(self.ngen[e])] for e in CENG}
        dsem = [stack.enter_context(nc.semaphore(f"d{i}")) for i in range(min(N_DSEM, max(self.n_dma, 1)))]
        block = stack.enter_context(nc.Block())
        ops = self.ops
        n_dma = self.n_dma

        def stream(ename):
            def body(eng):
                for o in ops:
                    if o.eng != ename:
                        continue
                    for w in o.waits:
                        if w.is_dma:
                            eng.wait_ge(dsem[w.dk % N_DSEM], 16 * (w.dk // N_DSEM + 1))
                        else:
                            eng.wait_ge(esem[w.eng][w.semgen], w.semval)
                    ins = o.fn(eng)
                    if o.is_dma:
                        ins.then_inc(dsem[o.dk % N_DSEM], 16)
                    elif o.signal:
                        ins.then_inc(esem[o.eng][o.semgen], 1)
                if ename == 'sp':
                    for i in range(min(N_DSEM, n_dma)):
                        cnt = (n_dma - 1 - i) // N_DSEM + 1
                        eng.wait_ge(dsem[i], 16 * cnt)
            return body

        block.sync(stream('sp'))
        block.tensor(stream('pe'))
        block.scalar(stream('act'))
        block.vector(stream('dve'))
        block.gpsimd(stream('pool'))

    def stats(self):
        from collections import Counter
        c = Counter(o.eng + ('_dma' if o.is_dma else '') for o in self.ops)
        w = sum(len(o.waits) for o in self.ops)
        s = sum(1 for o in self.ops if o.signal)
        return dict(c), w, s

from contextlib import ExitStack

D = 1024
DFF = 2816
NJ = 22
EPS = 1e-6
MAGIC = 12582912.0
TWO_PI = float(2 * np.pi)
ARENA_F = 47104


def bc(ap, shape):
    return ap.to_broadcast(list(shape))


class Arena:
    def __init__(self, base):
        self.base = base
        self.top = 0
        self.marks = []
        self.peak = 0

    def push(self):
        self.marks.append(self.top)

    def pop(self):
        self.top = self.marks.pop()

    def alloc(self, dims, dt=F32, parts=128):
        if isinstance(dims, int):
            dims = (dims,)
        n = int(np.prod(dims))
        es = 4 if dt == F32 else 2
        nb = (n * es + 63) // 64 * 64
        lo = self.top
        self.top += nb
        self.peak = max(self.peak, self.top)
        assert self.top <= ARENA_F * 4, f"arena overflow {self.top}"
        a = self.base[:, lo // 4:(lo + nb) // 4]
        if dt != F32:
            a = a.bitcast(dt)
        a = a[0:parts, 0:n]
        if len(dims) == 2:
            a = a.rearrange("p (a b) -> p a b", a=dims[0])
        elif len(dims) == 3:
            a = a.rearrange("p (a b c) -> p a b c", a=dims[0], b=dims[1])
        return a


class Ring:
    def __init__(self, bld, nbytes):
        self.b = bld
        self.size = nbytes
        self.region = bld.ar.alloc(nbytes // 4)
        self.plan = bld.plan
        self.dry = bld.kb.dry
        self.newplan = []
        self.issued = 0
        self.cons = 0
        self.head = 0
        self.live = []
        self.aps = {}
        self.held = set()

    def _view(self, lo, parts, n):
        a = self.region[:, lo // 4:(lo + (n * 2 + 3) // 4 * 4) // 4].bitcast(BF16)
        return a[0:parts, 0:n]

    def _try_issue(self):
        src, parts, n, rkeys = self.plan[self.issued]
        if any(k not in self.b.done for k in rkeys):
            return False
        nb = (n * 2 + 63) // 64 * 64
        lo = self.head
        if lo + nb > self.size:
            lo = 0
        hi = lo + nb
        for (_, l2, h2) in self.live:
            if l2 < hi and lo < h2:
                return False
        ap = self._view(lo, parts, n)
        dst = ap
        if len(src.shape) == 3:
            dst = ap.rearrange("p (a b) -> p a b", a=src.shape[1])
        self.b.kb.dma(dst, src, rkeys=rkeys)
        self.live.append((self.issued, lo, hi))
        self.aps[self.issued] = ap
        self.head = hi
        self.issued += 1
        return True

    def unhold(self):
        self.held = set()

    def fetch(self, src, rkeys=(), hold=False):
        parts, n = src.shape[0], int(np.prod(src.shape[1:]))
        if self.dry:
            self.newplan.append((src, parts, n, tuple(rkeys)))
            return self._view(0, parts, n)
        k = self.cons
        if hold:
            self.held.add(k)
        self.live = [x for x in self.live if x[0] >= k or x[0] in self.held]
        while self.issued < len(self.plan) and (self.issued - k) < 24:
            if not self._try_issue():
                break
        assert self.issued > k, "ring too small"
        ps, pp, pn, _ = self.plan[k]
        assert (pp, pn) == (parts, n), "plan mismatch"
        self.cons += 1
        return self.aps.pop(k)


class Group:
    pass


class Builder:
    def __init__(self, S, PAST, plan=None):
        self.S, self.PAST = S, PAST
        self.plan = plan
        self.kb = KB()
        self.kb.dry = plan is None
        self.nc = self.kb.nc
        self.st = ExitStack()
        self.din = {}
        self.dout = {}
        self._evac_i = 0
        self.ps_i = 0
        self.ps_res = set()
        self.debug = False

    def inp(self, name, shape, dt=F32):
        t = self.nc.dram_tensor(name, list(shape), dt, kind="ExternalInput").ap()
        self.din[name] = t
        return t

    def outp(self, name, shape):
        t = self.nc.dram_tensor(name, list(shape), F32, kind="ExternalOutput").ap()
        self.dout[name] = t
        return t

    def scratch(self, name, shape, dt=BF16):
        return self.nc.dram_tensor(name, list(shape), dt).ap()

    def dbg(self, name, ap, dims):
        if name not in self.dout:
            self.outp(name, [128] + list(dims))
        self.ar.push()
        t = self.ar.alloc(dims)
        self.kb.copy(t, ap)
        self.kb.dma(self.dout[name], t)
        self.ar.pop()

    def psum(self):
        while True:
            i = self.ps_i % 8
            self.ps_i += 1
            if i not in self.ps_res:
                return self.ps[i]

    def reserve(self, n):
        out = []
        for i in range(8):
            if i not in self.ps_res and len(out) < n:
                self.ps_res.add(i)
                out.append(i)
        return out

    def release(self, idxs):
        for i in idxs:
            self.ps_res.discard(i)

    def evac(self, out, in_, scale=None):
        self._evac_i += 1
        if self._evac_i % 2 == 0:
            if scale is None:
                self.kb.copy(out, in_, eng='dve')
            else:
                self.kb.ts(out, in_, float(scale), ALU.mult)
        else:
            self.kb.act(out, in_, AF.Copy, scale=1.0 if scale is None else float(scale))

    def declare(self):
        S, PAST = self.S, self.PAST
        i = self.inp
        self.xp = i("xp", [S, D]); self.xs = i("xs", [64, D])
        self.c_lat = i("c_lat", [PAST, 256]); self.c_kr = i("c_kr", [PAST, 64])
        self.st_C = i("st_C", [4, 128, 128]); self.st_n = i("st_n", [4, 128]); self.st_m = i("st_m", [1, 4])
        self.st_re = i("st_re", [16, 128]); self.st_im = i("st_im", [16, 128])
        self.c_sbk = i("c_sbk", [PAST, 512]); self.c_sbv = i("c_sbv", [PAST, 512])
        self.w_gate = i("w_gate", [4, D, DFF]); self.w_up = i("w_up", [4, D, DFF]); self.w_down = i("w_down", [4, DFF, D])
        self.e_win = i("e_win", [D, 2760]); self.e_wout = i("e_wout", [D, D])
        self.wuq_d = i("wuq", [384, 768]); self.wukv_d = i("wukv", [256, 1024])
        self.o_win = i("o_win", [D, 2048]); self.o_wout = i("o_wout", [D, D]); self.wglu_d = i("wglu", [512, 512])
        self.grows = i("grows", [69, 128]); self.bi_d = i("bi", [1, 4]); self.bf_d = i("bf", [1, 4])
        self.s5rows = i("s5rows", [48, 128])
        self.s5B = [i("s5B_re", [2048, 16]), i("s5B_im", [2048, 16])]
        self.s5C = [i("s5C_re", [512, 64]), i("s5C_im", [512, 64])]
        self.c_ident = i("c_ident", [128, 128]); self.c_masksb = i("c_masksb", [128, 128]); self.c_maskml = i("c_maskml", [64, 64])
        self.c_uincl = i("c_uincl", [128, 128]); self.c_sel = i("c_sel", [8, 1024]); self.c_cm = i("c_cm", [128, 512])
        self.c_iota = i("c_iota", [128, 128])
        self.c_ropep = [i("c_cosp", [64, S]), i("c_sinp", [64, S])]
        self.c_ropes = [i("c_coss", [64, 64]), i("c_sins", [64, 64])]
        o = self.outp
        self.groups = []
        for g, (n, nm) in enumerate([(S, "p"), (64, "s")]):
            G = Group()
            G.name = nm; G.n = n
            G.T = 512 if nm == "p" else 64
            G.ntile = n // G.T
            G.npast = 0 if nm == "p" else PAST // 512
            G.pos0 = 0 if nm == "p" else PAST
            G.x = self.xp if nm == "p" else self.xs
            G.rope = self.c_ropep if nm == "p" else self.c_ropes
            G.y = o("y_" + nm, [n, D]); G.lat = o("lat_" + nm, [n, 256]); G.kr = o("kr_" + nm, [n, 64])
            G.oC = o("C_" + nm, [4, 128, 128]); G.on = o("n_" + nm, [4, 128]); G.om = o("m_" + nm, [1, 4])
            G.ore = o("re_" + nm, [16, 128]); G.oim = o("im_" + nm, [16, 128])
            G.ok = o("k_" + nm, [n, 512]); G.ov = o("v_" + nm, [n, 512])
            NT = G.npast + G.ntile
            G.KnT = self.scratch("KnT_" + nm, [NT, 128, 2048]); G.KrT = self.scratch("KrT_" + nm, [NT, 64, 512])
            G.V = self.scratch("V_" + nm, [NT, 128, 2048])
            G.sKT = self.scratch("sKT_" + nm, [NT, 128, 2048]); G.sV = self.scratch("sV_" + nm, [NT, 128, 2048])
            self.groups.append(G)
        sc = self.scratch
        self.wgu = sc("wgu", [4, NJ, 128, 2048]); self.wdn = sc("wdn", [4, 8, 128, DFF])
        self.winF_e = sc("winF_e", [19, 128, 1024]); self.winT_e = sc("winT_e", [2, 128, 4096]); self.wout_e = sc("wout_e", [8, 128, 1024])
        self.winF_o = sc("winF_o", [12, 128, 1024]); self.winT_o = sc("winT_o", [2, 128, 4096]); self.wout_o = sc("wout_o", [8, 128, 1024])
        self.s5tab = sc("s5tab", [128, 4096 + 4096], F32)
        self.s5mat = sc("s5mat", [128, 8192], BF16)
        nc = self.nc
        self.arena = self.st.enter_context(nc.sbuf_tensor("arena", [128, ARENA_F], F32))
        self.ps = [self.st.enter_context(nc.psum_tensor(f"ps{k}", [128, 512], F32)) for k in range(8)]

    def load_consts(self):
        kb, ar = self.kb, self.ar
        A = ar.alloc
        self.ident = A(128); kb.dma(self.ident, self.c_ident)
        t = A(128); kb.dma(t, self.c_masksb)
        self.masksb = A(128, BF16); kb.copy(self.masksb, t)
        self.maskml = A(64, parts=64); kb.dma(self.maskml, self.c_maskml)
        t2 = A(128); kb.dma(t2, self.c_uincl)
        self.uincl = A(128, BF16); kb.copy(self.uincl, t2)
        self.ones = A(128, BF16); kb.memset(self.ones, 1.0)
        self.negones = A(128, BF16); kb.memset(self.negones, -1.0)
        self.onef = A(8); kb.memset(self.onef, 1.0)
        self.sel = A((8, 128), parts=8); kb.dma(self.sel, self.c_sel.rearrange("r (g p) -> r g p", g=8))
        self.cm = A(512); kb.dma(self.cm, self.c_cm)
        gr = A(128); kb.memset(gr, 0.0); kb.dma(gr[0:69, :], self.grows)
        p = self.psum()
        kb.tr(p[:, 0:128], gr, self.ident)
        self.gains = A(72); kb.copy(self.gains[:, 0:69], p[:, 0:69])
        self.bi = A(4); kb.dma(self.bi, bc(self.bi_d[0:1, :], [128, 4]))
        bfp = A(4); kb.dma(bfp, bc(self.bf_d[0:1, :], [128, 4]))
        self.nbf = A(4); kb.ts(self.nbf, bfp, -1.0, ALU.mult)
        self.Cst = A((4, 129)); self.mst = A(4); self.wst = A((16, 2))
        self.wuq = A((3, 768), BF16); self.wukv = A((2, 1024), BF16); self.wglu = A((4, 512), BF16)

    def conv_rows(self, src, ncols, stores, wkeys):
        kb, ar = self.kb, self.ar
        st = self.cv_f[self.cv_i % 2]; sb = self.cv_b[self.cv_i % 2]
        self.cv_i += 1
        kb.dma(st[:, 0:ncols], src)
        h = (ncols // 2 + 63) // 64 * 64
        engs = ['act', 'dve', 'pool']
        e1 = engs[self.cv_i % 3]; e2 = engs[(self.cv_i + 1) % 3]
        kb.copy(sb[:, 0:h], st[:, 0:h], eng=e1)
        kb.copy(sb[:, h:ncols], st[:, h:ncols], eng=e2)
        for dst, c0, c1, shape in stores:
            v = sb[:, c0:c1]
            if shape is not None:
                v = v.rearrange("p (a b) -> p a b", a=shape[0])
            kb.dma(dst, v, eng='pool', wkeys=wkeys)

    def prologue_weights(self):
        kb, ar = self.kb, self.ar
        ar.push()
        self.cv_f = [ar.alloc(DFF) for _ in range(2)]
        self.cv_b = [ar.alloc(DFF, BF16) for _ in range(2)]
        self.cv_i = 0
        for f in range(4):
            for kc in range(8):
                for g, W in enumerate((self.w_gate, self.w_up)):
                    dst = self.wgu[f].rearrange("j p x -> p j x")[:, :, kc * 256 + g * 128: kc * 256 + g * 128 + 128]
                    self.conv_rows(W[f, kc * 128:(kc + 1) * 128, :], DFF, [(dst, 0, DFF, (NJ, 128))], [("wgu", f)])
            for jc in range(NJ):
                dst = self.wdn[f].rearrange("m p x -> p m x")[:, :, jc * 128:(jc + 1) * 128]
                self.conv_rows(self.w_down[f, jc * 128:(jc + 1) * 128, :], D, [(dst, 0, D, (8, 128))], [("wdn", f)])
        for kc in range(8):
            F = self.winF_e.rearrange("n p x -> p n x")
            ks = slice(kc * 128, (kc + 1) * 128)
            stores = [(F[:, 0:4, ks], 0, 512, (4, 128)), (F[:, 4:8, ks], 512, 1024, (4, 128)),
                      (F[:, 8:12, ks], 1536, 2048, (4, 128)), (F[:, 12:15, ks], 2056, 2440, (3, 128)),
                      (F[:, 15:17, ks], 2440, 2696, (2, 128)),
                      (self.winF_e[17][:, kc * 128:kc * 128 + 64], 2696, 2760, None),
                      (self.winF_e[18][:, kc * 128:kc * 128 + 8], 2048, 2056, None),
                      (self.winT_e[0][:, kc * 512:(kc + 1) * 512], 512, 1024, None),
                      (self.winT_e[1][:, kc * 512:(kc + 1) * 512], 1024, 1536, None)]
            self.conv_rows(self.e_win[ks, :], 2760, stores, [("we",)])
            Fo = self.winF_o.rearrange("n p x -> p n x")
            stores = [(Fo[:, 0:12, ks], 0, 1536, (12, 128)),
                      (self.winT_o[0][:, kc * 512:(kc + 1) * 512], 1024, 1536, None),
                      (self.winT_o[1][:, kc * 512:(kc + 1) * 512], 1536, 2048, None)]
            self.conv_rows(self.o_win[ks, :], 2048, stores, [("wo",)])
            for W, dstt, key in ((self.e_wout, self.wout_e, "we"), (self.o_wout, self.wout_o, "wo")):
                dst = dstt.rearrange("m p x -> p m x")[:, :, ks]
                self.conv_rows(W[ks, :], D, [(dst, 0, D, (8, 128))], [(key,)])
        for kc in range(3):
            stg = self.cv_f[self.cv_i % 2]; self.cv_i += 1
            kb.dma(stg[:, 0:768], self.wuq_d[kc * 128:(kc + 1) * 128, :])
            kb.copy(self.wuq[:, kc, :], stg[:, 0:768], eng='dve')
        for kc in range(2):
            stg = self.cv_f[self.cv_i % 2]; self.cv_i += 1
            kb.dma(stg[:, 0:1024], self.wukv_d[kc * 128:(kc + 1) * 128, :])
            kb.copy(self.wukv[:, kc, :], stg[:, 0:1024], eng='act')
        for kc in range(4):
            stg = self.cv_f[self.cv_i % 2]; self.cv_i += 1
            kb.dma(stg[:, 0:512], self.wglu_d[kc * 128:(kc + 1) * 128, :])
            kb.copy(self.wglu[:, kc, :], stg[:, 0:512], eng='dve')
        ar.pop()

    def rmsnorm(self, xT, C, T, gcol, out, dim):
        kb, ar = self.kb, self.ar
        ar.push()
        sq = ar.alloc((C, T), BF16)
        kb.act(sq, xT, AF.Square)
        p = self.psum()
        for c in range(C):
            kb.mm(p[:, 0:T], self.ones, sq[:, c, :], start=(c == 0), stop=(c == C - 1))
        rs = ar.alloc(T)
        kb.act(rs, p[:, 0:T], AF.Sqrt, scale=1.0 / dim, bias=self.epsc[:, 0:1])
        kb.recip(rs, rs)
        for c in range(C):
            kb.stt(out[:, c, :], xT[:, c, :], self.gains[:, gcol + c:gcol + c + 1], rs, ALU.mult, ALU.mult)
        ar.pop()

    def transpose_out(self, srcT, rows, T, dst, eng_out='pool'):
        kb, ar = self.kb, self.ar
        tb = min(128, T); nb = T // tb
        tot = sum(r for _, r in srcT)
        ar.push()
        stg = ar.alloc((nb, tot), parts=tb)
        c0 = 0
        for ap, r in srcT:
            p = self.psum()
            if r < 128 and tb < 128:
                tmp = ar.alloc(128, parts=r)
                kb.memset(tmp, 0.0)
                kb.copy(tmp[:, 0:tb], ap[0:r, 0:tb])
                kb.tr(p[:, 0:r], tmp, self.ident[0:r, 0:r])
            else:
                for b in range(nb):
                    kb.tr(p[0:tb, b * r:(b + 1) * r], ap[0:r, b * tb:(b + 1) * tb], self.ident[0:r, 0:r])
            self.evac(stg[:, :, c0:c0 + r], p[0:tb, 0:nb * r].rearrange("p (b r) -> p b r", b=nb))
            c0 += r
        kb.dma(dst.rearrange("(b p) f -> p b f", p=tb), stg, eng=eng_out)
        ar.pop()

    def load_xT(self, G, ti, xT):
        kb, ar = self.kb, self.ar
        T = G.T; tb = min(128, T); nb = T // tb
        ar.push()
        stg = ar.alloc((nb, D), parts=tb)
        kb.dma(stg, G.x[ti * T:(ti + 1) * T, :].rearrange("(b p) f -> p b f", p=tb))
        for c in range(8):
            p = self.psum()
            for b in range(nb):
                kb.tr(p[:, b * tb:(b + 1) * tb], stg[:, b, c * 128:(c + 1) * 128], self.ident[0:tb, 0:tb])
            self.evac(xT[:, c, :], p[:, 0:T])
        ar.pop()

    def ffn(self, xT, T, f):
        kb, ar = self.kb, self.ar
        ar.push()
        hn = ar.alloc((8, T), BF16)
        self.rmsnorm(xT, 8, T, f * 8, hn, D)
        a = ar.alloc((NJ, T), BF16)
        sg = [ar.alloc(T) for _ in range(2)]
        for j in range(NJ):
            w = self.ring.fetch(self.wgu[f, j], rkeys=[("wgu", f)]).rearrange("p (k g c) -> p k g c", k=8, g=2)
            pg = self.psum(); pu = self.psum()
            for kc in range(8):
                kb.mm(pg[:, 0:T], w[:, kc, 0, :], hn[:, kc, :], start=(kc == 0), stop=(kc == 7))
            for kc in range(8):
                kb.mm(pu[:, 0:T], w[:, kc, 1, :], hn[:, kc, :], start=(kc == 0), stop=(kc == 7))
            s = sg[j % 2]
            kb.act(s, pg[:, 0:T], AF.Silu)
            kb.tt(a[:, j, :], s, pu[:, 0:T], ALU.mult)
        for m in range(8):
            w = self.ring.fetch(self.wdn[f, m], rkeys=[("wdn", f)]).rearrange("p (j c) -> p j c", j=NJ)
            p = self.psum()
            for j in range(NJ):
                kb.mm(p[:, 0:T], w[:, j, :], a[:, j, :], start=(j == 0), stop=(j == NJ - 1))
            kb.stt(xT[:, m, :], p[:, 0:T], 0.5, xT[:, m, :], ALU.mult, ALU.add)
        ar.pop()

    def out_proj(self, xT, T, mixed, wout, key):
        kb = self.kb
        for m in range(8):
            w = self.ring.fetch(wout[m], rkeys=[(key,)]).rearrange("p (k c) -> p k c", k=8)
            p = self.psum()
            for kc in range(8):
                kb.mm(p[:, 0:T], w[:, kc, :], mixed[:, kc, :], start=(kc == 0), stop=(kc == 7))
            kb.tt(xT[:, m, :], p[:, 0:T], xT[:, m, :], ALU.add)

    def inF(self, winF, ci, key, hn, T, ncols=128):
        kb = self.kb
        w = self.ring.fetch(winF[ci], rkeys=[(key,)]).rearrange("p (k c) -> p k c", k=8)
        p = self.psum()
        for kc in range(8):
            kb.mm(p[0:ncols, 0:T], w[:, kc, 0:ncols], hn[:, kc, :], start=(kc == 0), stop=(kc == 7))
        return p[0:ncols, 0:T]

    def even_mixer(self, G, ti, xT):
        kb, ar = self.kb, self.ar
        T = G.T; NC = T // 64
        kt = G.npast + ti
        ar.push()
        hn = ar.alloc((8, T), BF16)
        self.rmsnorm(xT, 8, T, 32 + 0, hn, D)
        mixed = ar.alloc((8, T), BF16)
        ar.push()
        pgt = self.inF(self.winF_e, 18, "we", hn, T, ncols=8)
        gT = ar.alloc(T, parts=8); kb.copy(gT, pgt, eng='act')
        ga = ar.alloc((4, T)); gb = ar.alloc((4, T))
        for g in range(8):
            p = self.psum()
            kb.mm(p[:, 0:T], self.sel[0:8, g, :], gT[0:8, :])
            h = g % 4
            if g < 4:
                kb.act(ga[:, h, :], p[:, 0:T], AF.Identity, bias=self.bi[:, h:h + 1])
            else:
                kb.act(gb[:, h, :], p[:, 0:T], AF.Exp, scale=-1.0, bias=self.nbf[:, h:h + 1])
        gl = ar.alloc((4, T))
        kb.act(gl, gb, AF.Ln, bias=self.onec[:, 0:1])
        for h in range(4):
            kb.scan(gb[:, h, :], self.cm[:, 0:T], gl[:, h, :], 0.0, ALU.mult, ALU.add)
        kb.tt(ga, ga, gb, ALU.add)
        amax = ar.alloc((4, NC)); mext = ar.alloc((4, NC + 1)); Ac = ar.alloc((4, NC)); wi = ar.alloc((4, NC))
        kb.reduce(amax.rearrange("p h c -> p (h c)"), ga.rearrange("p h (c t) -> p (h c) t", t=64), ALU.max)
        gbend = gb.rearrange("p h (c t) -> p h c t", t=64)[:, :, :, 63]
        kb.copy(mext[:, :, 0], self.mst)
        for h in range(4):
            kb.scan(mext[:, h, 1:NC + 1], amax[:, h, :], gbend[:, h, :], self.mst[:, h:h + 1], ALU.max, ALU.subtract)
        kb.tt(Ac, mext[:, :, 1:NC + 1], gbend, ALU.add)
        kb.tt(wi, mext[:, :, 0:NC], Ac, ALU.subtract)
        kb.act(wi, wi, AF.Exp)
        kb.copy(self.mst, mext[:, :, NC])
        Ab = bc(Ac.rearrange("p h c -> p (h c)").unsqueeze(2), [128, 4 * NC, 64])
        gav = ga.rearrange("p h (c t) -> p (h c) t", t=64); gbv = gb.rearrange("p h (c t) -> p (h c) t", t=64)
        kb.tt(gav, gav, Ab, ALU.subtract); kb.act(ga, ga, AF.Exp)
        kb.tt(gbv, gbv, Ab, ALU.subtract); kb.act(gb, gb, AF.Exp)
        pw = self.psum()
        for c in range(NC):
            for h in range(4):
                kb.mm(pw[0:64, c * 4 + h:c * 4 + h + 1], ga[0:1, h, c * 64:(c + 1) * 64], self.onef[0:1, 0:1])
        wacol = ar.alloc((NC, 4), parts=64)
        kb.copy(wacol.rearrange("p c h -> p (c h)"), pw[0:64, 0:NC * 4])
        qmT = ar.alloc((4, T), BF16); kmT = ar.alloc((4, T), BF16); osig = ar.alloc((4, T))
        for h in range(4):
            p = self.inF(self.winF_e, h, "we", hn, T)
            kb.copy(qmT[:, h, :], p, eng='act')
        for h in range(4):
            p = self.inF(self.winF_e, 4 + h, "we", hn, T)
            kb.stt(kmT[:, h, :], p, 128 ** -0.5, ga[:, h, :], ALU.mult, ALU.mult)
        for h in range(4):
            p = self.inF(self.winF_e, 8 + h, "we", hn, T)
            kb.act(osig[:, h, :], p, AF.Sigmoid)
        ktok = ar.alloc((NC, 4, 128), BF16, parts=64); vaug = ar.alloc((NC, 4, 129), BF16, parts=64)
        kb.memset(vaug[:, :, :, 128:129], 1.0, eng='pool')
        wk = self.ring.fetch(self.winT_e[0], rkeys=[("we",)]).rearrange("p (k n) -> p k n", k=8)
        for c in range(NC):
            p = self.psum()
            for kc in range(8):
                kb.mm(p[0:64, :], hn[:, kc, c * 64:(c + 1) * 64], wk[:, kc, :], start=(kc == 0), stop=(kc == 7))
            for h in range(4):
                kb.ts(ktok[:, c, h, :], p[0:64, h * 128:(h + 1) * 128], wacol[:, c, h:h + 1], ALU.mult, 128 ** -0.5, ALU.mult)
        wv = self.ring.fetch(self.winT_e[1], rkeys=[("we",)]).rearrange("p (k n) -> p k n", k=8)
        for c in range(NC):
            p = self.psum()
            for kc in range(8):
                kb.mm(p[0:64, :], hn[:, kc, c * 64:(c + 1) * 64], wv[:, kc, :], start=(kc == 0), stop=(kc == 7))
            self.evac(vaug[:, c, :, 0:128], p[0:64, :].rearrange("p (h d) -> p h d", h=4))
        hm = ar.alloc((4, T))
        Cs = [ar.alloc((4, 129)) for _ in range(2)]
        Csb = [ar.alloc((4, 128), BF16) for _ in range(2)]
        nrep = [ar.alloc((4, 128), BF16) for _ in range(2)]
        GT = [ar.alloc((4, 64), BF16, parts=64) for _ in range(2)]
        dab = [ar.alloc((4, 64)) for _ in range(2)]
        for c in range(NC):
            b = c % 2
            cs = slice(c * 64, (c + 1) * 64)
            pS = self.psum(); pN = self.psum(); pD = self.psum(); pC0 = self.psum(); pC1 = self.psum()
            for h in range(4):
                kb.ts(Cs[b][:, h, :], self.Cst[:, h, :], wi[:, h, c:c + 1], ALU.mult)
                kb.copy(Csb[b][:, h, :], Cs[b][:, h, 0:128], eng='act')
                kb.copy(nrep[b][:, h, :], bc(Cs[b][:, h, 128:129], [128, 128]), eng='pool')
                kb.mm(pS[0:64, h * 64:(h + 1) * 64], kmT[:, h, cs], qmT[:, h, cs])
            kb.tt(GT[b], pS[0:64, 0:256].rearrange("p (h t) -> p h t", h=4), bc(self.maskml.unsqueeze(1), [64, 4, 64]), ALU.mult)
            for h in range(4):
                kb.mm(pN[:, h * 64:(h + 1) * 64], Csb[b][:, h, :], qmT[:, h, cs], start=True, stop=False)
                kb.mm(pN[:, h * 64:(h + 1) * 64], vaug[:, c, h, 0:128], GT[b][:, h, :], start=False, stop=True)
            for h in range(4):
                kb.mm(pD[:, h * 64:(h + 1) * 64], nrep[b][:, h, :], qmT[:, h, cs], start=True, stop=False)
                kb.mm(pD[:, h * 64:(h + 1) * 64], self.ones[0:64, :], GT[b][:, h, :], start=False, stop=True)
            for h in range(4):
                pc = pC0 if h < 2 else pC1
                kb.mm(pc[:, (h % 2) * 129:(h % 2) * 129 + 129], ktok[:, c, h, :], vaug[:, c, h, :])
            d = dab[b]
            kb.act(d, pD[:, 0:256].rearrange("p (h t) -> p h t", h=4), AF.Abs)
            kb.tt(d, d, gb[:, :, cs], ALU.max)
            kb.recip(d, d)
            kb.tt(hm[:, :, cs], pN[:, 0:256].rearrange("p (h t) -> p h t", h=4), d, ALU.mult)
            kb.tt(self.Cst[:, 0:2, :], pC0[:, 0:258].rearrange("p (h d) -> p h d", h=2), Cs[b][:, 0:2, :], ALU.add)
            kb.tt(self.Cst[:, 2:4, :], pC1[:, 0:258].rearrange("p (h d) -> p h d", h=2), Cs[b][:, 2:4, :], ALU.add)
        sq = kmT
        kb.act(sq, hm, AF.Square)
        rs = gl
        for h in range(4):
            p = self.psum()
            kb.mm(p[:, 0:T], self.ones, sq[:, h, :])
            kb.act(rs[:, h, :], p[:, 0:T], AF.Sqrt, scale=1.0 / 128, bias=self.epsc[:, 0:1])
        kb.recip(rs, rs)
        for h in range(4):
            kb.stt(hm[:, h, :], hm[:, h, :], self.gains[:, 56 + h:57 + h], rs[:, h, :], ALU.mult, ALU.mult)
        kb.tt(mixed[:, 0:4, :], hm, osig, ALU.mult)
        ar.pop()
        ar.push()
        cqT = ar.alloc((3, T)); ckvT = ar.alloc((2, T)); qr5 = ar.alloc((5, T), parts=64)
        for c in range(3):
            p = self.inF(self.winF_e, 12 + c, "we", hn, T); self.evac(cqT[:, c, :], p)
        for c in range(2):
            p = self.inF(self.winF_e, 15 + c, "we", hn, T); self.evac(ckvT[:, c, :], p)
        p = self.inF(self.winF_e, 17, "we", hn, T, ncols=64); self.evac(qr5[:, 4, :], p)
        cqn = ar.alloc((3, T), BF16)
        self.rmsnorm(cqT, 3, T, 60, cqn, 384)
        qnT = ar.alloc((4, T), BF16)
        for h in range(4):
            p = self.psum()
            for kc in range(3):
                kb.mm(p[:, 0:T], self.wuq[:, kc, h * 192:h * 192 + 128], cqn[:, kc, :], start=(kc == 0), stop=(kc == 2))
            kb.copy(qnT[:, h, :], p[:, 0:T], eng='act')
            p2 = self.psum()
            for kc in range(3):
                kb.mm(p2[0:64, 0:T], self.wuq[:, kc, h * 192 + 128:h * 192 + 192], cqn[:, kc, :], start=(kc == 0), stop=(kc == 2))
            kb.copy(qr5[:, h, :], p2[0:64, 0:T], eng='dve')
        cs2 = ar.alloc(T, parts=64); sn2 = ar.alloc(T, parts=64)
        kb.dma(cs2, G.rope[0][:, ti * T:(ti + 1) * T]); kb.dma(sn2, G.rope[1][:, ti * T:(ti + 1) * T])
        qrf = ar.alloc((5, T), parts=64); qrb = ar.alloc((5, T), BF16, parts=64)
        ar.push()
        t1 = ar.alloc((5, T), parts=64); t2 = ar.alloc((5, T), parts=64)
        kb.tt(t1, qr5, bc(cs2.unsqueeze(1), [64, 5, T]), ALU.mult)
        kb.tt(t2[0:32], qr5[32:64], bc(sn2[32:64].unsqueeze(1), [32, 5, T]), ALU.mult)
        kb.tt(t2[32:64], qr5[0:32], bc(sn2[0:32].unsqueeze(1), [32, 5, T]), ALU.mult, eng='pool')
        kb.tt(qrf[0:32], t1[0:32], t2[0:32], ALU.subtract)
        kb.tt(qrf[32:64], t1[32:64], t2[32:64], ALU.add, eng='pool')
        ar.pop()
        kb.copy(qrb, qrf, eng='act')
        latf = ar.alloc((2, T)); latb = ar.alloc((2, T), BF16)
        self.rmsnorm(ckvT, 2, T, 63, latf, 256)
        kb.copy(latb, latf, eng='act')
        r0 = ti * T
        self.transpose_out([(latf[:, 0, :], 128), (latf[:, 1, :], 128)], 256, T, G.lat[r0:r0 + T, :])
        self.transpose_out([(qrf[:, 4, :], 64)], 64, T, G.kr[r0:r0 + T, :])
        self.kv_expand(G, kt, latb, T)
        kb.dma(G.KrT[kt][:, 0:T], qrb[:, 4, :], eng='pool', wkeys=[("kv" + G.name, kt)])
        self.done.add(("kv" + G.name, kt))
        self.mla_attend(G, ti, T, qnT, qrb, mixed)
        ar.pop()
        if G.name == "p" and ti == 0 and self.debug:
            self.dbg("dbg_mixed", mixed, (8, T))
        self.out_proj(xT, T, mixed, self.wout_e, "we")
        ar.pop()

    def kv_expand(self, G, kt, latb, T):
        kb, ar = self.kb, self.ar
        ar.push()
        kn = ar.alloc((4, T), BF16)
        for h in range(4):
            p = self.psum()
            for kc in range(2):
                kb.mm(p[:, 0:T], self.wukv[:, kc, h * 256:h * 256 + 128], latb[:, kc, :], start=(kc == 0), stop=(kc == 1))
            self.evac(kn[:, h, :], p[:, 0:T])
        kb.dma(G.KnT[kt].rearrange("p (h t) -> p h t", h=4)[:, :, 0:T], kn, eng='pool', wkeys=[("kv" + G.name, kt)])
        if self.debug and G.name == "p" and kt == 0:
            self.dbg("dbg_kn", kn, (4, T))
        tb = min(128, T); nb = T // tb
        vt = ar.alloc((nb, 512), BF16, parts=tb)
        wv = self.wukv.rearrange("p k (h x) -> p k h x", h=4)[:, :, :, 128:256]
        for b in range(nb):
            p = self.psum()
            for kc in range(2):
                kb.mm(p[0:tb, :], latb[:, kc, b * tb:(b + 1) * tb], wv[:, kc, :, :], start=(kc == 0), stop=(kc == 1))
            self.evac(vt[:, b, :], p[0:tb, :])
        kb.dma(G.V[kt].rearrange("p (b x) -> p b x", b=4)[0:tb, 0:nb, :], vt, eng='pool', wkeys=[("kv" + G.name, kt)])
        ar.pop()

    def mla_attend(self, G, ti, T, qnT, qrb, mixed):
        kb, ar = self.kb, self.ar
        scale = 192 ** -0.5
        ktc = G.npast + ti
        ar.push()
        Pb = [ar.alloc(T, BF16) for _ in range(3)]
        rden = ar.alloc(T)
        pi = 0
        for h in range(4):
            res = self.reserve(2)
            pO = self.ps[res[0]]; pDn = self.ps[res[1]]
            first = True
            for kt in range(0, ktc + 1):
                diag = (kt == ktc)
                nk = T if diag else 512
                key = [("kv" + G.name, kt)]
                self.ring.unhold()
                knT = self.ring.fetch(G.KnT[kt][:, h * 512:h * 512 + nk], rkeys=key, hold=True)
                krT = self.ring.fetch(G.KrT[kt][:, 0:nk], rkeys=key, hold=True)
                kbs = min(128, nk); nkb = nk // kbs
                vv = self.ring.fetch(G.V[kt].rearrange("p (b x) -> p b x", b=4)[0:kbs, 0:nkb, h * 128:(h + 1) * 128], rkeys=key)
                vv = vv.rearrange("p (b x) -> p b x", b=nkb)
                if self.debug and G.name == "p" and ti == 0 and h < 2:
                    self.dbg("dbg_knT%d" % h, knT, (512,))
                    if h == 0:
                        self.dbg("dbg_qnT", qnT, (4, T))
                for b in range(nkb):
                    ko = b * kbs
                    qs = ko if diag else 0
                    last = diag and (b == nkb - 1)
                    pS = self.psum()
                    kb.mm(pS[0:kbs, qs:T], knT[:, ko:ko + kbs], qnT[:, h, qs:T], start=True, stop=False)
                    kb.mm(pS[0:kbs, qs:T], krT[0:64, ko:ko + kbs], qrb[:, h, qs:T], start=False, stop=True)
                    P = Pb[pi % 3]; pi += 1
                    kb.act(P[0:kbs, qs:T], pS[0:kbs, qs:T], AF.Exp, scale=scale)
                    if diag and kbs == 128:
                        kb.ts(P[64:128, qs:qs + 64], P[64:128, qs:qs + 64], 0.0, ALU.mult)
                    kb.mm(pO[:, qs:T], vv[:, b, :], P[0:kbs, qs:T], start=first, stop=last)
                    kb.mm(pDn[:, qs:T], self.ones[0:kbs, :], P[0:kbs, qs:T], start=first, stop=last)
                    first = False
            kb.recip(rden, pDn[:, 0:T])
            kb.tt(mixed[:, 4 + h, :], pO[:, 0:T], rden, ALU.mult)
            self.release(res)
            self.ring.unhold()
        ar.pop()

    def odd_mixer(self, G, ti, xT):
        kb, ar = self.kb, self.ar
        T = G.T
        kt = G.npast + ti
        ar.push()
        hn = ar.alloc((8, T), BF16)
        self.rmsnorm(xT, 8, T, 32 + 8, hn, D)
        mixed = ar.alloc((8, T), BF16)
        ar.push()
        Ts = min(128, T); nsc = T // Ts
        Ec = ar.alloc((16, 128)); Es = ar.alloc((16, 128)); BT = ar.alloc((32, 128), BF16); Cm = ar.alloc((32, 128), BF16)
        kb.dma(Ec, self.s5tab[:, 0:2048].rearrange("p (s t) -> p s t", s=16), rkeys=[("s5",)])
        kb.dma(Es, self.s5tab[:, 4096:4096 + 2048].rearrange("p (s t) -> p s t", s=16), rkeys=[("s5",)])
        kb.dma(BT, self.s5mat[:, 0:4096].rearrange("p (s t) -> p s t", s=32), rkeys=[("s5",)])
        kb.dma(Cm, self.s5mat[:, 4096:8192].rearrange("p (s t) -> p s t", s=32), rkeys=[("s5",)])
        uf = ar.alloc((4, T)); ub = ar.alloc((4, T), BF16)
        for j in range(4):
            p = self.inF(self.winF_o, j, "wo", hn, T)
            kb.copy(uf[:, j, :], p, eng='act'); kb.copy(ub[:, j, :], p, eng='dve')
        yT = ar.alloc((4, T))
        tA = [ar.alloc(T)] * 2; tB = [ar.alloc(T)] * 2
        cR = [ar.alloc(T)] * 2; cI = [ar.alloc(T)] * 2
        wR = [ar.alloc(T)] * 2; wI = [ar.alloc(T)] * 2
        xR = [ar.alloc(T, BF16) for _ in range(4)]; xI = [ar.alloc(T, BF16) for _ in range(4)]
        tmp1 = ar.alloc(4)
        v3 = lambda a: a.rearrange("p (c t) -> p c t", t=Ts)
        for j in range(4):
            for sl in range(4):
                s = 4 * j + sl
                b = s % 2
                pr = self.psum(); pim = self.psum()
                kb.mm(pr[:, 0:T], BT[:, 2 * s, :], ub[:, j, :])
                kb.mm(pim[:, 0:T], BT[:, 2 * s + 1, :], ub[:, j, :])
                ecb = bc(Ec[:, s, 0:Ts].unsqueeze(1), [128, nsc, Ts]); esb = bc(Es[:, s, 0:Ts].unsqueeze(1), [128, nsc, Ts])
                kb.tt(v3(tA[b]), v3(pr[:, 0:T]), ecb, ALU.mult)
                kb.tt(v3(tB[b]), v3(pim[:, 0:T]), esb, ALU.mult)
                kb.tt(cR[b], tA[b], tB[b], ALU.add, eng='pool')
                kb.tt(v3(tA[b]), v3(pim[:, 0:T]), ecb, ALU.mult)
                kb.tt(v3(tB[b]), v3(pr[:, 0:T]), esb, ALU.mult)
                kb.tt(cI[b], tA[b], tB[b], ALU.subtract, eng='pool')
                rb = bc(self.rdec[:, s:s + 1], [128, Ts])
                for sc in range(nsc):
                    cs = slice(sc * Ts, (sc + 1) * Ts)
                    kb.scan(wR[b][:, cs], rb, cR[b][:, cs], self.wst[:, s, 0:1], ALU.mult, ALU.add)
                    kb.scan(wI[b][:, cs], rb, cI[b][:, cs], self.wst[:, s, 1:2], ALU.mult, ALU.add)
                    e = sc * Ts + Ts - 1
                    ecl = Ec[:, s, Ts - 1:Ts]; esl = Es[:, s, Ts - 1:Ts]
                    kb.ts(tmp1[:, 0:1], wI[b][:, e:e + 1], esl, ALU.mult)
                    kb.ts(tmp1[:, 1:2], wR[b][:, e:e + 1], esl, ALU.mult)
                    kb.stt(self.wst[:, s, 0:1], wR[b][:, e:e + 1], ecl, tmp1[:, 0:1], ALU.mult, ALU.subtract)
                    kb.stt(self.wst[:, s, 1:2], wI[b][:, e:e + 1], ecl, tmp1[:, 1:2], ALU.mult, ALU.add)
                xi = (s % 4)
                kb.tt(v3(tA[b]), v3(wR[b]), ecb, ALU.mult)
                kb.tt(v3(tB[b]), v3(wI[b]), esb, ALU.mult, eng='pool')
                kb.tt(xR[xi], tA[b], tB[b], ALU.subtract)
                kb.tt(v3(cR[b]), v3(wI[b]), ecb, ALU.mult, eng='pool')
                kb.tt(v3(cI[b]), v3(wR[b]), esb, ALU.mult)
                kb.tt(xI[xi], cR[b], cI[b], ALU.add, eng='pool')
            py = self.psum()
            for sl in range(4):
                s = 4 * j + sl
                kb.mm(py[:, 0:T], Cm[:, 2 * s, :], xR[s % 4], start=(sl == 0), stop=False)
                kb.mm(py[:, 0:T], Cm[:, 2 * s + 1, :], xI[s % 4], start=False, stop=(sl == 3))
            kb.stt(yT[:, j, :], uf[:, j, :], self.gains[:, 65 + j:66 + j], py[:, 0:T], ALU.mult, ALU.add)
        gq = ar.alloc((4, T)); gg = ar.alloc((4, T)); gbf = ar.alloc((4, T), BF16)
        kb.act(gq, yT, AF.Square)
        kb.ts(gq, gq, 0.044715, ALU.mult, 1.0, ALU.add)
        kb.tt(gq, gq, yT, ALU.mult)
        kb.act(gq, gq, AF.Sigmoid, scale=2.0 * 0.7978845608028654)
        kb.tt(gg, gq, yT, ALU.mult)
        kb.copy(gbf, gg, eng='act')
        for m in range(4):
            p = self.psum()
            for kc in range(4):
                kb.mm(p[:, 0:T], self.wglu[:, kc, m * 128:(m + 1) * 128], gbf[:, kc, :], start=(kc == 0), stop=(kc == 3))
            kb.act(gq[:, m, :], p[:, 0:T], AF.Sigmoid)
        kb.tt(mixed[:, 0:4, :], gg, gq, ALU.mult)
        ar.pop()
        ar.push()
        qT = ar.alloc((4, T), BF16); kT = ar.alloc((4, T), BF16)
        for j in range(4):
            p = self.inF(self.winF_o, 4 + j, "wo", hn, T)
            kb.act(qT[:, j, :], p, AF.Copy, scale=0.125)
        for j in range(4):
            p = self.inF(self.winF_o, 8 + j, "wo", hn, T)
            self.evac(kT[:, j, :], p)
        kb.dma(G.sKT[kt].rearrange("p (j t) -> p j t", j=4)[:, :, 0:T], kT, eng='pool', wkeys=[("sb" + G.name, kt)])
        tb = min(128, T); nb = T // tb
        r0 = ti * T
        for which, (dst, oo) in enumerate(((None, G.ok), (G.sV, G.ov))):
            w = self.ring.fetch(self.winT_o[which], rkeys=[("wo",)]).rearrange("p (k n) -> p k n", k=8)
            ar.push()
            stf = ar.alloc((nb, 512), parts=tb)
            stb = ar.alloc((nb, 512), BF16, parts=tb)
            for b in range(nb):
                p = self.psum()
                for kc in range(8):
                    kb.mm(p[0:tb, :], hn[:, kc, b * tb:(b + 1) * tb], w[:, kc, :], start=(kc == 0), stop=(kc == 7))
                kb.copy(stf[:, b, :], p[0:tb, :], eng='act')
                if dst is not None:
                    kb.copy(stb[:, b, :], p[0:tb, :], eng='dve')
            kb.dma(oo[r0:r0 + T, :].rearrange("(b p) f -> p b f", p=tb), stf, eng='pool')
            if dst is not None:
                kb.dma(dst[kt].rearrange("p (b x) -> p b x", b=4)[0:tb, 0:nb, :], stb, eng='pool', wkeys=[("sb" + G.name, kt)])
            ar.pop()
        self.done.add(("sb" + G.name, kt))
        self.sb_attend(G, ti, T, qT, mixed)
        ar.pop()
        self.out_proj(xT, T, mixed, self.wout_o, "wo")
        ar.pop()

    def sb_attend(self, G, ti, T, qT, mixed):
        kb, ar = self.kb, self.ar
        ktc = G.npast + ti
        ar.push()
        Eb = [ar.alloc(T) for _ in range(2)]
        Lb = [ar.alloc(T, BF16) for _ in range(3)]
        Wb = [ar.alloc(T, BF16) for _ in range(3)]
        Racc = [ar.alloc(T) for _ in range(2)]
        Rhi = [ar.alloc(T, BF16) for _ in range(4)]; Rlo = [ar.alloc(T, BF16) for _ in range(4)]
        bi = 0
        for j in range(4):
            res = self.reserve(2)
            pO = [self.ps[res[0]], self.ps[res[1]]]
            for e in range(2):
                kb.memset(Racc[e], 0.0, eng='pool')
            first = [True, True]
            for kt in range(ktc, -1, -1):
                diag = (kt == ktc)
                nk = T if diag else 512
                key = [("sb" + G.name, kt)]
                self.ring.unhold()
                kT = self.ring.fetch(G.sKT[kt][:, j * 512:j * 512 + nk], rkeys=key, hold=True)
                kbs = min(128, nk); nkb = nk // kbs
                vv = self.ring.fetch(G.sV[kt].rearrange("p (b x) -> p b x", b=4)[0:kbs, 0:nkb, j * 128:(j + 1) * 128], rkeys=key)
                vv = vv.rearrange("p (b x) -> p b x", b=nkb)
                for b in range(nkb - 1, -1, -1):
                    ko = b * kbs
                    qs = ko if diag else 0
                    for e in range(2):
                        pb = 64 * e
                        last = (kt == 0 and b == 0)
                        pZ = self.psum()
                        kb.mm(pZ[0:kbs, qs:T], kT[pb:pb + 64, ko:ko + kbs], qT[pb:pb + 64, j, qs:T], start=True, stop=False)
                        E = Eb[bi % 2]; L = Lb[bi % 3]; W = Wb[bi % 3]; rh = Rhi[bi % 4]; rl = Rlo[bi % 4]
                        bi += 1
                        kb.act(E[0:kbs, qs:T], pZ[0:kbs, qs:T], AF.Exp)
                        kb.act(L[0:kbs, qs:T], E[0:kbs, qs:T], AF.Ln, bias=self.onec[0:kbs, 0:1])
                        if diag:
                            kb.tt(L[0:kbs, qs:qs + kbs], L[0:kbs, qs:qs + kbs], self.masksb[0:kbs, 0:kbs], ALU.mult, eng='pool')
                        kb.mm(pZ[0:kbs, qs:T], self.uincl[0:kbs, 0:kbs], L[0:kbs, qs:T], start=False, stop=first[e])
                        if not first[e]:
                            kb.copy(rh[:, qs:T], Racc[e][:, qs:T], eng='pool')
                            kb.tt(rl[:, qs:T], Racc[e][:, qs:T], rh[:, qs:T], ALU.subtract, eng='pool')
                            kb.mm(pZ[0:kbs, qs:T], self.negones[:, 0:kbs], rh[:, qs:T], start=False, stop=False)
                            kb.mm(pZ[0:kbs, qs:T], self.negones[:, 0:kbs], rl[:, qs:T], start=False, stop=True)
                        kb.act(W[0:kbs, qs:T], pZ[0:kbs, qs:T], AF.Exp)
                        if diag:
                            kb.tt(W[0:kbs, qs:qs + kbs], W[0:kbs, qs:qs + kbs], self.masksb[0:kbs, 0:kbs], ALU.mult, eng='pool')
                        kb.mm(pO[e][0:64, qs:T], vv[:, b, pb:pb + 64], W[0:kbs, qs:T], start=first[e], stop=last)
                        if not last:
                            kb.tt(Racc[e][0:kbs, qs:T], Racc[e][0:kbs, qs:T], L[0:kbs, qs:T], ALU.add, eng='pool')
                        first[e] = False
            for e in range(2):
                kb.copy(mixed[64 * e:64 * e + 64, 4 + j, :], pO[e][0:64, 0:T], eng='act')
            self.release(res)
            self.ring.unhold()
        ar.pop()


RING_BYTES = 32768


def _sin_of(self, out, ang, shift, shape):
    kb, ar = self.kb, self.ar
    ar.push()
    t = ar.alloc(shape); k = ar.alloc(shape)
    kb.ts(t, ang, float(shift), ALU.add)
    kb.ts(k, t, 1.0 / TWO_PI, ALU.mult, MAGIC, ALU.add)
    kb.ts(k, k, -MAGIC, ALU.add)
    kb.stt(t, k, -TWO_PI, t, ALU.mult, ALU.add)
    kb.ts(t, t, float(np.pi), ALU.min, -float(np.pi), ALU.max)
    kb.act(out, t, AF.Sin)
    ar.pop()


def prologue_s5(self):
    kb, ar = self.kb, self.ar
    A = ar.alloc
    self.rdec = A(16)
    ar.push()
    rows = A(128); kb.memset(rows, 0.0); kb.dma(rows[0:48, :], self.s5rows)
    p = self.psum(); kb.tr(p[:, 0:128], rows, self.ident)
    prm = A(48); kb.copy(prm, p[:, 0:48])
    a_r = prm[:, 0:16]; a_i = prm[:, 16:32]; ldt = prm[:, 32:48]
    dt = A(16); dar = A(16); dai = A(16)
    kb.act(dt, ldt, AF.Exp)
    kb.tt(dar, dt, a_r, ALU.mult); kb.tt(dai, dt, a_i, ALU.mult)
    kb.act(self.rdec, dar, AF.Exp)
    cs = A(16); sn = A(16)
    _sin_of(self, sn, dai, 0.0, 16); _sin_of(self, cs, dai, np.pi / 2, 16)
    abr = A(16); abi = A(16); den = A(16); t1 = A(16); t2 = A(16); cr = A(16); ci = A(16)
    kb.tt(abr, self.rdec, cs, ALU.mult); kb.tt(abi, self.rdec, sn, ALU.mult)
    kb.tt(t1, a_r, a_r, ALU.mult); kb.tt(t2, a_i, a_i, ALU.mult); kb.tt(den, t1, t2, ALU.add); kb.recip(den, den)
    kb.ts(abr, abr, -1.0, ALU.add)
    kb.tt(t1, abr, a_r, ALU.mult); kb.tt(t2, abi, a_i, ALU.mult); kb.tt(cr, t1, t2, ALU.add); kb.tt(cr, cr, den, ALU.mult)
    kb.tt(t1, abi, a_r, ALU.mult); kb.tt(t2, abr, a_i, ALU.mult); kb.tt(ci, t1, t2, ALU.subtract); kb.tt(ci, ci, den, ALU.mult)
    Bre = A((16, 16)); Bim = A((16, 16))
    kb.dma(Bre, self.s5B[0].rearrange("(s q) c -> q s c", q=128)); kb.dma(Bim, self.s5B[1].rearrange("(s q) c -> q s c", q=128))
    crb = bc(cr.unsqueeze(2), [128, 16, 16]); cib = bc(ci.unsqueeze(2), [128, 16, 16])
    u1 = A((16, 16)); u2 = A((16, 16)); bbr = A((16, 16)); bbi = A((16, 16))
    kb.tt(u1, Bre, crb, ALU.mult); kb.tt(u2, Bim, cib, ALU.mult); kb.tt(bbr, u1, u2, ALU.subtract)
    kb.tt(u1, Bim, crb, ALU.mult); kb.tt(u2, Bre, cib, ALU.mult); kb.tt(bbi, u1, u2, ALU.add)
    bb2 = A((16, 2, 32)); kb.memset(bb2, 0.0)
    for ri, bb in enumerate((bbr, bbi)):
        kb.copy(bb2[0:64, :, ri, 0:16], bb[0:64]); kb.copy(bb2[64:128, :, ri, 16:32], bb[64:128])
    BT = A((32, 128), BF16); kb.memset(BT, 0.0)
    for s in range(16):
        p = self.psum()
        for ri in range(2):
            kb.tr(p[0:32, ri * 128:(ri + 1) * 128], bb2[:, s, ri, :], self.ident)
        a = s % 4
        kb.copy(BT[32 * a:32 * a + 32, 2 * s:2 * s + 2, :], p[0:32, 0:256].rearrange("p (r c) -> p r c", r=2))
    kb.dma(self.s5mat[:, 0:4096].rearrange("p (s t) -> p s t", s=32), BT, eng='pool', wkeys=[("s5",)])
    Cm = A((32, 128), BF16); kb.memset(Cm, 0.0)
    for ri in range(2):
        Cin = A((4, 64)); kb.dma(Cin, self.s5C[ri].rearrange("(j q) p -> q j p", q=128))
        CT = A((4, 128), parts=64)
        for j in range(4):
            p = self.psum()
            kb.tr(p[0:64, 0:128], Cin[:, j, :], self.ident)
            kb.act(CT[:, j, :], p[0:64, 0:128], AF.Copy, scale=(1.0 if ri == 0 else -1.0))
        for e in range(2):
            for sl in range(4):
                c0 = (2 * sl + e) * 16
                kb.copy(Cm[64 * e:64 * e + 64, ri + 2 * sl:32:8, c0:c0 + 16], CT[:, :, c0:c0 + 16])
    kb.dma(self.s5mat[:, 4096:8192].rearrange("p (s t) -> p s t", s=32), Cm, eng='pool', wkeys=[("s5",)])
    io = A(128); kb.dma(io, self.c_iota)
    ang = A((16, 128)); tb = A((16, 128))
    for s in range(16):
        kb.ts(ang[:, s, :], io, dai[:, s:s + 1], ALU.mult)
    _sin_of(self, tb, ang, np.pi / 2, (16, 128))
    kb.dma(self.s5tab[:, 0:2048].rearrange("p (s t) -> p s t", s=16), tb, eng='pool', wkeys=[("s5",)])
    tb2 = A((16, 128))
    _sin_of(self, tb2, ang, 0.0, (16, 128))
    kb.dma(self.s5tab[:, 4096:4096 + 2048].rearrange("p (s t) -> p s t", s=16), tb2, eng='pool', wkeys=[("s5",)])
    ar.pop()


def sample_prep(self, G):
    kb, ar = self.kb, self.ar
    A = ar.alloc
    for kt in range(G.npast):
        ar.push()
        rs = slice(kt * 512, (kt + 1) * 512)
        stg = A((4, 256)); kb.dma(stg, self.c_lat[rs, :].rearrange("(b p) f -> p b f", p=128))
        latb = A((2, 512), BF16)
        for c in range(2):
            p = self.psum()
            for b in range(4):
                kb.tr(p[:, b * 128:(b + 1) * 128], stg[:, b, c * 128:(c + 1) * 128], self.ident)
            self.evac(latb[:, c, :], p[:, 0:512])
        self.kv_expand(G, kt, latb, 512)
        stg2 = A((4, 64)); kb.dma(stg2, self.c_kr[rs, :].rearrange("(b p) f -> p b f", p=128))
        p = self.psum()
        for b in range(4):
            kb.tr(p[0:64, b * 128:(b + 1) * 128], stg2[:, b, :], self.ident)
        krb = A(512, BF16, parts=64); self.evac(krb, p[0:64, 0:512])
        kb.dma(G.KrT[kt], krb, eng='pool', wkeys=[("kv" + G.name, kt)])
        stk = A((4, 512)); kb.dma(stk, self.c_sbk[rs, :].rearrange("(b p) f -> p b f", p=128))
        kT = A((4, 512), BF16)
        for j in range(4):
            p = self.psum()
            for b in range(4):
                kb.tr(p[:, b * 128:(b + 1) * 128], stk[:, b, j * 128:(j + 1) * 128], self.ident)
            self.evac(kT[:, j, :], p[:, 0:512])
        kb.dma(G.sKT[kt].rearrange("p (j t) -> p j t", j=4), kT, eng='pool', wkeys=[("sb" + G.name, kt)])
        stv = A((4, 512)); kb.dma(stv, self.c_sbv[rs, :].rearrange("(b p) f -> p b f", p=128))
        vb = A((4, 512), BF16); kb.copy(vb, stv, eng='pool')
        kb.dma(G.sV[kt].rearrange("p (b x) -> p b x", b=4), vb, eng='pool', wkeys=[("sb" + G.name, kt)])
        self.done.add(("kv" + G.name, kt)); self.done.add(("sb" + G.name, kt))
        ar.pop()
    ar.push()
    stC = A((4, 128)); kb.dma(stC, self.st_C.rearrange("h v k -> v h k"))
    for h in range(4):
        p = self.psum(); kb.tr(p[:, 0:128], stC[:, h, :], self.ident)
        self.evac(self.Cst[:, h, 0:128], p[:, 0:128])
    rows = A(128); kb.memset(rows, 0.0); kb.dma(rows[0:4, :], self.st_n)
    p = self.psum(); kb.tr(p[:, 0:128], rows, self.ident)
    kb.copy(self.Cst[:, :, 128], p[:, 0:4])
    kb.dma(self.mst, bc(self.st_m[0:1, :], [128, 4]))
    rows2 = A(128); kb.memset(rows2, 0.0); kb.dma(rows2[0:16, :], self.st_re); kb.dma(rows2[16:32, :], self.st_im)
    p = self.psum(); kb.tr(p[:, 0:128], rows2, self.ident)
    kb.copy(self.wst[:, :, 0], p[:, 0:16]); kb.copy(self.wst[:, :, 1], p[:, 16:32])
    ar.pop()


def write_states(self, G):
    kb, ar = self.kb, self.ar
    A = ar.alloc
    ar.push()
    stg = A((4, 128))
    for h in range(4):
        p = self.psum(); kb.tr(p[:, 0:128], self.Cst[:, h, 0:128], self.ident)
        self.evac(stg[:, h, :], p[:, 0:128])
    kb.dma(G.oC.rearrange("h v k -> v h k"), stg, eng='pool')
    tmp = A(4); kb.copy(tmp, self.Cst[:, :, 128])
    p = self.psum(); kb.tr(p[0:4, 0:128], tmp, self.ident)
    st2 = A(128, parts=4); kb.copy(st2, p[0:4, 0:128]); kb.dma(G.on, st2, eng='pool')
    kb.dma(G.om, self.mst[0:1, :], eng='pool')
    tmp2 = A(32); kb.copy(tmp2[:, 0:16], self.wst[:, :, 0]); kb.copy(tmp2[:, 16:32], self.wst[:, :, 1])
    p = self.psum(); kb.tr(p[0:32, 0:128], tmp2, self.ident)
    st3 = A(128, parts=32); kb.copy(st3, p[0:32, 0:128])
    kb.dma(G.ore, st3[0:16, :], eng='pool'); kb.dma(G.oim, st3[16:32, :], eng='pool')
    ar.pop()


def final_out(self, G, ti, xT):
    kb, ar = self.kb, self.ar
    T = G.T
    ar.push()
    yT = ar.alloc((8, T))
    self.rmsnorm(xT, 8, T, 48, yT, D)
    r0 = ti * T
    self.transpose_out([(yT[:, c, :], 128) for c in range(8)], D, T, G.y[r0:r0 + T, :])
    ar.pop()


def build(self, stages=("ffn0", "even", "ffn1", "ffn2", "odd", "ffn3")):
    kb = self.kb
    import os as _os
    self.serial_stages = tuple(_os.environ.get("KSERIAL", "odd").split(","))
    self.declare()
    plan = None
    for pass_ in (0, 1):
        kb.dry = (pass_ == 0)
        self.plan = plan
        self.ar = ar = Arena(self.arena)
        self.ps_i = 0; self.ps_res = set(); self._evac_i = 0
        self.done = set()
        self.load_consts()
        self.epsc = ar.alloc(1); kb.memset(self.epsc, EPS)
        self.onec = ar.alloc(1); kb.memset(self.onec, 1.0)
        self.ring = Ring(self, RING_BYTES)
        xTfull = ar.alloc((8, 512))
        self.prologue_weights()
        prologue_s5(self)
        self.done.update([("wgu", f) for f in range(4)] + [("wdn", f) for f in range(4)] + [("we",), ("wo",), ("s5",)])
        for G in self.groups:
            if G.name == "p":
                kb.memset(self.Cst, 0.0); kb.memset(self.mst, 0.0); kb.memset(self.wst, 0.0)
            else:
                sample_prep(self, G)
            for ti in range(G.ntile):
                xT = xTfull[:, :, 0:G.T]
                self.load_xT(G, ti, xT)
                SER = self.serial_stages
                kb.cserial = "ffn" in SER
                if "ffn0" in stages: self.ffn(xT, G.T, 0)
                kb.cserial = "even" in SER
                if "even" in stages: self.even_mixer(G, ti, xT)
                kb.cserial = "ffn" in SER
                if "ffn1" in stages: self.ffn(xT, G.T, 1)
                if "ffn2" in stages: self.ffn(xT, G.T, 2)
                kb.cserial = "odd" in SER
                if "odd" in stages: self.odd_mixer(G, ti, xT)
                kb.cserial = "ffn" in SER
                if "ffn3" in stages: self.ffn(xT, G.T, 3)
                kb.cserial = True
                final_out(self, G, ti, xT)
            write_states(self, G)
        if pass_ == 0:
            plan = self.ring.newplan
    print("arena peak", self.ar.peak, "ops", len(kb.ops), "plan", len(plan))
    kb.analyze()
    print(kb.stats())
    kb.emit(self.st)
    self.st.close()
    return self.nc


def rope_tables(pos):
    half = 32
    inv = (np.float32(10000.0) ** (-np.arange(half, dtype=np.float32) / np.float32(half))).astype(np.float32)
    ang = pos.astype(np.float32)[None, :] * inv[:, None]
    c = np.cos(ang).astype(np.float32); s = np.sin(ang).astype(np.float32)
    return np.concatenate([c, c], 0), np.concatenate([s, s], 0)


def host_consts(S, PAST):
    k = np.arange(128)
    c = {}
    c["c_ident"] = np.eye(128, dtype=np.float32)
    c["c_masksb"] = (k[:, None] < k[None, :]).astype(np.float32)
    c["c_maskml"] = (k[:64, None] <= k[None, :64]).astype(np.float32)
    c["c_uincl"] = -(k[:, None] >= k[None, :]).astype(np.float32)
    sel = np.zeros((8, 8, 128), np.float32)
    for g in range(8):
        sel[g, g, :] = 1.0
    c["c_sel"] = sel.reshape(8, 1024)
    cm = np.ones((128, 512), np.float32); cm[:, ::64] = 0.0
    c["c_cm"] = cm
    c["c_iota"] = np.tile(np.arange(1, 129, dtype=np.float32)[None, :], (128, 1))
    c["c_cosp"], c["c_sinp"] = rope_tables(np.arange(S))
    c["c_coss"], c["c_sins"] = rope_tables(PAST + np.arange(64))
    return c


def make_inmaps(inputs, ncores, S, PAST):
    f = lambda a: np.ascontiguousarray(np.asarray(a), dtype=np.float32)
    I = {k: np.asarray(v) for k, v in inputs.items()}
    shared = dict(
        w_gate=f(I["ffn_w_gate"].reshape(4, D, DFF)), w_up=f(I["ffn_w_up"].reshape(4, D, DFF)),
        w_down=f(I["ffn_w_down"].reshape(4, DFF, D)),
        e_win=f(I["even_w_in"][0]), e_wout=f(I["even_w_out"][0]), wuq=f(I["mla_w_uq"][0]), wukv=f(I["mla_w_ukv"][0]),
        o_win=f(I["odd_w_in"][0]), o_wout=f(I["odd_w_out"][0]), wglu=f(I["s5_w_glu"][0]),
        grows=f(np.concatenate([I["norm_ffn"].reshape(32, 128), I["norm_mix"].reshape(16, 128), I["norm_final"].reshape(8, 128),
                                I["mlstm_out_norm"][0].reshape(4, 128), I["mla_q_norm"][0].reshape(3, 128),
                                I["mla_kv_norm"][0].reshape(2, 128), I["s5_D"][0].reshape(4, 128)], 0)),
        bi=f(I["mlstm_b_igate"][0][None]), bf=f(I["mlstm_b_fgate"][0][None]),
        s5rows=f(np.concatenate([I["s5_A_re"][0].reshape(16, 128), I["s5_A_im"][0].reshape(16, 128),
                                 np.repeat(I["s5_log_dt"][0][:, None], 64, axis=1).reshape(16, 128)], 0)),
        s5B_re=f(I["s5_B_re"][0].reshape(2048, 16)), s5B_im=f(I["s5_B_im"][0].reshape(2048, 16)),
        s5C_re=f(I["s5_C_re"][0].reshape(512, 64)), s5C_im=f(I["s5_C_im"][0].reshape(512, 64)),
    )
    shared.update(host_consts(S, PAST))
    maps = []
    for b in range(ncores):
        m = dict(shared)
        m.update(xp=f(I["x_prompt"][b]), xs=f(I["x_sample"][b]), c_lat=f(I["cache_mla_latent"][0, b]),
                 c_kr=f(I["cache_mla_krope"][0, b]), st_C=f(I["state_mlstm_C"][0, b]), st_n=f(I["state_mlstm_n"][0, b]),
                 st_m=f(I["state_mlstm_m"][0, b][None]), st_re=f(I["state_s5_re"][0, b].reshape(16, 128)),
                 st_im=f(I["state_s5_im"][0, b].reshape(16, 128)), c_sbk=f(I["cache_sb_k"][0, b].reshape(PAST, 512)),
                 c_sbv=f(I["cache_sb_v"][0, b].reshape(PAST, 512)))
        maps.append(m)
    return maps


def assemble(results, S):
    def st(name, shape):
        return np.stack([np.asarray(r[name], dtype=np.float32).reshape(shape) for r in results], 0)
    outs = [st("y_p", (S, D)), st("y_s", (64, D))]
    for nm, n in (("p", S), ("s", 64)):
        outs += [st("lat_" + nm, (n, 256))[None], st("kr_" + nm, (n, 64))[None], st("C_" + nm, (4, 128, 128))[None],
                 st("n_" + nm, (4, 128))[None], st("m_" + nm, (4,))[None], st("re_" + nm, (32, 64))[None],
                 st("im_" + nm, (32, 64))[None], st("k_" + nm, (n, 8, 64))[None], st("v_" + nm, (n, 8, 64))[None]]
    return tuple(outs)


_S, _PAST = 8192, 2048


def kernel(**inputs):
    ncores = 8
    b = Builder(_S, _PAST)
    nc = build(b)
    maps = make_inmaps(inputs, ncores, _S, _PAST)
    res = run_bass_kernel_spmd(nc, maps, core_ids=list(range(ncores)))
    return assemble(res.results, _S)
```

```python
import os
import numpy as np
import concourse.bass as bass
import concourse.mybir as mybir
from concourse.bass_utils import run_bass_kernel_spmd

F32 = mybir.dt.float32
BF16 = mybir.dt.bfloat16
I32 = mybir.dt.int32
ALU = mybir.AluOpType
AF = mybir.ActivationFunctionType
AX = mybir.AxisListType

CENG = ['pe', 'act', 'dve', 'pool']
ALLENG = CENG + ['sp']
EIDX = {e: i for i, e in enumerate(ALLENG)}
BLK = 32
DBLK = 2048
N_DSEM = 48
SEM_ROLL = 30000


def _esize(dt):
    return {F32: 4, BF16: 2, I32: 4}.get(dt, None) or mybir.dt.size(dt)


class Op:
    __slots__ = ('eng', 'fn', 'deps', 'id', 'pos', 'signal', 'semgen', 'semval', 'waits',
                 'is_dma', 'dk', 'sk', 'prev_dma')

    def __init__(self, eng, fn, is_dma):
        self.eng = eng
        self.fn = fn
        self.is_dma = is_dma
        self.deps = set()
        self.signal = False
        self.waits = []
        self.pos = -1
        self.sk = None
        self.prev_dma = None


class Space:
    def __init__(self, nblocks):
        self.last_w = np.full(nblocks, -1, dtype=np.int64)
        self.last_r = np.full((nblocks, len(CENG)), -1, dtype=np.int64)
        self.dma_readers = []


class KB:
    def __init__(self):
        self.nc = bass.Bass("TRN2", target_bir_lowering=False)
        self.ops = []
        self.spaces = {}
        self.untracked = set()
        self.n_dma = 0
        self.dma_ops = []
        self.dry = False
        self.keyw = {}
        self.last_compute = None
        self.cserial = True

    def _range(self, ap):
        sp = str(ap.space)
        t = ap.tensor
        name = ap.name
        es = _esize(ap.dtype)
        if 'DRAM' in sp:
            return None
            lo = ap.offset
            hi = ap.offset
            for st, cnt in ap.ap:
                if st >= 0:
                    hi += st * (cnt - 1)
                else:
                    lo += st * (cnt - 1)
            key = 'D:' + name
            if key not in self.spaces:
                n = 1
                for s in t.shape:
                    n *= s
                self.spaces[key] = Space((n * es + DBLK - 1) // DBLK)
            return key, (lo * es) // DBLK, (hi * es + es - 1) // DBLK + 1
        rowlen = ap.ap[0][0]
        if rowlen == 0:
            rowlen = t.shape[1] if len(t.shape) == 2 else int(np.prod(t.shape[1:]))
        off = ap.offset % rowlen
        lo = off
        hi = off
        for st, cnt in ap.ap[1:]:
            if st >= 0:
                hi += st * (cnt - 1)
            else:
                lo += st * (cnt - 1)
        key = ('P:' if 'PSUM' in sp else 'S:') + name
        if key not in self.spaces:
            self.spaces[key] = Space((rowlen * es + BLK - 1) // BLK)
        return key, (lo * es) // BLK, (hi * es + es - 1) // BLK + 1

    def _hazards(self, op, reads, writes):
        deps = op.deps
        slot = EIDX[op.eng] if (not op.is_dma and op.eng in CENG) else None
        for ap in reads:
            r = self._range(ap)
            if r is None:
                continue
            key, lo, hi = r
            s = self.spaces[key]
            deps.update(np.unique(s.last_w[lo:hi]).tolist())
            if slot is not None:
                s.last_r[lo:hi, slot] = op.id
            else:
                s.dma_readers.append((op.id, lo, hi))
        for ap in writes:
            r = self._range(ap)
            if r is None:
                continue
            key, lo, hi = r
            s = self.spaces[key]
            deps.update(np.unique(s.last_w[lo:hi]).tolist())
            deps.update(np.unique(s.last_r[lo:hi]).tolist())
            if s.dma_readers:
                keep = []
                for (oid, l2, h2) in s.dma_readers:
                    if l2 < hi and lo < h2:
                        deps.add(oid)
                        if l2 < lo:
                            keep.append((oid, l2, lo))
                        if hi < h2:
                            keep.append((oid, hi, h2))
                    else:
                        keep.append((oid, l2, h2))
                s.dma_readers = keep
            s.last_w[lo:hi] = op.id
            s.last_r[lo:hi] = -1
        deps.discard(-1)
        deps.discard(op.id)

    def op(self, eng, fn, reads, writes, is_dma=False, rkeys=(), wkeys=()):
        if self.dry:
            return None
        o = Op(eng, fn, is_dma)
        o.id = len(self.ops)
        self.ops.append(o)
        self._hazards(o, reads, writes)
        for k in rkeys:
            o.deps.update(self.keyw.get(k, ()))
        for k in wkeys:
            self.keyw.setdefault(k, []).append(o.id)
        if eng == 'pe':
            o.deps = {d for d in o.deps if self.ops[d].eng != 'pe' or self.ops[d].is_dma}
        if os.environ.get("FW_SERIAL") and o.id > 0:
            o.deps.add(o.id - 1)
        if not is_dma:
            pc = self.last_compute
            if self.cserial and pc is not None and not (eng == 'pe' and pc.eng == 'pe'):
                o.deps.add(pc.id)
            self.last_compute = o
        if is_dma:
            o.dk = self.n_dma
            self.n_dma += 1
            self.dma_ops.append(o)
            if o.dk >= N_DSEM:
                o.prev_dma = self.dma_ops[o.dk - N_DSEM]
                o.deps.add(o.prev_dma.id)
        return o

    def dma(self, out, in_, eng='sp', rkeys=(), wkeys=(), **kw):
        if True:
            eng = 'sp'
        return self.op(eng, lambda e: e.dma_start(out=out, in_=in_, **kw), [in_], [out], is_dma=True,
                       rkeys=rkeys, wkeys=wkeys)

    def mm(self, out, lhsT, rhs, start=True, stop=True, **kw):
        return self.op('pe', lambda e: e.matmul(out, lhsT, rhs, start=start, stop=stop, **kw),
                       [lhsT, rhs], [out])

    def tr(self, out, in_, ident):
        return self.op('pe', lambda e: e.transpose(out, in_, ident), [in_, ident], [out])

    def act(self, out, in_, func, bias=None, scale=1.0, accum_out=None, eng='act'):
        rd = [in_]
        kw = {}
        if bias is not None:
            kw['bias'] = bias
            if not isinstance(bias, (int, float)):
                rd.append(bias)
        if not isinstance(scale, (int, float)):
            rd.append(scale)
        wr = [out]
        if accum_out is not None:
            kw['accum_out'] = accum_out
            wr.append(accum_out)
        return self.op(eng, lambda e: e.activation(out=out, in_=in_, func=func, scale=scale, **kw), rd, wr)

    def tt(self, out, in0, in1, op, eng='dve'):
        return self.op(eng, lambda e: e.tensor_tensor(out=out, in0=in0, in1=in1, op=op), [in0, in1], [out])

    def ts(self, out, in0, s1, op0, s2=None, op1=None, eng='dve', accum_out=None):
        rd = [in0]
        if not isinstance(s1, (int, float)):
            rd.append(s1)
        if s2 is not None and not isinstance(s2, (int, float)):
            rd.append(s2)
        kw = {}
        if op1 is not None:
            kw['op1'] = op1
        wr = [out]
        if accum_out is not None:
            kw['accum_out'] = accum_out
            wr.append(accum_out)
        return self.op(eng, lambda e: e.tensor_scalar(out=out, in0=in0, scalar1=s1, scalar2=s2, op0=op0, **kw), rd, wr)

    def stt(self, out, in0, scalar, in1, op0, op1, eng='dve'):
        rd = [in0, in1]
        if not isinstance(scalar, (int, float)):
            rd.append(scalar)
        return self.op(eng, lambda e: e.scalar_tensor_tensor(out=out, in0=in0, scalar=scalar, in1=in1, op0=op0, op1=op1), rd, [out])

    def copy(self, out, in_, eng='dve'):
        if eng == 'act':
            return self.act(out, in_, AF.Copy)
        return self.op(eng, lambda e: e.tensor_copy(out=out, in_=in_), [in_], [out])

    def memset(self, out, val, eng='dve'):
        return self.op(eng, lambda e: e.memset(out, val), [], [out])

    def scan(self, out, d0, d1, initial, op0, op1):
        rd = [d0, d1]
        if not isinstance(initial, (int, float)):
            rd.append(initial)
        return self.op('dve', lambda e: e.tensor_tensor_scan(out=out, data0=d0, data1=d1, initial=initial, op0=op0, op1=op1), rd, [out])

    def reduce(self, out, in_, op, axis=AX.X, eng='dve'):
        return self.op(eng, lambda e: e.tensor_reduce(out=out, in_=in_, axis=axis, op=op), [in_], [out])

    def recip(self, out, in_):
        return self.op('dve', lambda e: e.reciprocal(out=out, in_=in_), [in_], [out])

    def analyze(self):
        nce = len(CENG)
        cur = {e: np.zeros(nce, dtype=np.int64) for e in ALLENG}
        dknown = {e: set() for e in ALLENG}
        count = {e: 0 for e in CENG}
        byeng = {e: [] for e in CENG}
        for o in self.ops:
            e = o.eng
            c = cur[e]
            dk = dknown[e]
            need = {}
            for d in o.deps:
                dop = self.ops[d]
                if dop.is_dma:
                    if d not in dk or os.environ.get("FW_NOPRUNE"):
                        dk.add(d)
                        o.waits.append(dop)
                        np.maximum(c, dop.sk, out=c)
                else:
                    b = EIDX[dop.eng]
                    if dop.pos + 1 > need.get(b, 0):
                        need[b] = dop.pos + 1
            noprune = bool(os.environ.get("FW_NOPRUNE"))
            for b, p in sorted(need.items(), key=lambda kv: -kv[1]):
                if c[b] >= p and not noprune:
                    continue
                t = byeng[CENG[b]][p - 1]
                t.signal = True
                o.waits.append(t)
                np.maximum(c, t.sk, out=c)
                c[b] = max(c[b], p)
            o.sk = c.copy()
            if not o.is_dma and e in CENG:
                o.pos = count[e]
                count[e] += 1
                byeng[e].append(o)
        self.ngen = {}
        for e in CENG:
            n = 0
            for o in byeng[e]:
                if o.signal:
                    o.semgen = n // SEM_ROLL
                    o.semval = n % SEM_ROLL + 1
                    n += 1
            self.ngen[e] = n // SEM_ROLL + 1
        self.byeng = byeng

    def emit(self, stack):
        nc = self.nc
        esem = {e: [stack.enter_context(nc.semaphore(f"s_{e}{g}")) for g in range(self.ngen[e])] for e in CENG}
        dsem = [stack.enter_context(nc.semaphore(f"d{i}")) for i in range(min(N_DSEM, max(self.n_dma, 1)))]
        block = stack.enter_context(nc.Block())
        ops = self.ops
        n_dma = self.n_dma

        def stream(ename):
            def body(eng):
                for o in ops:
                    if o.eng != ename:
                        continue
                    for w in o.waits:
                        if w.is_dma:
                            eng.wait_ge(dsem[w.dk % N_DSEM], 16 * (w.dk // N_DSEM + 1))
                        else:
                            eng.wait_ge(esem[w.eng][w.semgen], w.semval)
                    ins = o.fn(eng)
                    if o.is_dma:
                        ins.then_inc(dsem[o.dk % N_DSEM], 16)
                    elif o.signal:
                        ins.then_inc(esem[o.eng][o.semgen], 1)
                if ename == 'sp':
                    for i in range(min(N_DSEM, n_dma)):
                        cnt = (n_dma - 1 - i) // N_DSEM + 1
                        eng.wait_ge(dsem[i], 16 * cnt)
            return body

        block.sync(stream('sp'))
        block.tensor(stream('pe'))
        block.scalar(stream('act'))
        block.vector(stream('dve'))
        block.gpsimd(stream('pool'))

    def stats(self):
        from collections import Counter
        c = Counter(o.eng + ('_dma' if o.is_dma else '') for o in self.ops)
        w = sum(len(o.waits) for o in self.ops)
        s = sum(1 for o in self.ops if o.signal)
        return dict(c), w, s

from contextlib import ExitStack

D = 1024
DFF = 2816
NJ = 22
EPS = 1e-6
MAGIC = 12582912.0
TWO_PI = float(2 * np.pi)
ARENA_F = 47104


def bc(ap, shape):
    return ap.to_broadcast(list(shape))


class Arena:
    def __init__(self, base):
        self.base = base
        self.top = 0
        self.marks = []
        self.peak = 0

    def push(self):
        self.marks.append(self.top)

    def pop(self):
        self.top = self.marks.pop()

    def alloc(self, dims, dt=F32, parts=128):
        if isinstance(dims, int):
            dims = (dims,)
        n = int(np.prod(dims))
        es = 4 if dt == F32 else 2
        nb = (n * es + 63) // 64 * 64
        lo = self.top
        self.top += nb
        self.peak = max(self.peak, self.top)
        assert self.top <= ARENA_F * 4, f"arena overflow {self.top}"
        a = self.base[:, lo // 4:(lo + nb) // 4]
        if dt != F32:
            a = a.bitcast(dt)
        a = a[0:parts, 0:n]
        if len(dims) == 2:
            a = a.rearrange("p (a b) -> p a b", a=dims[0])
        elif len(dims) == 3:
            a = a.rearrange("p (a b c) -> p a b c", a=dims[0], b=dims[1])
        return a


class Ring:
    def __init__(self, bld, nbytes):
        self.b = bld
        self.size = nbytes
        self.region = bld.ar.alloc(nbytes // 4)
        self.plan = bld.plan
        self.dry = bld.kb.dry
        self.newplan = []
        self.issued = 0
        self.cons = 0
        self.head = 0
        self.live = []
        self.aps = {}
        self.held = set()

    def _view(self, lo, parts, n):
        a = self.region[:, lo // 4:(lo + (n * 2 + 3) // 4 * 4) // 4].bitcast(BF16)
        return a[0:parts, 0:n]

    def _try_issue(self):
        src, parts, n, rkeys = self.plan[self.issued]
        if any(k not in self.b.done for k in rkeys):
            return False
        nb = (n * 2 + 63) // 64 * 64
        lo = self.head
        if lo + nb > self.size:
            lo = 0
        hi = lo + nb
        for (_, l2, h2) in self.live:
            if l2 < hi and lo < h2:
                return False
        ap = self._view(lo, parts, n)
        dst = ap
        if len(src.shape) == 3:
            dst = ap.rearrange("p (a b) -> p a b", a=src.shape[1])
        self.b.kb.dma(dst, src, rkeys=rkeys)
        self.live.append((self.issued, lo, hi))
        self.aps[self.issued] = ap
        self.head = hi
        self.issued += 1
        return True

    def unhold(self):
        self.held = set()

    def fetch(self, src, rkeys=(), hold=False):
        parts, n = src.shape[0], int(np.prod(src.shape[1:]))
        if self.dry:
            self.newplan.append((src, parts, n, tuple(rkeys)))
            return self._view(0, parts, n)
        k = self.cons
        if hold:
            self.held.add(k)
        self.live = [x for x in self.live if x[0] >= k or x[0] in self.held]
        while self.issued < len(self.plan) and (self.issued - k) < 24:
            if not self._try_issue():
                break
        assert self.issued > k, "ring too small"
        ps, pp, pn, _ = self.plan[k]
        assert (pp, pn) == (parts, n), "plan mismatch"
        self.cons += 1
        return self.aps.pop(k)


class Group:
    pass


class Builder:
    def __init__(self, S, PAST, plan=None):
        self.S, self.PAST = S, PAST
        self.plan = plan
        self.kb = KB()
        self.kb.dry = plan is None
        self.nc = self.kb.nc
        self.st = ExitStack()
        self.din = {}
        self.dout = {}
        self._evac_i = 0
        self.ps_i = 0
        self.ps_res = set()
        self.debug = False

    def inp(self, name, shape, dt=F32):
        t = self.nc.dram_tensor(name, list(shape), dt, kind="ExternalInput").ap()
        self.din[name] = t
        return t

    def outp(self, name, shape):
        t = self.nc.dram_tensor(name, list(shape), F32, kind="ExternalOutput").ap()
        self.dout[name] = t
        return t

    def scratch(self, name, shape, dt=BF16):
        return self.nc.dram_tensor(name, list(shape), dt).ap()

    def dbg(self, name, ap, dims):
        if name not in self.dout:
            self.outp(name, [128] + list(dims))
        self.ar.push()
        t = self.ar.alloc(dims)
        self.kb.copy(t, ap)
        self.kb.dma(self.dout[name], t)
        self.ar.pop()

    def psum(self):
        while True:
            i = self.ps_i % 8
            self.ps_i += 1
            if i not in self.ps_res:
                return self.ps[i]

    def reserve(self, n):
        out = []
        for i in range(8):
            if i not in self.ps_res and len(out) < n:
                self.ps_res.add(i)
                out.append(i)
        return out

    def release(self, idxs):
        for i in idxs:
            self.ps_res.discard(i)

    def evac(self, out, in_, scale=None):
        self._evac_i += 1
        if self._evac_i % 2 == 0:
            if scale is None:
                self.kb.copy(out, in_, eng='dve')
            else:
                self.kb.ts(out, in_, float(scale), ALU.mult)
        else:
            self.kb.act(out, in_, AF.Copy, scale=1.0 if scale is None else float(scale))

    def declare(self):
        S, PAST = self.S, self.PAST
        i = self.inp
        self.xp = i("xp", [S, D]); self.xs = i("xs", [64, D])
        self.c_lat = i("c_lat", [PAST, 256]); self.c_kr = i("c_kr", [PAST, 64])
        self.st_C = i("st_C", [4, 128, 128]); self.st_n = i("st_n", [4, 128]); self.st_m = i("st_m", [1, 4])
        self.st_re = i("st_re", [16, 128]); self.st_im = i("st_im", [16, 128])
        self.c_sbk = i("c_sbk", [PAST, 512]); self.c_sbv = i("c_sbv", [PAST, 512])
        self.w_gate = i("w_gate", [4, D, DFF]); self.w_up = i("w_up", [4, D, DFF]); self.w_down = i("w_down", [4, DFF, D])
        self.e_win = i("e_win", [D, 2760]); self.e_wout = i("e_wout", [D, D])
        self.wuq_d = i("wuq", [384, 768]); self.wukv_d = i("wukv", [256, 1024])
        self.o_win = i("o_win", [D, 2048]); self.o_wout = i("o_wout", [D, D]); self.wglu_d = i("wglu", [512, 512])
        self.grows = i("grows", [69, 128]); self.bi_d = i("bi", [1, 4]); self.bf_d = i("bf", [1, 4])
        self.s5rows = i("s5rows", [48, 128])
        self.s5B = [i("s5B_re", [2048, 16]), i("s5B_im", [2048, 16])]
        self.s5C = [i("s5C_re", [512, 64]), i("s5C_im", [512, 64])]
        self.c_ident = i("c_ident", [128, 128]); self.c_masksb = i("c_masksb", [128, 128]); self.c_maskml = i("c_maskml", [64, 64])
        self.c_uincl = i("c_uincl", [128, 128]); self.c_sel = i("c_sel", [8, 1024]); self.c_cm = i("c_cm", [128, 512])
        self.c_iota = i("c_iota", [128, 128])
        self.c_ropep = [i("c_cosp", [64, S]), i("c_sinp", [64, S])]
        self.c_ropes = [i("c_coss", [64, 64]), i("c_sins", [64, 64])]
        o = self.outp
        self.groups = []
        for g, (n, nm) in enumerate([(S, "p"), (64, "s")]):
            G = Group()
            G.name = nm; G.n = n
            G.T = 512 if nm == "p" else 64
            G.ntile = n // G.T
            G.npast = 0 if nm == "p" else PAST // 512
            G.pos0 = 0 if nm == "p" else PAST
            G.x = self.xp if nm == "p" else self.xs
            G.rope = self.c_ropep if nm == "p" else self.c_ropes
            G.y = o("y_" + nm, [n, D]); G.lat = o("lat_" + nm, [n, 256]); G.kr = o("kr_" + nm, [n, 64])
            G.oC = o("C_" + nm, [4, 128, 128]); G.on = o("n_" + nm, [4, 128]); G.om = o("m_" + nm, [1, 4])
            G.ore = o("re_" + nm, [16, 128]); G.oim = o("im_" + nm, [16, 128])
            G.ok = o("k_" + nm, [n, 512]); G.ov = o("v_" + nm, [n, 512])
            NT = G.npast + G.ntile
            G.KnT = self.scratch("KnT_" + nm, [NT, 128, 2048]); G.KrT = self.scratch("KrT_" + nm, [NT, 64, 512])
            G.V = self.scratch("V_" + nm, [NT, 128, 2048])
            G.sKT = self.scratch("sKT_" + nm, [NT, 128, 2048]); G.sV = self.scratch("sV_" + nm, [NT, 128, 2048])
            self.groups.append(G)
        sc = self.scratch
        self.wgu = sc("wgu", [4, NJ, 128, 2048]); self.wdn = sc("wdn", [4, 8, 128, DFF])
        self.winF_e = sc("winF_e", [19, 128, 1024]); self.winT_e = sc("winT_e", [2, 128, 4096]); self.wout_e = sc("wout_e", [8, 128, 1024])
        self.winF_o = sc("winF_o", [12, 128, 1024]); self.winT_o = sc("winT_o", [2, 128, 4096]); self.wout_o = sc("wout_o", [8, 128, 1024])
        self.s5tab = sc("s5tab", [128, 4096 + 4096], F32)
        self.s5mat = sc("s5mat", [128, 8192], BF16)
        nc = self.nc
        self.arena = self.st.enter_context(nc.sbuf_tensor("arena", [128, ARENA_F], F32))
        self.ps = [self.st.enter_context(nc.psum_tensor(f"ps{k}", [128, 512], F32)) for k in range(8)]

    def load_consts(self):
        kb, ar = self.kb, self.ar
        A = ar.alloc
        self.ident = A(128); kb.dma(self.ident, self.c_ident)
        t = A(128); kb.dma(t, self.c_masksb)
        self.masksb = A(128, BF16); kb.copy(self.masksb, t)
        self.maskml = A(64, parts=64); kb.dma(self.maskml, self.c_maskml)
        t2 = A(128); kb.dma(t2, self.c_uincl)
        self.uincl = A(128, BF16); kb.copy(self.uincl, t2)
        self.ones = A(128, BF16); kb.memset(self.ones, 1.0)
        self.negones = A(128, BF16); kb.memset(self.negones, -1.0)
        self.negonesf = A(128); kb.memset(self.negonesf, -1.0)
        self.onef = A(8); kb.memset(self.onef, 1.0)
        self.sel = A((8, 128), parts=8); kb.dma(self.sel, self.c_sel.rearrange("r (g p) -> r g p", g=8))
        self.cm = A(512); kb.dma(self.cm, self.c_cm)
        gr = A(128); kb.memset(gr, 0.0); kb.dma(gr[0:69, :], self.grows)
        p = self.psum()
        kb.tr(p[:, 0:128], gr, self.ident)
        self.gains = A(72); kb.copy(self.gains[:, 0:69], p[:, 0:69])
        self.bi = A(4); kb.dma(self.bi, bc(self.bi_d[0:1, :], [128, 4]))
        bfp = A(4); kb.dma(bfp, bc(self.bf_d[0:1, :], [128, 4]))
        self.nbf = A(4); kb.ts(self.nbf, bfp, -1.0, ALU.mult)
        self.Cst = A((4, 129)); self.mst = A(4); self.wst = A((16, 2))
        self.wuq = A((3, 768), BF16); self.wukv = A((2, 1024), BF16); self.wglu = A((4, 512), BF16)

    def conv_rows(self, src, ncols, stores, wkeys):
        kb, ar = self.kb, self.ar
        st = self.cv_f[self.cv_i % 2]; sb = self.cv_b[self.cv_i % 2]
        self.cv_i += 1
        kb.dma(st[:, 0:ncols], src)
        h = (ncols // 2 + 63) // 64 * 64
        engs = ['act', 'dve', 'pool']
        e1 = engs[self.cv_i % 3]; e2 = engs[(self.cv_i + 1) % 3]
        kb.copy(sb[:, 0:h], st[:, 0:h], eng=e1)
        kb.copy(sb[:, h:ncols], st[:, h:ncols], eng=e2)
        for dst, c0, c1, shape in stores:
            v = sb[:, c0:c1]
            if shape is not None:
                v = v.rearrange("p (a b) -> p a b", a=shape[0])
            kb.dma(dst, v, eng='pool', wkeys=wkeys)

    def prologue_weights(self):
        kb, ar = self.kb, self.ar
        ar.push()
        self.cv_f = [ar.alloc(DFF) for _ in range(2)]
        self.cv_b = [ar.alloc(DFF, BF16) for _ in range(2)]
        self.cv_i = 0
        for f in range(4):
            for kc in range(8):
                for g, W in enumerate((self.w_gate, self.w_up)):
                    dst = self.wgu[f].rearrange("j p x -> p j x")[:, :, kc * 256 + g * 128: kc * 256 + g * 128 + 128]
                    self.conv_rows(W[f, kc * 128:(kc + 1) * 128, :], DFF, [(dst, 0, DFF, (NJ, 128))], [("wgu", f)])
            for jc in range(NJ):
                dst = self.wdn[f].rearrange("m p x -> p m x")[:, :, jc * 128:(jc + 1) * 128]
                self.conv_rows(self.w_down[f, jc * 128:(jc + 1) * 128, :], D, [(dst, 0, D, (8, 128))], [("wdn", f)])
        for kc in range(8):
            F = self.winF_e.rearrange("n p x -> p n x")
            ks = slice(kc * 128, (kc + 1) * 128)
            stores = [(F[:, 0:4, ks], 0, 512, (4, 128)), (F[:, 4:8, ks], 512, 1024, (4, 128)),
                      (F[:, 8:12, ks], 1536, 2048, (4, 128)), (F[:, 12:15, ks], 2056, 2440, (3, 128)),
                      (F[:, 15:17, ks], 2440, 2696, (2, 128)),
                      (self.winF_e[17][:, kc * 128:kc * 128 + 64], 2696, 2760, None),
                      (self.winF_e[18][:, kc * 128:kc * 128 + 8], 2048, 2056, None),
                      (self.winT_e[0][:, kc * 512:(kc + 1) * 512], 512, 1024, None),
                      (self.winT_e[1][:, kc * 512:(kc + 1) * 512], 1024, 1536, None)]
            self.conv_rows(self.e_win[ks, :], 2760, stores, [("we",)])
            Fo = self.winF_o.rearrange("n p x -> p n x")
            stores = [(Fo[:, 0:12, ks], 0, 1536, (12, 128)),
                      (self.winT_o[0][:, kc * 512:(kc + 1) * 512], 1024, 1536, None),
                      (self.winT_o[1][:, kc * 512:(kc + 1) * 512], 1536, 2048, None)]
            self.conv_rows(self.o_win[ks, :], 2048, stores, [("wo",)])
            for W, dstt, key in ((self.e_wout, self.wout_e, "we"), (self.o_wout, self.wout_o, "wo")):
                dst = dstt.rearrange("m p x -> p m x")[:, :, ks]
                self.conv_rows(W[ks, :], D, [(dst, 0, D, (8, 128))], [(key,)])
        for kc in range(3):
            stg = self.cv_f[self.cv_i % 2]; self.cv_i += 1
            kb.dma(stg[:, 0:768], self.wuq_d[kc * 128:(kc + 1) * 128, :])
            kb.copy(self.wuq[:, kc, :], stg[:, 0:768], eng='dve')
        for kc in range(2):
            stg = self.cv_f[self.cv_i % 2]; self.cv_i += 1
            kb.dma(stg[:, 0:1024], self.wukv_d[kc * 128:(kc + 1) * 128, :])
            kb.copy(self.wukv[:, kc, :], stg[:, 0:1024], eng='act')
        for kc in range(4):
            stg = self.cv_f[self.cv_i % 2]; self.cv_i += 1
            kb.dma(stg[:, 0:512], self.wglu_d[kc * 128:(kc + 1) * 128, :])
            kb.copy(self.wglu[:, kc, :], stg[:, 0:512], eng='dve')
        ar.pop()

    def rmsnorm(self, xT, C, T, gcol, out, dim):
        kb, ar = self.kb, self.ar
        ar.push()
        sq = ar.alloc((C, T), BF16)
        kb.act(sq, xT, AF.Square)
        p = self.psum()
        for c in range(C):
            kb.mm(p[:, 0:T], self.ones, sq[:, c, :], start=(c == 0), stop=(c == C - 1))
        rs = ar.alloc(T)
        kb.act(rs, p[:, 0:T], AF.Sqrt, scale=1.0 / dim, bias=self.epsc[:, 0:1])
        kb.recip(rs, rs)
        for c in range(C):
            kb.stt(out[:, c, :], xT[:, c, :], self.gains[:, gcol + c:gcol + c + 1], rs, ALU.mult, ALU.mult)
        ar.pop()

    def transpose_out(self, srcT, rows, T, dst, eng_out='pool'):
        kb, ar = self.kb, self.ar
        tb = min(128, T); nb = T // tb
        tot = sum(r for _, r in srcT)
        ar.push()
        stg = ar.alloc((nb, tot), parts=tb)
        c0 = 0
        for ap, r in srcT:
            p = self.psum()
            if r < 128 and tb < 128:
                tmp = ar.alloc(128, parts=r)
                kb.memset(tmp, 0.0)
                kb.copy(tmp[:, 0:tb], ap[0:r, 0:tb])
                kb.tr(p[:, 0:r], tmp, self.ident[0:r, 0:r])
            else:
                for b in range(nb):
                    kb.tr(p[0:tb, b * r:(b + 1) * r], ap[0:r, b * tb:(b + 1) * tb], self.ident[0:r, 0:r])
            self.evac(stg[:, :, c0:c0 + r], p[0:tb, 0:nb * r].rearrange("p (b r) -> p b r", b=nb))
            c0 += r
        kb.dma(dst.rearrange("(b p) f -> p b f", p=tb), stg, eng=eng_out)
        ar.pop()

    def load_xT(self, G, ti, xT):
        kb, ar = self.kb, self.ar
        T = G.T; tb = min(128, T); nb = T // tb
        ar.push()
        stg = ar.alloc((nb, D), parts=tb)
        kb.dma(stg, G.x[ti * T:(ti + 1) * T, :].rearrange("(b p) f -> p b f", p=tb))
        for c in range(8):
            p = self.psum()
            for b in range(nb):
                kb.tr(p[:, b * tb:(b + 1) * tb], stg[:, b, c * 128:(c + 1) * 128], self.ident[0:tb, 0:tb])
            self.evac(xT[:, c, :], p[:, 0:T])
        ar.pop()

    def ffn(self, xT, T, f):
        kb, ar = self.kb, self.ar
        ar.push()
        hn = ar.alloc((8, T), BF16)
        self.rmsnorm(xT, 8, T, f * 8, hn, D)
        a = ar.alloc((NJ, T), BF16)
        sg = [ar.alloc(T) for _ in range(2)]
        for j in range(NJ):
            w = self.ring.fetch(self.wgu[f, j], rkeys=[("wgu", f)]).rearrange("p (k g c) -> p k g c", k=8, g=2)
            pg = self.psum(); pu = self.psum()
            for kc in range(8):
                kb.mm(pg[:, 0:T], w[:, kc, 0, :], hn[:, kc, :], start=(kc == 0), stop=(kc == 7))
            for kc in range(8):
                kb.mm(pu[:, 0:T], w[:, kc, 1, :], hn[:, kc, :], start=(kc == 0), stop=(kc == 7))
            s = sg[j % 2]
            kb.act(s, pg[:, 0:T], AF.Silu)
            kb.tt(a[:, j, :], s, pu[:, 0:T], ALU.mult)
        for m in range(8):
            w = self.ring.fetch(self.wdn[f, m], rkeys=[("wdn", f)]).rearrange("p (j c) -> p j c", j=NJ)
            p = self.psum()
            for j in range(NJ):
                kb.mm(p[:, 0:T], w[:, j, :], a[:, j, :], start=(j == 0), stop=(j == NJ - 1))
            kb.stt(xT[:, m, :], p[:, 0:T], 0.5, xT[:, m, :], ALU.mult, ALU.add)
        ar.pop()

    def out_proj(self, xT, T, mixed, wout, key):
        kb = self.kb
        for m in range(8):
            w = self.ring.fetch(wout[m], rkeys=[(key,)]).rearrange("p (k c) -> p k c", k=8)
            p = self.psum()
            for kc in range(8):
                kb.mm(p[:, 0:T], w[:, kc, :], mixed[:, kc, :], start=(kc == 0), stop=(kc == 7))
            kb.tt(xT[:, m, :], p[:, 0:T], xT[:, m, :], ALU.add)

    def inF(self, winF, ci, key, hn, T, ncols=128):
        kb = self.kb
        w = self.ring.fetch(winF[ci], rkeys=[(key,)]).rearrange("p (k c) -> p k c", k=8)
        p = self.psum()
        for kc in range(8):
            kb.mm(p[0:ncols, 0:T], w[:, kc, 0:ncols], hn[:, kc, :], start=(kc == 0), stop=(kc == 7))
        return p[0:ncols, 0:T]

    def even_mixer(self, G, ti, xT):
        kb, ar = self.kb, self.ar
        T = G.T; NC = T // 64
        kt = G.npast + ti
        ar.push()
        hn = ar.alloc((8, T), BF16)
        self.rmsnorm(xT, 8, T, 32 + 0, hn, D)
        mixed = ar.alloc((8, T), BF16)
        ar.push()
        pgt = self.inF(self.winF_e, 18, "we", hn, T, ncols=8)
        gT = ar.alloc(T, parts=8); kb.copy(gT, pgt, eng='act')
        ga = ar.alloc((4, T)); gb = ar.alloc((4, T))
        for g in range(8):
            p = self.psum()
            kb.mm(p[:, 0:T], self.sel[0:8, g, :], gT[0:8, :])
            h = g % 4
            if g < 4:
                kb.act(ga[:, h, :], p[:, 0:T], AF.Identity, bias=self.bi[:, h:h + 1])
            else:
                kb.act(gb[:, h, :], p[:, 0:T], AF.Exp, scale=-1.0, bias=self.nbf[:, h:h + 1])
        gl = ar.alloc((4, T))
        kb.act(gl, gb, AF.Ln, bias=self.onec[:, 0:1])
        for h in range(4):
            kb.scan(gb[:, h, :], self.cm[:, 0:T], gl[:, h, :], 0.0, ALU.mult, ALU.add)
        kb.tt(ga, ga, gb, ALU.add)
        amax = ar.alloc((4, NC)); mext = ar.alloc((4, NC + 1)); Ac = ar.alloc((4, NC)); wi = ar.alloc((4, NC))
        kb.reduce(amax.rearrange("p h c -> p (h c)"), ga.rearrange("p h (c t) -> p (h c) t", t=64), ALU.max)
        gbend = gb.rearrange("p h (c t) -> p h c t", t=64)[:, :, :, 63]
        kb.copy(mext[:, :, 0], self.mst)
        for h in range(4):
            kb.scan(mext[:, h, 1:NC + 1], amax[:, h, :], gbend[:, h, :], self.mst[:, h:h + 1], ALU.max, ALU.subtract)
        kb.tt(Ac, mext[:, :, 1:NC + 1], gbend, ALU.add)
        kb.tt(wi, mext[:, :, 0:NC], Ac, ALU.subtract)
        kb.act(wi, wi, AF.Exp)
        kb.copy(self.mst, mext[:, :, NC])
        Ab = bc(Ac.rearrange("p h c -> p (h c)").unsqueeze(2), [128, 4 * NC, 64])
        gav = ga.rearrange("p h (c t) -> p (h c) t", t=64); gbv = gb.rearrange("p h (c t) -> p (h c) t", t=64)
        kb.tt(gav, gav, Ab, ALU.subtract); kb.act(ga, ga, AF.Exp)
        kb.tt(gbv, gbv, Ab, ALU.subtract); kb.act(gb, gb, AF.Exp)
        pw = self.psum()
        for c in range(NC):
            for h in range(4):
                kb.mm(pw[0:64, c * 4 + h:c * 4 + h + 1], ga[0:1, h, c * 64:(c + 1) * 64], self.onef[0:1, 0:1])
        wacol = ar.alloc((NC, 4), parts=64)
        kb.copy(wacol.rearrange("p c h -> p (c h)"), pw[0:64, 0:NC * 4])
        qmT = ar.alloc((4, T), BF16); kmT = ar.alloc((4, T), BF16); osig = ar.alloc((4, T))
        for h in range(4):
            p = self.inF(self.winF_e, h, "we", hn, T)
            kb.copy(qmT[:, h, :], p, eng='act')
        for h in range(4):
            p = self.inF(self.winF_e, 4 + h, "we", hn, T)
            kb.stt(kmT[:, h, :], p, 128 ** -0.5, ga[:, h, :], ALU.mult, ALU.mult)
        for h in range(4):
            p = self.inF(self.winF_e, 8 + h, "we", hn, T)
            kb.act(osig[:, h, :], p, AF.Sigmoid)
        ktok = ar.alloc((NC, 4, 128), BF16, parts=64); vaug = ar.alloc((NC, 4, 129), BF16, parts=64)
        kb.memset(vaug[:, :, :, 128:129], 1.0, eng='pool')
        wk = self.ring.fetch(self.winT_e[0], rkeys=[("we",)]).rearrange("p (k n) -> p k n", k=8)
        for c in range(NC):
            p = self.psum()
            for kc in range(8):
                kb.mm(p[0:64, :], hn[:, kc, c * 64:(c + 1) * 64], wk[:, kc, :], start=(kc == 0), stop=(kc == 7))
            for h in range(4):
                kb.ts(ktok[:, c, h, :], p[0:64, h * 128:(h + 1) * 128], wacol[:, c, h:h + 1], ALU.mult, 128 ** -0.5, ALU.mult)
        wv = self.ring.fetch(self.winT_e[1], rkeys=[("we",)]).rearrange("p (k n) -> p k n", k=8)
        for c in range(NC):
            p = self.psum()
            for kc in range(8):
                kb.mm(p[0:64, :], hn[:, kc, c * 64:(c + 1) * 64], wv[:, kc, :], start=(kc == 0), stop=(kc == 7))
            self.evac(vaug[:, c, :, 0:128], p[0:64, :].rearrange("p (h d) -> p h d", h=4))
        hm = ar.alloc((4, T))
        Cs = [ar.alloc((4, 129)) for _ in range(2)]
        Csb = [ar.alloc((4, 128), BF16) for _ in range(2)]
        nrep = [ar.alloc((4, 128), BF16) for _ in range(2)]
        GT = [ar.alloc((4, 64), BF16, parts=64) for _ in range(2)]
        dab = [ar.alloc((4, 64)) for _ in range(2)]
        for c in range(NC):
            b = c % 2
            cs = slice(c * 64, (c + 1) * 64)
            pS = self.psum(); pN = self.psum(); pD = self.psum(); pC0 = self.psum(); pC1 = self.psum()
            for h in range(4):
                kb.ts(Cs[b][:, h, :], self.Cst[:, h, :], wi[:, h, c:c + 1], ALU.mult)
                kb.copy(Csb[b][:, h, :], Cs[b][:, h, 0:128], eng='act')
                kb.copy(nrep[b][:, h, :], bc(Cs[b][:, h, 128:129], [128, 128]), eng='pool')
                kb.mm(pS[0:64, h * 64:(h + 1) * 64], kmT[:, h, cs], qmT[:, h, cs])
            kb.tt(GT[b], pS[0:64, 0:256].rearrange("p (h t) -> p h t", h=4), bc(self.maskml.unsqueeze(1), [64, 4, 64]), ALU.mult)
            for h in range(4):
                kb.mm(pN[:, h * 64:(h + 1) * 64], Csb[b][:, h, :], qmT[:, h, cs], start=True, stop=False)
                kb.mm(pN[:, h * 64:(h + 1) * 64], vaug[:, c, h, 0:128], GT[b][:, h, :], start=False, stop=True)
            for h in range(4):
                kb.mm(pD[:, h * 64:(h + 1) * 64], nrep[b][:, h, :], qmT[:, h, cs], start=True, stop=False)
                kb.mm(pD[:, h * 64:(h + 1) * 64], self.ones[0:64, :], GT[b][:, h, :], start=False, stop=True)
            for h in range(4):
                pc = pC0 if h < 2 else pC1
                kb.mm(pc[:, (h % 2) * 129:(h % 2) * 129 + 129], ktok[:, c, h, :], vaug[:, c, h, :])
            d = dab[b]
            kb.act(d, pD[:, 0:256].rearrange("p (h t) -> p h t", h=4), AF.Abs)
            kb.tt(d, d, gb[:, :, cs], ALU.max)
            kb.recip(d, d)
            kb.tt(hm[:, :, cs], pN[:, 0:256].rearrange("p (h t) -> p h t", h=4), d, ALU.mult)
            kb.tt(self.Cst[:, 0:2, :], pC0[:, 0:258].rearrange("p (h d) -> p h d", h=2), Cs[b][:, 0:2, :], ALU.add)
            kb.tt(self.Cst[:, 2:4, :], pC1[:, 0:258].rearrange("p (h d) -> p h d", h=2), Cs[b][:, 2:4, :], ALU.add)
        sq = kmT
        kb.act(sq, hm, AF.Square)
        rs = gl
        for h in range(4):
            p = self.psum()
            kb.mm(p[:, 0:T], self.ones, sq[:, h, :])
            kb.act(rs[:, h, :], p[:, 0:T], AF.Sqrt, scale=1.0 / 128, bias=self.epsc[:, 0:1])
        kb.recip(rs, rs)
        for h in range(4):
            kb.stt(hm[:, h, :], hm[:, h, :], self.gains[:, 56 + h:57 + h], rs[:, h, :], ALU.mult, ALU.mult)
        kb.tt(mixed[:, 0:4, :], hm, osig, ALU.mult)
        ar.pop()
        ar.push()
        cqT = ar.alloc((3, T)); ckvT = ar.alloc((2, T)); qr5 = ar.alloc((5, T), parts=64)
        for c in range(3):
            p = self.inF(self.winF_e, 12 + c, "we", hn, T); self.evac(cqT[:, c, :], p)
        for c in range(2):
            p = self.inF(self.winF_e, 15 + c, "we", hn, T); self.evac(ckvT[:, c, :], p)
        p = self.inF(self.winF_e, 17, "we", hn, T, ncols=64); self.evac(qr5[:, 4, :], p)
        cqn = ar.alloc((3, T), BF16)
        self.rmsnorm(cqT, 3, T, 60, cqn, 384)
        qnT = ar.alloc((4, T), BF16)
        for h in range(4):
            p = self.psum()
            for kc in range(3):
                kb.mm(p[:, 0:T], self.wuq[:, kc, h * 192:h * 192 + 128], cqn[:, kc, :], start=(kc == 0), stop=(kc == 2))
            kb.copy(qnT[:, h, :], p[:, 0:T], eng='act')
            p2 = self.psum()
            for kc in range(3):
                kb.mm(p2[0:64, 0:T], self.wuq[:, kc, h * 192 + 128:h * 192 + 192], cqn[:, kc, :], start=(kc == 0), stop=(kc == 2))
            kb.copy(qr5[:, h, :], p2[0:64, 0:T], eng='dve')
        cs2 = ar.alloc(T, parts=64); sn2 = ar.alloc(T, parts=64)
        kb.dma(cs2, G.rope[0][:, ti * T:(ti + 1) * T]); kb.dma(sn2, G.rope[1][:, ti * T:(ti + 1) * T])
        qrf = ar.alloc((5, T), parts=64); qrb = ar.alloc((5, T), BF16, parts=64)
        ar.push()
        t1 = ar.alloc((5, T), parts=64); t2 = ar.alloc((5, T), parts=64)
        kb.tt(t1, qr5, bc(cs2.unsqueeze(1), [64, 5, T]), ALU.mult)
        kb.tt(t2[0:32], qr5[32:64], bc(sn2[32:64].unsqueeze(1), [32, 5, T]), ALU.mult)
        kb.tt(t2[32:64], qr5[0:32], bc(sn2[0:32].unsqueeze(1), [32, 5, T]), ALU.mult, eng='pool')
        kb.tt(qrf[0:32], t1[0:32], t2[0:32], ALU.subtract)
        kb.tt(qrf[32:64], t1[32:64], t2[32:64], ALU.add, eng='pool')
        ar.pop()
        kb.copy(qrb, qrf, eng='act')
        latf = ar.alloc((2, T)); latb = ar.alloc((2, T), BF16)
        self.rmsnorm(ckvT, 2, T, 63, latf, 256)
        kb.copy(latb, latf, eng='act')
        r0 = ti * T
        self.transpose_out([(latf[:, 0, :], 128), (latf[:, 1, :], 128)], 256, T, G.lat[r0:r0 + T, :])
        self.transpose_out([(qrf[:, 4, :], 64)], 64, T, G.kr[r0:r0 + T, :])
        self.kv_expand(G, kt, latb, T)
        kb.dma(G.KrT[kt][:, 0:T], qrb[:, 4, :], eng='pool', wkeys=[("kv" + G.name, kt)])
        self.done.add(("kv" + G.name, kt))
        self.mla_attend(G, ti, T, qnT, qrb, mixed)
        ar.pop()
        if G.name == "p" and ti == 0 and self.debug:
            self.dbg("dbg_mixed", mixed, (8, T))
        self.out_proj(xT, T, mixed, self.wout_e, "we")
        ar.pop()

    def kv_expand(self, G, kt, latb, T):
        kb, ar = self.kb, self.ar
        ar.push()
        kn = ar.alloc((4, T), BF16)
        for h in range(4):
            p = self.psum()
            for kc in range(2):
                kb.mm(p[:, 0:T], self.wukv[:, kc, h * 256:h * 256 + 128], latb[:, kc, :], start=(kc == 0), stop=(kc == 1))
            self.evac(kn[:, h, :], p[:, 0:T])
        kb.dma(G.KnT[kt].rearrange("p (h t) -> p h t", h=4)[:, :, 0:T], kn, eng='pool', wkeys=[("kv" + G.name, kt)])
        if self.debug and G.name == "p" and kt == 0:
            self.dbg("dbg_kn", kn, (4, T))
        tb = min(128, T); nb = T // tb
        vt = ar.alloc((nb, 512), BF16, parts=tb)
        wv = self.wukv.rearrange("p k (h x) -> p k h x", h=4)[:, :, :, 128:256]
        for b in range(nb):
            p = self.psum()
            for kc in range(2):
                kb.mm(p[0:tb, :], latb[:, kc, b * tb:(b + 1) * tb], wv[:, kc, :, :], start=(kc == 0), stop=(kc == 1))
            self.evac(vt[:, b, :], p[0:tb, :])
        kb.dma(G.V[kt].rearrange("p (b x) -> p b x", b=4)[0:tb, 0:nb, :], vt, eng='pool', wkeys=[("kv" + G.name, kt)])
        ar.pop()

    def mla_attend(self, G, ti, T, qnT, qrb, mixed):
        kb, ar = self.kb, self.ar
        scale = 192 ** -0.5
        ktc = G.npast + ti
        ar.push()
        Pb = [ar.alloc(T, BF16) for _ in range(3)]
        rden = ar.alloc(T)
        pi = 0
        for h in range(4):
            res = self.reserve(2)
            pO = self.ps[res[0]]; pDn = self.ps[res[1]]
            first = True
            for kt in range(0, ktc + 1):
                diag = (kt == ktc)
                nk = T if diag else 512
                key = [("kv" + G.name, kt)]
                self.ring.unhold()
                knT = self.ring.fetch(G.KnT[kt][:, h * 512:h * 512 + nk], rkeys=key, hold=True)
                krT = self.ring.fetch(G.KrT[kt][:, 0:nk], rkeys=key, hold=True)
                kbs = min(128, nk); nkb = nk // kbs
                vv = self.ring.fetch(G.V[kt].rearrange("p (b x) -> p b x", b=4)[0:kbs, 0:nkb, h * 128:(h + 1) * 128], rkeys=key)
                vv = vv.rearrange("p (b x) -> p b x", b=nkb)
                if self.debug and G.name == "p" and ti == 0 and h < 2:
                    self.dbg("dbg_knT%d" % h, knT, (512,))
                    if h == 0:
                        self.dbg("dbg_qnT", qnT, (4, T))
                for b in range(nkb):
                    ko = b * kbs
                    qs = ko if diag else 0
                    last = diag and (b == nkb - 1)
                    pS = self.psum()
                    kb.mm(pS[0:kbs, qs:T], knT[:, ko:ko + kbs], qnT[:, h, qs:T], start=True, stop=False)
                    kb.mm(pS[0:kbs, qs:T], krT[0:64, ko:ko + kbs], qrb[:, h, qs:T], start=False, stop=True)
                    P = Pb[pi % 3]; pi += 1
                    kb.act(P[0:kbs, qs:T], pS[0:kbs, qs:T], AF.Exp, scale=scale)
                    if diag and kbs == 128:
                        kb.ts(P[64:128, qs:qs + 64], P[64:128, qs:qs + 64], 0.0, ALU.mult)
                    kb.mm(pO[:, qs:T], vv[:, b, :], P[0:kbs, qs:T], start=first, stop=last)
                    kb.mm(pDn[:, qs:T], self.ones[0:kbs, :], P[0:kbs, qs:T], start=first, stop=last)
                    first = False
            kb.recip(rden, pDn[:, 0:T])
            kb.tt(mixed[:, 4 + h, :], pO[:, 0:T], rden, ALU.mult)
            self.release(res)
            self.ring.unhold()
        ar.pop()

    def odd_mixer(self, G, ti, xT):
        kb, ar = self.kb, self.ar
        T = G.T
        kt = G.npast + ti
        ar.push()
        hn = ar.alloc((8, T), BF16)
        self.rmsnorm(xT, 8, T, 32 + 8, hn, D)
        mixed = ar.alloc((8, T), BF16)
        kb.cserial = ("odd" in self.serial_stages) or ("s5" in self.serial_stages)
        ar.push()
        Ts = min(128, T); nsc = T // Ts
        Ec = ar.alloc((16, 128)); Es = ar.alloc((16, 128)); BT = ar.alloc((32, 128), BF16); Cm = ar.alloc((32, 128), BF16)
        kb.dma(Ec, self.s5tab[:, 0:2048].rearrange("p (s t) -> p s t", s=16), rkeys=[("s5",)])
        kb.dma(Es, self.s5tab[:, 4096:4096 + 2048].rearrange("p (s t) -> p s t", s=16), rkeys=[("s5",)])
        kb.dma(BT, self.s5mat[:, 0:4096].rearrange("p (s t) -> p s t", s=32), rkeys=[("s5",)])
        kb.dma(Cm, self.s5mat[:, 4096:8192].rearrange("p (s t) -> p s t", s=32), rkeys=[("s5",)])
        uf = ar.alloc((4, T)); ub = ar.alloc((4, T), BF16)
        for j in range(4):
            p = self.inF(self.winF_o, j, "wo", hn, T)
            kb.copy(uf[:, j, :], p, eng='act'); kb.copy(ub[:, j, :], p, eng='dve')
        yT = ar.alloc((4, T))
        tA = [ar.alloc(T)] * 2; tB = [ar.alloc(T)] * 2
        cR = [ar.alloc(T)] * 2; cI = [ar.alloc(T)] * 2
        wR = [ar.alloc(T)] * 2; wI = [ar.alloc(T)] * 2
        xR = [ar.alloc(T, BF16) for _ in range(4)]; xI = [ar.alloc(T, BF16) for _ in range(4)]
        tmp1 = ar.alloc(4)
        v3 = lambda a: a.rearrange("p (c t) -> p c t", t=Ts)
        for j in range(4):
            for sl in range(4):
                s = 4 * j + sl
                b = s % 2
                pr = self.psum(); pim = self.psum()
                kb.mm(pr[:, 0:T], BT[:, 2 * s, :], ub[:, j, :])
                kb.mm(pim[:, 0:T], BT[:, 2 * s + 1, :], ub[:, j, :])
                ecb = bc(Ec[:, s, 0:Ts].unsqueeze(1), [128, nsc, Ts]); esb = bc(Es[:, s, 0:Ts].unsqueeze(1), [128, nsc, Ts])
                kb.tt(v3(tA[b]), v3(pr[:, 0:T]), ecb, ALU.mult)
                kb.tt(v3(tB[b]), v3(pim[:, 0:T]), esb, ALU.mult)
                kb.tt(cR[b], tA[b], tB[b], ALU.add, eng='pool')
                kb.tt(v3(tA[b]), v3(pim[:, 0:T]), ecb, ALU.mult)
                kb.tt(v3(tB[b]), v3(pr[:, 0:T]), esb, ALU.mult)
                kb.tt(cI[b], tA[b], tB[b], ALU.subtract, eng='pool')
                rb = bc(self.rdec[:, s:s + 1], [128, Ts])
                for sc in range(nsc):
                    cs = slice(sc * Ts, (sc + 1) * Ts)
                    kb.scan(wR[b][:, cs], rb, cR[b][:, cs], self.wst[:, s, 0:1], ALU.mult, ALU.add)
                    kb.scan(wI[b][:, cs], rb, cI[b][:, cs], self.wst[:, s, 1:2], ALU.mult, ALU.add)
                    e = sc * Ts + Ts - 1
                    ecl = Ec[:, s, Ts - 1:Ts]; esl = Es[:, s, Ts - 1:Ts]
                    kb.ts(tmp1[:, 0:1], wI[b][:, e:e + 1], esl, ALU.mult)
                    kb.ts(tmp1[:, 1:2], wR[b][:, e:e + 1], esl, ALU.mult)
                    kb.stt(self.wst[:, s, 0:1], wR[b][:, e:e + 1], ecl, tmp1[:, 0:1], ALU.mult, ALU.subtract)
                    kb.stt(self.wst[:, s, 1:2], wI[b][:, e:e + 1], ecl, tmp1[:, 1:2], ALU.mult, ALU.add)
                xi = (s % 4)
                kb.tt(v3(tA[b]), v3(wR[b]), ecb, ALU.mult)
                kb.tt(v3(tB[b]), v3(wI[b]), esb, ALU.mult, eng='pool')
                kb.tt(xR[xi], tA[b], tB[b], ALU.subtract)
                kb.tt(v3(cR[b]), v3(wI[b]), ecb, ALU.mult, eng='pool')
                kb.tt(v3(cI[b]), v3(wR[b]), esb, ALU.mult)
                kb.tt(xI[xi], cR[b], cI[b], ALU.add, eng='pool')
            py = self.psum()
            for sl in range(4):
                s = 4 * j + sl
                kb.mm(py[:, 0:T], Cm[:, 2 * s, :], xR[s % 4], start=(sl == 0), stop=False)
                kb.mm(py[:, 0:T], Cm[:, 2 * s + 1, :], xI[s % 4], start=False, stop=(sl == 3))
            kb.stt(yT[:, j, :], uf[:, j, :], self.gains[:, 65 + j:66 + j], py[:, 0:T], ALU.mult, ALU.add)
        gq = ar.alloc((4, T)); gg = ar.alloc((4, T)); gbf = ar.alloc((4, T), BF16)
        kb.act(gq, yT, AF.Square)
        kb.ts(gq, gq, 0.044715, ALU.mult, 1.0, ALU.add)
        kb.tt(gq, gq, yT, ALU.mult)
        kb.act(gq, gq, AF.Sigmoid, scale=2.0 * 0.7978845608028654)
        kb.tt(gg, gq, yT, ALU.mult)
        kb.copy(gbf, gg, eng='act')
        for m in range(4):
            p = self.psum()
            for kc in range(4):
                kb.mm(p[:, 0:T], self.wglu[:, kc, m * 128:(m + 1) * 128], gbf[:, kc, :], start=(kc == 0), stop=(kc == 3))
            kb.act(gq[:, m, :], p[:, 0:T], AF.Sigmoid)
        kb.tt(mixed[:, 0:4, :], gg, gq, ALU.mult)
        ar.pop()
        kb.cserial = ("odd" in self.serial_stages) or ("sb" in self.serial_stages)
        ar.push()
        qT = ar.alloc((2, 4, T), BF16); kT = ar.alloc((4, T), BF16)
        kb.memset(qT, 0.0)
        for j in range(4):
            p = self.inF(self.winF_o, 4 + j, "wo", hn, T)
            kb.act(qT[0:64, 0, j, :], p[0:64, :], AF.Copy, scale=0.125)
            kb.act(qT[64:128, 1, j, :], p[64:128, :], AF.Copy, scale=0.125)
        for j in range(4):
            p = self.inF(self.winF_o, 8 + j, "wo", hn, T)
            self.evac(kT[:, j, :], p)
        kb.dma(G.sKT[kt].rearrange("p (j t) -> p j t", j=4)[:, :, 0:T], kT, eng='pool', wkeys=[("sb" + G.name, kt)])
        tb = min(128, T); nb = T // tb
        r0 = ti * T
        for which, (dst, oo) in enumerate(((None, G.ok), (G.sV, G.ov))):
            w = self.ring.fetch(self.winT_o[which], rkeys=[("wo",)]).rearrange("p (k n) -> p k n", k=8)
            ar.push()
            stf = ar.alloc((nb, 512), parts=tb)
            stb = ar.alloc((nb, 512), BF16, parts=tb)
            for b in range(nb):
                p = self.psum()
                for kc in range(8):
                    kb.mm(p[0:tb, :], hn[:, kc, b * tb:(b + 1) * tb], w[:, kc, :], start=(kc == 0), stop=(kc == 7))
                kb.copy(stf[:, b, :], p[0:tb, :], eng='act')
                if dst is not None:
                    kb.copy(stb[:, b, :], p[0:tb, :], eng='dve')
            kb.dma(oo[r0:r0 + T, :].rearrange("(b p) f -> p b f", p=tb), stf, eng='pool')
            if dst is not None:
                kb.dma(dst[kt].rearrange("p (b x) -> p b x", b=4)[0:tb, 0:nb, :], stb, eng='pool', wkeys=[("sb" + G.name, kt)])
            ar.pop()
        self.done.add(("sb" + G.name, kt))
        self.sb_attend(G, ti, T, qT, mixed)
        ar.pop()
        self.out_proj(xT, T, mixed, self.wout_o, "wo")
        ar.pop()

    def sb_attend(self, G, ti, T, qT, mixed):
        kb, ar = self.kb, self.ar
        ktc = G.npast + ti
        ar.push()
        Eb = [ar.alloc(T) for _ in range(2)]
        Lb = [ar.alloc(T, BF16) for _ in range(2)]
        Wb = [ar.alloc(T, BF16) for _ in range(2)]
        Racc = [ar.alloc(T) for _ in range(2)]
        for j in range(4):
            res = self.reserve(2)
            pO = [self.ps[res[0]], self.ps[res[1]]]
            for e in range(2):
                kb.memset(Racc[e], 0.0, eng='pool')
            first = True
            for kt in range(ktc, -1, -1):
                diag = (kt == ktc)
                nk = T if diag else 512
                key = [("sb" + G.name, kt)]
                self.ring.unhold()
                kT = self.ring.fetch(G.sKT[kt][:, j * 512:j * 512 + nk], rkeys=key, hold=True)
                kbs = min(128, nk); nkb = nk // kbs
                vv = self.ring.fetch(G.sV[kt].rearrange("p (b x) -> p b x", b=4)[0:kbs, 0:nkb, j * 128:(j + 1) * 128], rkeys=key)
                vv = vv.rearrange("p (b x) -> p b x", b=nkb)
                for b in range(nkb - 1, -1, -1):
                    ko = b * kbs
                    qs = ko if diag else 0
                    last = (kt == 0 and b == 0)
                    pZ = [self.psum(), self.psum()]
                    for e in range(2):
                        kb.mm(pZ[e][0:kbs, qs:T], kT[:, ko:ko + kbs], qT[:, e, j, qs:T], start=True, stop=False)
                    for e in range(2):
                        kb.act(Eb[e][0:kbs, qs:T], pZ[e][0:kbs, qs:T], AF.Exp)
                    for e in range(2):
                        kb.act(Lb[e][0:kbs, qs:T], Eb[e][0:kbs, qs:T], AF.Ln, bias=self.onec[0:kbs, 0:1])
                    if diag:
                        for e in range(2):
                            kb.tt(Lb[e][0:kbs, qs:qs + kbs], Lb[e][0:kbs, qs:qs + kbs], self.masksb[0:kbs, 0:kbs], ALU.mult, eng='pool')
                    for e in range(2):
                        kb.mm(pZ[e][0:kbs, qs:T], self.uincl[0:kbs, 0:kbs], Lb[e][0:kbs, qs:T], start=False, stop=first)
                        if not first:
                            kb.mm(pZ[e][0:kbs, qs:T], self.negonesf[:, 0:kbs], Racc[e][:, qs:T], start=False, stop=True)
                    for e in range(2):
                        kb.act(Wb[e][0:kbs, qs:T], pZ[e][0:kbs, qs:T], AF.Exp)
                    if diag:
                        for e in range(2):
                            kb.tt(Wb[e][0:kbs, qs:qs + kbs], Wb[e][0:kbs, qs:qs + kbs], self.masksb[0:kbs, 0:kbs], ALU.mult, eng='pool')
                    for e in range(2):
                        kb.mm(pO[e][0:64, qs:T], vv[:, b, 64 * e:64 * e + 64], Wb[e][0:kbs, qs:T], start=first, stop=last)
                    if not last:
                        for e in range(2):
                            kb.tt(Racc[e][0:kbs, qs:T], Racc[e][0:kbs, qs:T], Lb[e][0:kbs, qs:T], ALU.add, eng='pool')
                    first = False
            for e in range(2):
                kb.copy(mixed[64 * e:64 * e + 64, 4 + j, :], pO[e][0:64, 0:T], eng='act')
            self.release(res)
            self.ring.unhold()
        ar.pop()


RING_BYTES = 32768


def _sin_of(self, out, ang, shift, shape):
    kb, ar = self.kb, self.ar
    ar.push()
    t = ar.alloc(shape); k = ar.alloc(shape)
    kb.ts(t, ang, float(shift), ALU.add)
    kb.ts(k, t, 1.0 / TWO_PI, ALU.mult, MAGIC, ALU.add)
    kb.ts(k, k, -MAGIC, ALU.add)
    kb.stt(t, k, -TWO_PI, t, ALU.mult, ALU.add)
    kb.ts(t, t, float(np.pi), ALU.min, -float(np.pi), ALU.max)
    kb.act(out, t, AF.Sin)
    ar.pop()


def prologue_s5(self):
    kb, ar = self.kb, self.ar
    A = ar.alloc
    self.rdec = A(16)
    ar.push()
    rows = A(128); kb.memset(rows, 0.0); kb.dma(rows[0:48, :], self.s5rows)
    p = self.psum(); kb.tr(p[:, 0:128], rows, self.ident)
    prm = A(48); kb.copy(prm, p[:, 0:48])
    a_r = prm[:, 0:16]; a_i = prm[:, 16:32]; ldt = prm[:, 32:48]
    dt = A(16); dar = A(16); dai = A(16)
    kb.act(dt, ldt, AF.Exp)
    kb.tt(dar, dt, a_r, ALU.mult); kb.tt(dai, dt, a_i, ALU.mult)
    kb.act(self.rdec, dar, AF.Exp)
    cs = A(16); sn = A(16)
    _sin_of(self, sn, dai, 0.0, 16); _sin_of(self, cs, dai, np.pi / 2, 16)
    abr = A(16); abi = A(16); den = A(16); t1 = A(16); t2 = A(16); cr = A(16); ci = A(16)
    kb.tt(abr, self.rdec, cs, ALU.mult); kb.tt(abi, self.rdec, sn, ALU.mult)
    kb.tt(t1, a_r, a_r, ALU.mult); kb.tt(t2, a_i, a_i, ALU.mult); kb.tt(den, t1, t2, ALU.add); kb.recip(den, den)
    kb.ts(abr, abr, -1.0, ALU.add)
    kb.tt(t1, abr, a_r, ALU.mult); kb.tt(t2, abi, a_i, ALU.mult); kb.tt(cr, t1, t2, ALU.add); kb.tt(cr, cr, den, ALU.mult)
    kb.tt(t1, abi, a_r, ALU.mult); kb.tt(t2, abr, a_i, ALU.mult); kb.tt(ci, t1, t2, ALU.subtract); kb.tt(ci, ci, den, ALU.mult)
    Bre = A((16, 16)); Bim = A((16, 16))
    kb.dma(Bre, self.s5B[0].rearrange("(s q) c -> q s c", q=128)); kb.dma(Bim, self.s5B[1].rearrange("(s q) c -> q s c", q=128))
    crb = bc(cr.unsqueeze(2), [128, 16, 16]); cib = bc(ci.unsqueeze(2), [128, 16, 16])
    u1 = A((16, 16)); u2 = A((16, 16)); bbr = A((16, 16)); bbi = A((16, 16))
    kb.tt(u1, Bre, crb, ALU.mult); kb.tt(u2, Bim, cib, ALU.mult); kb.tt(bbr, u1, u2, ALU.subtract)
    kb.tt(u1, Bim, crb, ALU.mult); kb.tt(u2, Bre, cib, ALU.mult); kb.tt(bbi, u1, u2, ALU.add)
    bb2 = A((16, 2, 32)); kb.memset(bb2, 0.0)
    for ri, bb in enumerate((bbr, bbi)):
        kb.copy(bb2[0:64, :, ri, 0:16], bb[0:64]); kb.copy(bb2[64:128, :, ri, 16:32], bb[64:128])
    BT = A((32, 128), BF16); kb.memset(BT, 0.0)
    for s in range(16):
        p = self.psum()
        for ri in range(2):
            kb.tr(p[0:32, ri * 128:(ri + 1) * 128], bb2[:, s, ri, :], self.ident)
        a = s % 4
        kb.copy(BT[32 * a:32 * a + 32, 2 * s:2 * s + 2, :], p[0:32, 0:256].rearrange("p (r c) -> p r c", r=2))
    kb.dma(self.s5mat[:, 0:4096].rearrange("p (s t) -> p s t", s=32), BT, eng='pool', wkeys=[("s5",)])
    Cm = A((32, 128), BF16); kb.memset(Cm, 0.0)
    for ri in range(2):
        Cin = A((4, 64)); kb.dma(Cin, self.s5C[ri].rearrange("(j q) p -> q j p", q=128))
        CT = A((4, 128), parts=64)
        for j in range(4):
            p = self.psum()
            kb.tr(p[0:64, 0:128], Cin[:, j, :], self.ident)
            kb.act(CT[:, j, :], p[0:64, 0:128], AF.Copy, scale=(1.0 if ri == 0 else -1.0))
        for e in range(2):
            for sl in range(4):
                c0 = (2 * sl + e) * 16
                kb.copy(Cm[64 * e:64 * e + 64, ri + 2 * sl:32:8, c0:c0 + 16], CT[:, :, c0:c0 + 16])
    kb.dma(self.s5mat[:, 4096:8192].rearrange("p (s t) -> p s t", s=32), Cm, eng='pool', wkeys=[("s5",)])
    io = A(128); kb.dma(io, self.c_iota)
    ang = A((16, 128)); tb = A((16, 128))
    for s in range(16):
        kb.ts(ang[:, s, :], io, dai[:, s:s + 1], ALU.mult)
    _sin_of(self, tb, ang, np.pi / 2, (16, 128))
    kb.dma(self.s5tab[:, 0:2048].rearrange("p (s t) -> p s t", s=16), tb, eng='pool', wkeys=[("s5",)])
    tb2 = A((16, 128))
    _sin_of(self, tb2, ang, 0.0, (16, 128))
    kb.dma(self.s5tab[:, 4096:4096 + 2048].rearrange("p (s t) -> p s t", s=16), tb2, eng='pool', wkeys=[("s5",)])
    ar.pop()


def sample_prep(self, G):
    kb, ar = self.kb, self.ar
    A = ar.alloc
    for kt in range(G.npast):
        ar.push()
        rs = slice(kt * 512, (kt + 1) * 512)
        stg = A((4, 256)); kb.dma(stg, self.c_lat[rs, :].rearrange("(b p) f -> p b f", p=128))
        latb = A((2, 512), BF16)
        for c in range(2):
            p = self.psum()
            for b in range(4):
                kb.tr(p[:, b * 128:(b + 1) * 128], stg[:, b, c * 128:(c + 1) * 128], self.ident)
            self.evac(latb[:, c, :], p[:, 0:512])
        self.kv_expand(G, kt, latb, 512)
        stg2 = A((4, 64)); kb.dma(stg2, self.c_kr[rs, :].rearrange("(b p) f -> p b f", p=128))
        p = self.psum()
        for b in range(4):
            kb.tr(p[0:64, b * 128:(b + 1) * 128], stg2[:, b, :], self.ident)
        krb = A(512, BF16, parts=64); self.evac(krb, p[0:64, 0:512])
        kb.dma(G.KrT[kt], krb, eng='pool', wkeys=[("kv" + G.name, kt)])
        stk = A((4, 512)); kb.dma(stk, self.c_sbk[rs, :].rearrange("(b p) f -> p b f", p=128))
        kT = A((4, 512), BF16)
        for j in range(4):
            p = self.psum()
            for b in range(4):
                kb.tr(p[:, b * 128:(b + 1) * 128], stk[:, b, j * 128:(j + 1) * 128], self.ident)
            self.evac(kT[:, j, :], p[:, 0:512])
        kb.dma(G.sKT[kt].rearrange("p (j t) -> p j t", j=4), kT, eng='pool', wkeys=[("sb" + G.name, kt)])
        stv = A((4, 512)); kb.dma(stv, self.c_sbv[rs, :].rearrange("(b p) f -> p b f", p=128))
        vb = A((4, 512), BF16); kb.copy(vb, stv, eng='pool')
        kb.dma(G.sV[kt].rearrange("p (b x) -> p b x", b=4), vb, eng='pool', wkeys=[("sb" + G.name, kt)])
        self.done.add(("kv" + G.name, kt)); self.done.add(("sb" + G.name, kt))
        ar.pop()
    ar.push()
    stC = A((4, 128)); kb.dma(stC, self.st_C.rearrange("h v k -> v h k"))
    for h in range(4):
        p = self.psum(); kb.tr(p[:, 0:128], stC[:, h, :], self.ident)
        self.evac(self.Cst[:, h, 0:128], p[:, 0:128])
    rows = A(128); kb.memset(rows, 0.0); kb.dma(rows[0:4, :], self.st_n)
    p = self.psum(); kb.tr(p[:, 0:128], rows, self.ident)
    kb.copy(self.Cst[:, :, 128], p[:, 0:4])
    kb.dma(self.mst, bc(self.st_m[0:1, :], [128, 4]))
    rows2 = A(128); kb.memset(rows2, 0.0); kb.dma(rows2[0:16, :], self.st_re); kb.dma(rows2[16:32, :], self.st_im)
    p = self.psum(); kb.tr(p[:, 0:128], rows2, self.ident)
    kb.copy(self.wst[:, :, 0], p[:, 0:16]); kb.copy(self.wst[:, :, 1], p[:, 16:32])
    ar.pop()


def write_states(self, G):
    kb, ar = self.kb, self.ar
    A = ar.alloc
    ar.push()
    stg = A((4, 128))
    for h in range(4):
        p = self.psum(); kb.tr(p[:, 0:128], self.Cst[:, h, 0:128], self.ident)
        self.evac(stg[:, h, :], p[:, 0:128])
    kb.dma(G.oC.rearrange("h v k -> v h k"), stg, eng='pool')
    tmp = A(4); kb.copy(tmp, self.Cst[:, :, 128])
    p = self.psum(); kb.tr(p[0:4, 0:128], tmp, self.ident)
    st2 = A(128, parts=4); kb.copy(st2, p[0:4, 0:128]); kb.dma(G.on, st2, eng='pool')
    kb.dma(G.om, self.mst[0:1, :], eng='pool')
    tmp2 = A(32); kb.copy(tmp2[:, 0:16], self.wst[:, :, 0]); kb.copy(tmp2[:, 16:32], self.wst[:, :, 1])
    p = self.psum(); kb.tr(p[0:32, 0:128], tmp2, self.ident)
    st3 = A(128, parts=32); kb.copy(st3, p[0:32, 0:128])
    kb.dma(G.ore, st3[0:16, :], eng='pool'); kb.dma(G.oim, st3[16:32, :], eng='pool')
    ar.pop()


def final_out(self, G, ti, xT):
    kb, ar = self.kb, self.ar
    T = G.T
    ar.push()
    yT = ar.alloc((8, T))
    self.rmsnorm(xT, 8, T, 48, yT, D)
    r0 = ti * T
    self.transpose_out([(yT[:, c, :], 128) for c in range(8)], D, T, G.y[r0:r0 + T, :])
    ar.pop()


def build(self, stages=("ffn0", "even", "ffn1", "ffn2", "odd", "ffn3")):
    kb = self.kb
    import os as _os
    self.serial_stages = tuple(_os.environ.get("KSERIAL", "odd").split(","))
    self.declare()
    plan = None
    for pass_ in (0, 1):
        kb.dry = (pass_ == 0)
        self.plan = plan
        self.ar = ar = Arena(self.arena)
        self.ps_i = 0; self.ps_res = set(); self._evac_i = 0
        self.done = set()
        self.load_consts()
        self.epsc = ar.alloc(1); kb.memset(self.epsc, EPS)
        self.onec = ar.alloc(1); kb.memset(self.onec, 1.0)
        self.ring = Ring(self, RING_BYTES)
        xTfull = ar.alloc((8, 512))
        self.prologue_weights()
        prologue_s5(self)
        self.done.update([("wgu", f) for f in range(4)] + [("wdn", f) for f in range(4)] + [("we",), ("wo",), ("s5",)])
        for G in self.groups:
            if G.name == "p":
                kb.memset(self.Cst, 0.0); kb.memset(self.mst, 0.0); kb.memset(self.wst, 0.0)
            else:
                sample_prep(self, G)
            for ti in range(G.ntile):
                xT = xTfull[:, :, 0:G.T]
                self.load_xT(G, ti, xT)
                SER = self.serial_stages
                kb.cserial = "ffn" in SER
                if "ffn0" in stages: self.ffn(xT, G.T, 0)
                kb.cserial = "even" in SER
                if "even" in stages: self.even_mixer(G, ti, xT)
                kb.cserial = "ffn" in SER
                if "ffn1" in stages: self.ffn(xT, G.T, 1)
                if "ffn2" in stages: self.ffn(xT, G.T, 2)
                kb.cserial = "odd" in SER
                if "odd" in stages: self.odd_mixer(G, ti, xT)
                kb.cserial = "ffn" in SER
                if "ffn3" in stages: self.ffn(xT, G.T, 3)
                kb.cserial = True
                final_out(self, G, ti, xT)
            write_states(self, G)
        if pass_ == 0:
            plan = self.ring.newplan
    print("arena peak", self.ar.peak, "ops", len(kb.ops), "plan", len(plan))
    kb.analyze()
    print(kb.stats())
    kb.emit(self.st)
    self.st.close()
    return self.nc


def rope_tables(pos):
    half = 32
    inv = (np.float32(10000.0) ** (-np.arange(half, dtype=np.float32) / np.float32(half))).astype(np.float32)
    ang = pos.astype(np.float32)[None, :] * inv[:, None]
    c = np.cos(ang).astype(np.float32); s = np.sin(ang).astype(np.float32)
    return np.concatenate([c, c], 0), np.concatenate([s, s], 0)


def host_consts(S, PAST):
    k = np.arange(128)
    c = {}
    c["c_ident"] = np.eye(128, dtype=np.float32)
    c["c_masksb"] = (k[:, None] < k[None, :]).astype(np.float32)
    c["c_maskml"] = (k[:64, None] <= k[None, :64]).astype(np.float32)
    c["c_uincl"] = -(k[:, None] >= k[None, :]).astype(np.float32)
    sel = np.zeros((8, 8, 128), np.float32)
    for g in range(8):
        sel[g, g, :] = 1.0
    c["c_sel"] = sel.reshape(8, 1024)
    cm = np.ones((128, 512), np.float32); cm[:, ::64] = 0.0
    c["c_cm"] = cm
    c["c_iota"] = np.tile(np.arange(1, 129, dtype=np.float32)[None, :], (128, 1))
    c["c_cosp"], c["c_sinp"] = rope_tables(np.arange(S))
    c["c_coss"], c["c_sins"] = rope_tables(PAST + np.arange(64))
    return c


def make_inmaps(inputs, ncores, S, PAST):
    f = lambda a: np.ascontiguousarray(np.asarray(a), dtype=np.float32)
    I = {k: np.asarray(v) for k, v in inputs.items()}
    shared = dict(
        w_gate=f(I["ffn_w_gate"].reshape(4, D, DFF)), w_up=f(I["ffn_w_up"].reshape(4, D, DFF)),
        w_down=f(I["ffn_w_down"].reshape(4, DFF, D)),
        e_win=f(I["even_w_in"][0]), e_wout=f(I["even_w_out"][0]), wuq=f(I["mla_w_uq"][0]), wukv=f(I["mla_w_ukv"][0]),
        o_win=f(I["odd_w_in"][0]), o_wout=f(I["odd_w_out"][0]), wglu=f(I["s5_w_glu"][0]),
        grows=f(np.concatenate([I["norm_ffn"].reshape(32, 128), I["norm_mix"].reshape(16, 128), I["norm_final"].reshape(8, 128),
                                I["mlstm_out_norm"][0].reshape(4, 128), I["mla_q_norm"][0].reshape(3, 128),
                                I["mla_kv_norm"][0].reshape(2, 128), I["s5_D"][0].reshape(4, 128)], 0)),
        bi=f(I["mlstm_b_igate"][0][None]), bf=f(I["mlstm_b_fgate"][0][None]),
        s5rows=f(np.concatenate([I["s5_A_re"][0].reshape(16, 128), I["s5_A_im"][0].reshape(16, 128),
                                 np.repeat(I["s5_log_dt"][0][:, None], 64, axis=1).reshape(16, 128)], 0)),
        s5B_re=f(I["s5_B_re"][0].reshape(2048, 16)), s5B_im=f(I["s5_B_im"][0].reshape(2048, 16)),
        s5C_re=f(I["s5_C_re"][0].reshape(512, 64)), s5C_im=f(I["s5_C_im"][0].reshape(512, 64)),
    )
    shared.update(host_consts(S, PAST))
    maps = []
    for b in range(ncores):
        m = dict(shared)
        m.update(xp=f(I["x_prompt"][b]), xs=f(I["x_sample"][b]), c_lat=f(I["cache_mla_latent"][0, b]),
                 c_kr=f(I["cache_mla_krope"][0, b]), st_C=f(I["state_mlstm_C"][0, b]), st_n=f(I["state_mlstm_n"][0, b]),
                 st_m=f(I["state_mlstm_m"][0, b][None]), st_re=f(I["state_s5_re"][0, b].reshape(16, 128)),
                 st_im=f(I["state_s5_im"][0, b].reshape(16, 128)), c_sbk=f(I["cache_sb_k"][0, b].reshape(PAST, 512)),
                 c_sbv=f(I["cache_sb_v"][0, b].reshape(PAST, 512)))
        maps.append(m)
    return maps


def assemble(results, S):
    def st(name, shape):
        return np.stack([np.asarray(r[name], dtype=np.float32).reshape(shape) for r in results], 0)
    outs = [st("y_p", (S, D)), st("y_s", (64, D))]
    for nm, n in (("p", S), ("s", 64)):
        outs += [st("lat_" + nm, (n, 256))[None], st("kr_" + nm, (n, 64))[None], st("C_" + nm, (4, 128, 128))[None],
                 st("n_" + nm, (4, 128))[None], st("m_" + nm, (4,))[None], st("re_" + nm, (32, 64))[None],
                 st("im_" + nm, (32, 64))[None], st("k_" + nm, (n, 8, 64))[None], st("v_" + nm, (n, 8, 64))[None]]
    return tuple(outs)


_S, _PAST = 8192, 2048


def kernel(**inputs):
    ncores = 8
    b = Builder(_S, _PAST)
    nc = build(b)
    maps = make_inmaps(inputs, ncores, _S, _PAST)
    res = run_bass_kernel_spmd(nc, maps, core_ids=list(range(ncores)))
    return assemble(res.results, _S)
```

```python
import os
import numpy as np
import concourse.bass as bass
import concourse.mybir as mybir
from concourse.bass_utils import run_bass_kernel_spmd

F32 = mybir.dt.float32
BF16 = mybir.dt.bfloat16
I32 = mybir.dt.int32
ALU = mybir.AluOpType
AF = mybir.ActivationFunctionType
AX = mybir.AxisListType

CENG = ['pe', 'act', 'dve', 'pool']
ALLENG = CENG + ['sp']
EIDX = {e: i for i, e in enumerate(ALLENG)}
BLK = 32
DBLK = 2048
N_DSEM = 48
SEM_ROLL = 30000


def _esize(dt):
    return {F32: 4, BF16: 2, I32: 4}.get(dt, None) or mybir.dt.size(dt)


class Op:
    __slots__ = ('eng', 'fn', 'deps', 'id', 'pos', 'signal', 'semgen', 'semval', 'waits',
                 'is_dma', 'dk', 'sk', 'prev_dma')

    def __init__(self, eng, fn, is_dma):
        self.eng = eng
        self.fn = fn
        self.is_dma = is_dma
        self.deps = set()
        self.signal = False
        self.waits = []
        self.pos = -1
        self.sk = None
        self.prev_dma = None


class Space:
    def __init__(self, nblocks):
        self.last_w = np.full(nblocks, -1, dtype=np.int64)
        self.last_r = np.full((nblocks, len(CENG)), -1, dtype=np.int64)
        self.dma_readers = []


class KB:
    def __init__(self):
        self.nc = bass.Bass("TRN2", target_bir_lowering=False)
        self.ops = []
        self.spaces = {}
        self.untracked = set()
        self.n_dma = 0
        self.dma_ops = []
        self.dry = False
        self.keyw = {}
        self.last_compute = None
        self.cserial = True

    def _range(self, ap):
        sp = str(ap.space)
        t = ap.tensor
        name = ap.name
        es = _esize(ap.dtype)
        if 'DRAM' in sp:
            return None
            lo = ap.offset
            hi = ap.offset
            for st, cnt in ap.ap:
                if st >= 0:
                    hi += st * (cnt - 1)
                else:
                    lo += st * (cnt - 1)
            key = 'D:' + name
            if key not in self.spaces:
                n = 1
                for s in t.shape:
                    n *= s
                self.spaces[key] = Space((n * es + DBLK - 1) // DBLK)
            return key, (lo * es) // DBLK, (hi * es + es - 1) // DBLK + 1
        rowlen = ap.ap[0][0]
        if rowlen == 0:
            rowlen = t.shape[1] if len(t.shape) == 2 else int(np.prod(t.shape[1:]))
        off = ap.offset % rowlen
        lo = off
        hi = off
        for st, cnt in ap.ap[1:]:
            if st >= 0:
                hi += st * (cnt - 1)
            else:
                lo += st * (cnt - 1)
        key = ('P:' if 'PSUM' in sp else 'S:') + name
        if key not in self.spaces:
            self.spaces[key] = Space((rowlen * es + BLK - 1) // BLK)
        return key, (lo * es) // BLK, (hi * es + es - 1) // BLK + 1

    def _hazards(self, op, reads, writes):
        deps = op.deps
        slot = EIDX[op.eng] if (not op.is_dma and op.eng in CENG) else None
        for ap in reads:
            r = self._range(ap)
            if r is None:
                continue
            key, lo, hi = r
            s = self.spaces[key]
            deps.update(np.unique(s.last_w[lo:hi]).tolist())
            if slot is not None:
                s.last_r[lo:hi, slot] = op.id
            else:
                s.dma_readers.append((op.id, lo, hi))
        for ap in writes:
            r = self._range(ap)
            if r is None:
                continue
            key, lo, hi = r
            s = self.spaces[key]
            deps.update(np.unique(s.last_w[lo:hi]).tolist())
            deps.update(np.unique(s.last_r[lo:hi]).tolist())
            if s.dma_readers:
                keep = []
                for (oid, l2, h2) in s.dma_readers:
                    if l2 < hi and lo < h2:
                        deps.add(oid)
                        if l2 < lo:
                            keep.append((oid, l2, lo))
                        if hi < h2:
                            keep.append((oid, hi, h2))
                    else:
                        keep.append((oid, l2, h2))
                s.dma_readers = keep
            s.last_w[lo:hi] = op.id
            s.last_r[lo:hi] = -1
        deps.discard(-1)
        deps.discard(op.id)

    def op(self, eng, fn, reads, writes, is_dma=False, rkeys=(), wkeys=()):
        if self.dry:
            return None
        o = Op(eng, fn, is_dma)
        o.id = len(self.ops)
        self.ops.append(o)
        self._hazards(o, reads, writes)
        for k in rkeys:
            o.deps.update(self.keyw.get(k, ()))
        for k in wkeys:
            self.keyw.setdefault(k, []).append(o.id)
        if eng == 'pe':
            o.deps = {d for d in o.deps if self.ops[d].eng != 'pe' or self.ops[d].is_dma}
        if os.environ.get("FW_SERIAL") and o.id > 0:
            o.deps.add(o.id - 1)
        if not is_dma:
            pc = self.last_compute
            if self.cserial and pc is not None and pc.eng != eng:
                o.deps.add(pc.id)
            self.last_compute = o
        if is_dma:
            o.dk = self.n_dma
            self.n_dma += 1
            self.dma_ops.append(o)
            if o.dk >= N_DSEM:
                o.prev_dma = self.dma_ops[o.dk - N_DSEM]
                o.deps.add(o.prev_dma.id)
        return o

    def dma(self, out, in_, eng='sp', rkeys=(), wkeys=(), **kw):
        if True:
            eng = 'sp'
        return self.op(eng, lambda e: e.dma_start(out=out, in_=in_, **kw), [in_], [out], is_dma=True,
                       rkeys=rkeys, wkeys=wkeys)

    def mm(self, out, lhsT, rhs, start=True, stop=True, **kw):
        return self.op('pe', lambda e: e.matmul(out, lhsT, rhs, start=start, stop=stop, **kw),
                       [lhsT, rhs], [out])

    def tr(self, out, in_, ident):
        return self.op('pe', lambda e: e.transpose(out, in_, ident), [in_, ident], [out])

    def act(self, out, in_, func, bias=None, scale=1.0, accum_out=None, eng='act'):
        rd = [in_]
        kw = {}
        if bias is not None:
            kw['bias'] = bias
            if not isinstance(bias, (int, float)):
                rd.append(bias)
        if not isinstance(scale, (int, float)):
            rd.append(scale)
        wr = [out]
        if accum_out is not None:
            kw['accum_out'] = accum_out
            wr.append(accum_out)
        return self.op(eng, lambda e: e.activation(out=out, in_=in_, func=func, scale=scale, **kw), rd, wr)

    def tt(self, out, in0, in1, op, eng='dve'):
        return self.op(eng, lambda e: e.tensor_tensor(out=out, in0=in0, in1=in1, op=op), [in0, in1], [out])

    def ts(self, out, in0, s1, op0, s2=None, op1=None, eng='dve', accum_out=None):
        rd = [in0]
        if not isinstance(s1, (int, float)):
            rd.append(s1)
        if s2 is not None and not isinstance(s2, (int, float)):
            rd.append(s2)
        kw = {}
        if op1 is not None:
            kw['op1'] = op1
        wr = [out]
        if accum_out is not None:
            kw['accum_out'] = accum_out
            wr.append(accum_out)
        return self.op(eng, lambda e: e.tensor_scalar(out=out, in0=in0, scalar1=s1, scalar2=s2, op0=op0, **kw), rd, wr)

    def stt(self, out, in0, scalar, in1, op0, op1, eng='dve'):
        rd = [in0, in1]
        if not isinstance(scalar, (int, float)):
            rd.append(scalar)
        return self.op(eng, lambda e: e.scalar_tensor_tensor(out=out, in0=in0, scalar=scalar, in1=in1, op0=op0, op1=op1), rd, [out])

    def copy(self, out, in_, eng='dve'):
        if eng == 'act':
            return self.act(out, in_, AF.Copy)
        return self.op(eng, lambda e: e.tensor_copy(out=out, in_=in_), [in_], [out])

    def memset(self, out, val, eng='dve'):
        return self.op(eng, lambda e: e.memset(out, val), [], [out])

    def scan(self, out, d0, d1, initial, op0, op1):
        rd = [d0, d1]
        if not isinstance(initial, (int, float)):
            rd.append(initial)
        return self.op('dve', lambda e: e.tensor_tensor_scan(out=out, data0=d0, data1=d1, initial=initial, op0=op0, op1=op1), rd, [out])

    def reduce(self, out, in_, op, axis=AX.X, eng='dve'):
        return self.op(eng, lambda e: e.tensor_reduce(out=out, in_=in_, axis=axis, op=op), [in_], [out])

    def recip(self, out, in_):
        return self.op('dve', lambda e: e.reciprocal(out=out, in_=in_), [in_], [out])

    def analyze(self):
        nce = len(CENG)
        cur = {e: np.zeros(nce, dtype=np.int64) for e in ALLENG}
        dknown = {e: set() for e in ALLENG}
        count = {e: 0 for e in CENG}
        byeng = {e: [] for e in CENG}
        for o in self.ops:
            e = o.eng
            c = cur[e]
            dk = dknown[e]
            need = {}
            for d in o.deps:
                dop = self.ops[d]
                if dop.is_dma:
                    if d not in dk or os.environ.get("FW_NOPRUNE"):
                        dk.add(d)
                        o.waits.append(dop)
                        np.maximum(c, dop.sk, out=c)
                else:
                    b = EIDX[dop.eng]
                    if dop.pos + 1 > need.get(b, 0):
                        need[b] = dop.pos + 1
            noprune = bool(os.environ.get("FW_NOPRUNE"))
            for b, p in sorted(need.items(), key=lambda kv: -kv[1]):
                if c[b] >= p and not noprune:
                    continue
                t = byeng[CENG[b]][p - 1]
                t.signal = True
                o.waits.append(t)
                np.maximum(c, t.sk, out=c)
                c[b] = max(c[b], p)
            o.sk = c.copy()
            if not o.is_dma and e in CENG:
                o.pos = count[e]
                count[e] += 1
                byeng[e].append(o)
        self.ngen = {}
        for e in CENG:
            n = 0
            for o in byeng[e]:
                if o.signal:
                    o.semgen = n // SEM_ROLL
                    o.semval = n % SEM_ROLL + 1
                    n += 1
            self.ngen[e] = n // SEM_ROLL + 1
        self.byeng = byeng

    def emit(self, stack):
        nc = self.nc
        esem = {e: [stack.enter_context(nc.semaphore(f"s_{e}{g}")) for g in range(self.ngen[e])] for e in CENG}
        dsem = [stack.enter_context(nc.semaphore(f"d{i}")) for i in range(min(N_DSEM, max(self.n_dma, 1)))]
        block = stack.enter_context(nc.Block())
        ops = self.ops
        n_dma = self.n_dma

        def stream(ename):
            def body(eng):
                for o in ops:
                    if o.eng != ename:
                        continue
                    for w in o.waits:
                        if w.is_dma:
                            eng.wait_ge(dsem[w.dk % N_DSEM], 16 * (w.dk // N_DSEM + 1))
                        else:
                            eng.wait_ge(esem[w.eng][w.semgen], w.semval)
                    ins = o.fn(eng)
                    if o.is_dma:
                        ins.then_inc(dsem[o.dk % N_DSEM], 16)
                    elif o.signal:
                        ins.then_inc(esem[o.eng][o.semgen], 1)
                if ename == 'sp':
                    for i in range(min(N_DSEM, n_dma)):
                        cnt = (n_dma - 1 - i) // N_DSEM + 1
                        eng.wait_ge(dsem[i], 16 * cnt)
            return body

        block.sync(stream('sp'))
        block.tensor(stream('pe'))
        block.scalar(stream('act'))
        block.vector(stream('dve'))
        block.gpsimd(stream('pool'))

    def stats(self):
        from collections import Counter
        c = Counter(o.eng + ('_dma' if o.is_dma else '') for o in self.ops)
        w = sum(len(o.waits) for o in self.ops)
        s = sum(1 for o in self.ops if o.signal)
        return dict(c), w, s

from contextlib import ExitStack

D = 1024
DFF = 2816
NJ = 22
EPS = 1e-6
MAGIC = 12582912.0
TWO_PI = float(2 * np.pi)
ARENA_F = 47104


def bc(ap, shape):
    return ap.to_broadcast(list(shape))


class Arena:
    def __init__(self, base):
        self.base = base
        self.top = 0
        self.marks = []
        self.peak = 0

    def push(self):
        self.marks.append(self.top)

    def pop(self):
        self.top = self.marks.pop()

    def alloc(self, dims, dt=F32, parts=128):
        if isinstance(dims, int):
            dims = (dims,)
        n = int(np.prod(dims))
        es = 4 if dt == F32 else 2
        nb = (n * es + 63) // 64 * 64
        lo = self.top
        self.top += nb
        self.peak = max(self.peak, self.top)
        assert self.top <= ARENA_F * 4, f"arena overflow {self.top}"
        a = self.base[:, lo // 4:(lo + nb) // 4]
        if dt != F32:
            a = a.bitcast(dt)
        a = a[0:parts, 0:n]
        if len(dims) == 2:
            a = a.rearrange("p (a b) -> p a b", a=dims[0])
        elif len(dims) == 3:
            a = a.rearrange("p (a b c) -> p a b c", a=dims[0], b=dims[1])
        return a


class Ring:
    def __init__(self, bld, nbytes):
        self.b = bld
        self.size = nbytes
        self.region = bld.ar.alloc(nbytes // 4)
        self.plan = bld.plan
        self.dry = bld.kb.dry
        self.newplan = []
        self.issued = 0
        self.cons = 0
        self.head = 0
        self.live = []
        self.aps = {}
        self.held = set()

    def _view(self, lo, parts, n):
        a = self.region[:, lo // 4:(lo + (n * 2 + 3) // 4 * 4) // 4].bitcast(BF16)
        return a[0:parts, 0:n]

    def _try_issue(self):
        src, parts, n, rkeys = self.plan[self.issued]
        if any(k not in self.b.done for k in rkeys):
            return False
        nb = (n * 2 + 63) // 64 * 64
        lo = self.head
        if lo + nb > self.size:
            lo = 0
        hi = lo + nb
        for (_, l2, h2) in self.live:
            if l2 < hi and lo < h2:
                return False
        ap = self._view(lo, parts, n)
        dst = ap
        if len(src.shape) == 3:
            dst = ap.rearrange("p (a b) -> p a b", a=src.shape[1])
        self.b.kb.dma(dst, src, rkeys=rkeys)
        self.live.append((self.issued, lo, hi))
        self.aps[self.issued] = ap
        self.head = hi
        self.issued += 1
        return True

    def unhold(self):
        self.held = set()

    def fetch(self, src, rkeys=(), hold=False):
        parts, n = src.shape[0], int(np.prod(src.shape[1:]))
        if self.dry:
            self.newplan.append((src, parts, n, tuple(rkeys)))
            return self._view(0, parts, n)
        k = self.cons
        if hold:
            self.held.add(k)
        self.live = [x for x in self.live if x[0] >= k or x[0] in self.held]
        while self.issued < len(self.plan) and (self.issued - k) < 24:
            if not self._try_issue():
                break
        assert self.issued > k, "ring too small"
        ps, pp, pn, _ = self.plan[k]
        assert (pp, pn) == (parts, n), "plan mismatch"
        self.cons += 1
        return self.aps.pop(k)


class Group:
    pass


class Builder:
    def __init__(self, S, PAST, plan=None):
        self.S, self.PAST = S, PAST
        self.plan = plan
        self.kb = KB()
        self.kb.dry = plan is None
        self.nc = self.kb.nc
        self.st = ExitStack()
        self.din = {}
        self.dout = {}
        self._evac_i = 0
        self.ps_i = 0
        self.ps_res = set()
        self.debug = False

    def inp(self, name, shape, dt=F32):
        t = self.nc.dram_tensor(name, list(shape), dt, kind="ExternalInput").ap()
        self.din[name] = t
        return t

    def outp(self, name, shape):
        t = self.nc.dram_tensor(name, list(shape), F32, kind="ExternalOutput").ap()
        self.dout[name] = t
        return t

    def scratch(self, name, shape, dt=BF16):
        return self.nc.dram_tensor(name, list(shape), dt).ap()

    def dbg(self, name, ap, dims):
        if name not in self.dout:
            self.outp(name, [128] + list(dims))
        self.ar.push()
        t = self.ar.alloc(dims)
        self.kb.copy(t, ap)
        self.kb.dma(self.dout[name], t)
        self.ar.pop()

    def psum(self):
        while True:
            i = self.ps_i % 8
            self.ps_i += 1
            if i not in self.ps_res:
                return self.ps[i]

    def reserve(self, n):
        out = []
        for i in range(8):
            if i not in self.ps_res and len(out) < n:
                self.ps_res.add(i)
                out.append(i)
        return out

    def release(self, idxs):
        for i in idxs:
            self.ps_res.discard(i)

    def evac(self, out, in_, scale=None):
        self._evac_i += 1
        if self._evac_i % 2 == 0:
            if scale is None:
                self.kb.copy(out, in_, eng='dve')
            else:
                self.kb.ts(out, in_, float(scale), ALU.mult)
        else:
            self.kb.act(out, in_, AF.Copy, scale=1.0 if scale is None else float(scale))

    def declare(self):
        S, PAST = self.S, self.PAST
        i = self.inp
        self.xp = i("xp", [S, D]); self.xs = i("xs", [64, D])
        self.c_lat = i("c_lat", [PAST, 256]); self.c_kr = i("c_kr", [PAST, 64])
        self.st_C = i("st_C", [4, 128, 128]); self.st_n = i("st_n", [4, 128]); self.st_m = i("st_m", [1, 4])
        self.st_re = i("st_re", [16, 128]); self.st_im = i("st_im", [16, 128])
        self.c_sbk = i("c_sbk", [PAST, 512]); self.c_sbv = i("c_sbv", [PAST, 512])
        self.w_gate = i("w_gate", [4, D, DFF]); self.w_up = i("w_up", [4, D, DFF]); self.w_down = i("w_down", [4, DFF, D])
        self.e_win = i("e_win", [D, 2760]); self.e_wout = i("e_wout", [D, D])
        self.wuq_d = i("wuq", [384, 768]); self.wukv_d = i("wukv", [256, 1024])
        self.o_win = i("o_win", [D, 2048]); self.o_wout = i("o_wout", [D, D]); self.wglu_d = i("wglu", [512, 512])
        self.grows = i("grows", [69, 128]); self.bi_d = i("bi", [1, 4]); self.bf_d = i("bf", [1, 4])
        self.s5rows = i("s5rows", [48, 128])
        self.s5B = [i("s5B_re", [2048, 16]), i("s5B_im", [2048, 16])]
        self.s5C = [i("s5C_re", [512, 64]), i("s5C_im", [512, 64])]
        self.c_ident = i("c_ident", [128, 128]); self.c_masksb = i("c_masksb", [128, 128]); self.c_maskml = i("c_maskml", [64, 64])
        self.c_uincl = i("c_uincl", [128, 128]); self.c_sel = i("c_sel", [8, 1024]); self.c_cm = i("c_cm", [128, 512])
        self.c_iota = i("c_iota", [128, 128])
        self.c_ropep = [i("c_cosp", [64, S]), i("c_sinp", [64, S])]
        self.c_ropes = [i("c_coss", [64, 64]), i("c_sins", [64, 64])]
        o = self.outp
        self.groups = []
        for g, (n, nm) in enumerate([(S, "p"), (64, "s")]):
            G = Group()
            G.name = nm; G.n = n
            G.T = 512 if nm == "p" else 64
            G.ntile = n // G.T
            G.npast = 0 if nm == "p" else PAST // 512
            G.pos0 = 0 if nm == "p" else PAST
            G.x = self.xp if nm == "p" else self.xs
            G.rope = self.c_ropep if nm == "p" else self.c_ropes
            G.y = o("y_" + nm, [n, D]); G.lat = o("lat_" + nm, [n, 256]); G.kr = o("kr_" + nm, [n, 64])
            G.oC = o("C_" + nm, [4, 128, 128]); G.on = o("n_" + nm, [4, 128]); G.om = o("m_" + nm, [1, 4])
            G.ore = o("re_" + nm, [16, 128]); G.oim = o("im_" + nm, [16, 128])
            G.ok = o("k_" + nm, [n, 512]); G.ov = o("v_" + nm, [n, 512])
            NT = G.npast + G.ntile
            G.KnT = self.scratch("KnT_" + nm, [NT, 128, 2048]); G.KrT = self.scratch("KrT_" + nm, [NT, 64, 512])
            G.V = self.scratch("V_" + nm, [NT, 128, 2048])
            G.sKT = self.scratch("sKT_" + nm, [NT, 128, 2048]); G.sV = self.scratch("sV_" + nm, [NT, 128, 2048])
            self.groups.append(G)
        sc = self.scratch
        self.wgu = sc("wgu", [4, NJ, 128, 2048]); self.wdn = sc("wdn", [4, 8, 128, DFF])
        self.winF_e = sc("winF_e", [19, 128, 1024]); self.winT_e = sc("winT_e", [2, 128, 4096]); self.wout_e = sc("wout_e", [8, 128, 1024])
        self.winF_o = sc("winF_o", [12, 128, 1024]); self.winT_o = sc("winT_o", [2, 128, 4096]); self.wout_o = sc("wout_o", [8, 128, 1024])
        self.s5tab = sc("s5tab", [128, 4096 + 4096], F32)
        self.s5mat = sc("s5mat", [128, 8192], BF16)
        nc = self.nc
        self.arena = self.st.enter_context(nc.sbuf_tensor("arena", [128, ARENA_F], F32))
        self.ps = [self.st.enter_context(nc.psum_tensor(f"ps{k}", [128, 512], F32)) for k in range(8)]

    def load_consts(self):
        kb, ar = self.kb, self.ar
        A = ar.alloc
        self.ident = A(128); kb.dma(self.ident, self.c_ident)
        t = A(128); kb.dma(t, self.c_masksb)
        self.masksb = A(128, BF16); kb.copy(self.masksb, t)
        self.maskml = A(64, parts=64); kb.dma(self.maskml, self.c_maskml)
        t2 = A(128); kb.dma(t2, self.c_uincl)
        self.uincl = A(128, BF16); kb.copy(self.uincl, t2)
        self.ones = A(128, BF16); kb.memset(self.ones, 1.0)
        self.negones = A(128, BF16); kb.memset(self.negones, -1.0)
        self.negonesf = A(128); kb.memset(self.negonesf, -1.0)
        self.onef = A(8); kb.memset(self.onef, 1.0)
        self.sel = A((8, 128), parts=8); kb.dma(self.sel, self.c_sel.rearrange("r (g p) -> r g p", g=8))
        self.cm = A(512); kb.dma(self.cm, self.c_cm)
        gr = A(128); kb.memset(gr, 0.0); kb.dma(gr[0:69, :], self.grows)
        p = self.psum()
        kb.tr(p[:, 0:128], gr, self.ident)
        self.gains = A(72); kb.copy(self.gains[:, 0:69], p[:, 0:69])
        self.bi = A(4); kb.dma(self.bi, bc(self.bi_d[0:1, :], [128, 4]))
        bfp = A(4); kb.dma(bfp, bc(self.bf_d[0:1, :], [128, 4]))
        self.nbf = A(4); kb.ts(self.nbf, bfp, -1.0, ALU.mult)
        self.Cst = A((4, 129)); self.mst = A(4); self.wst = A((16, 2))
        self.wuq = A((3, 768), BF16); self.wukv = A((2, 1024), BF16); self.wglu = A((4, 512), BF16)

    def conv_rows(self, src, ncols, stores, wkeys):
        kb, ar = self.kb, self.ar
        st = self.cv_f[self.cv_i % 2]; sb = self.cv_b[self.cv_i % 2]
        self.cv_i += 1
        kb.dma(st[:, 0:ncols], src)
        h = (ncols // 2 + 63) // 64 * 64
        engs = ['act', 'dve', 'pool']
        e1 = engs[self.cv_i % 3]; e2 = engs[(self.cv_i + 1) % 3]
        kb.copy(sb[:, 0:h], st[:, 0:h], eng=e1)
        kb.copy(sb[:, h:ncols], st[:, h:ncols], eng=e2)
        for dst, c0, c1, shape in stores:
            v = sb[:, c0:c1]
            if shape is not None:
                v = v.rearrange("p (a b) -> p a b", a=shape[0])
            kb.dma(dst, v, eng='pool', wkeys=wkeys)

    def prologue_weights(self):
        kb, ar = self.kb, self.ar
        ar.push()
        self.cv_f = [ar.alloc(DFF) for _ in range(2)]
        self.cv_b = [ar.alloc(DFF, BF16) for _ in range(2)]
        self.cv_i = 0
        for f in range(4):
            for kc in range(8):
                for g, W in enumerate((self.w_gate, self.w_up)):
                    dst = self.wgu[f].rearrange("j p x -> p j x")[:, :, kc * 256 + g * 128: kc * 256 + g * 128 + 128]
                    self.conv_rows(W[f, kc * 128:(kc + 1) * 128, :], DFF, [(dst, 0, DFF, (NJ, 128))], [("wgu", f)])
            for jc in range(NJ):
                dst = self.wdn[f].rearrange("m p x -> p m x")[:, :, jc * 128:(jc + 1) * 128]
                self.conv_rows(self.w_down[f, jc * 128:(jc + 1) * 128, :], D, [(dst, 0, D, (8, 128))], [("wdn", f)])
        for kc in range(8):
            F = self.winF_e.rearrange("n p x -> p n x")
            ks = slice(kc * 128, (kc + 1) * 128)
            stores = [(F[:, 0:4, ks], 0, 512, (4, 128)), (F[:, 4:8, ks], 512, 1024, (4, 128)),
                      (F[:, 8:12, ks], 1536, 2048, (4, 128)), (F[:, 12:15, ks], 2056, 2440, (3, 128)),
                      (F[:, 15:17, ks], 2440, 2696, (2, 128)),
                      (self.winF_e[17][:, kc * 128:kc * 128 + 64], 2696, 2760, None),
                      (self.winF_e[18][:, kc * 128:kc * 128 + 8], 2048, 2056, None),
                      (self.winT_e[0][:, kc * 512:(kc + 1) * 512], 512, 1024, None),
                      (self.winT_e[1][:, kc * 512:(kc + 1) * 512], 1024, 1536, None)]
            self.conv_rows(self.e_win[ks, :], 2760, stores, [("we",)])
            Fo = self.winF_o.rearrange("n p x -> p n x")
            stores = [(Fo[:, 0:12, ks], 0, 1536, (12, 128)),
                      (self.winT_o[0][:, kc * 512:(kc + 1) * 512], 1024, 1536, None),
                      (self.winT_o[1][:, kc * 512:(kc + 1) * 512], 1536, 2048, None)]
            self.conv_rows(self.o_win[ks, :], 2048, stores, [("wo",)])
            for W, dstt, key in ((self.e_wout, self.wout_e, "we"), (self.o_wout, self.wout_o, "wo")):
                dst = dstt.rearrange("m p x -> p m x")[:, :, ks]
                self.conv_rows(W[ks, :], D, [(dst, 0, D, (8, 128))], [(key,)])
        for kc in range(3):
            stg = self.cv_f[self.cv_i % 2]; self.cv_i += 1
            kb.dma(stg[:, 0:768], self.wuq_d[kc * 128:(kc + 1) * 128, :])
            kb.copy(self.wuq[:, kc, :], stg[:, 0:768], eng='dve')
        for kc in range(2):
            stg = self.cv_f[self.cv_i % 2]; self.cv_i += 1
            kb.dma(stg[:, 0:1024], self.wukv_d[kc * 128:(kc + 1) * 128, :])
            kb.copy(self.wukv[:, kc, :], stg[:, 0:1024], eng='act')
        for kc in range(4):
            stg = self.cv_f[self.cv_i % 2]; self.cv_i += 1
            kb.dma(stg[:, 0:512], self.wglu_d[kc * 128:(kc + 1) * 128, :])
            kb.copy(self.wglu[:, kc, :], stg[:, 0:512], eng='dve')
        ar.pop()

    def rmsnorm(self, xT, C, T, gcol, out, dim):
        kb, ar = self.kb, self.ar
        ar.push()
        sq = ar.alloc((C, T), BF16)
        kb.act(sq, xT, AF.Square)
        p = self.psum()
        for c in range(C):
            kb.mm(p[:, 0:T], self.ones, sq[:, c, :], start=(c == 0), stop=(c == C - 1))
        rs = ar.alloc(T)
        kb.act(rs, p[:, 0:T], AF.Sqrt, scale=1.0 / dim, bias=self.epsc[:, 0:1])
        kb.recip(rs, rs)
        for c in range(C):
            kb.stt(out[:, c, :], xT[:, c, :], self.gains[:, gcol + c:gcol + c + 1], rs, ALU.mult, ALU.mult)
        ar.pop()

    def transpose_out(self, srcT, rows, T, dst, eng_out='pool'):
        kb, ar = self.kb, self.ar
        tb = min(128, T); nb = T // tb
        tot = sum(r for _, r in srcT)
        ar.push()
        stg = ar.alloc((nb, tot), parts=tb)
        c0 = 0
        for ap, r in srcT:
            p = self.psum()
            if r < 128 and tb < 128:
                tmp = ar.alloc(128, parts=r)
                kb.memset(tmp, 0.0)
                kb.copy(tmp[:, 0:tb], ap[0:r, 0:tb])
                kb.tr(p[:, 0:r], tmp, self.ident[0:r, 0:r])
            else:
                for b in range(nb):
                    kb.tr(p[0:tb, b * r:(b + 1) * r], ap[0:r, b * tb:(b + 1) * tb], self.ident[0:r, 0:r])
            self.evac(stg[:, :, c0:c0 + r], p[0:tb, 0:nb * r].rearrange("p (b r) -> p b r", b=nb))
            c0 += r
        kb.dma(dst.rearrange("(b p) f -> p b f", p=tb), stg, eng=eng_out)
        ar.pop()

    def load_xT(self, G, ti, xT):
        kb, ar = self.kb, self.ar
        T = G.T; tb = min(128, T); nb = T // tb
        ar.push()
        stg = ar.alloc((nb, D), parts=tb)
        kb.dma(stg, G.x[ti * T:(ti + 1) * T, :].rearrange("(b p) f -> p b f", p=tb))
        for c in range(8):
            p = self.psum()
            for b in range(nb):
                kb.tr(p[:, b * tb:(b + 1) * tb], stg[:, b, c * 128:(c + 1) * 128], self.ident[0:tb, 0:tb])
            self.evac(xT[:, c, :], p[:, 0:T])
        ar.pop()

    def ffn(self, xT, T, f):
        kb, ar = self.kb, self.ar
        ar.push()
        hn = ar.alloc((8, T), BF16)
        self.rmsnorm(xT, 8, T, f * 8, hn, D)
        a = ar.alloc((NJ, T), BF16)
        sg = [ar.alloc(T) for _ in range(2)]
        for j in range(NJ):
            w = self.ring.fetch(self.wgu[f, j], rkeys=[("wgu", f)]).rearrange("p (k g c) -> p k g c", k=8, g=2)
            pg = self.psum(); pu = self.psum()
            for kc in range(8):
                kb.mm(pg[:, 0:T], w[:, kc, 0, :], hn[:, kc, :], start=(kc == 0), stop=(kc == 7))
            for kc in range(8):
                kb.mm(pu[:, 0:T], w[:, kc, 1, :], hn[:, kc, :], start=(kc == 0), stop=(kc == 7))
            s = sg[j % 2]
            kb.act(s, pg[:, 0:T], AF.Silu)
            kb.tt(a[:, j, :], s, pu[:, 0:T], ALU.mult)
        for m in range(8):
            w = self.ring.fetch(self.wdn[f, m], rkeys=[("wdn", f)]).rearrange("p (j c) -> p j c", j=NJ)
            p = self.psum()
            for j in range(NJ):
                kb.mm(p[:, 0:T], w[:, j, :], a[:, j, :], start=(j == 0), stop=(j == NJ - 1))
            kb.stt(xT[:, m, :], p[:, 0:T], 0.5, xT[:, m, :], ALU.mult, ALU.add)
        ar.pop()

    def out_proj(self, xT, T, mixed, wout, key):
        kb = self.kb
        for m in range(8):
            w = self.ring.fetch(wout[m], rkeys=[(key,)]).rearrange("p (k c) -> p k c", k=8)
            p = self.psum()
            for kc in range(8):
                kb.mm(p[:, 0:T], w[:, kc, :], mixed[:, kc, :], start=(kc == 0), stop=(kc == 7))
            kb.tt(xT[:, m, :], p[:, 0:T], xT[:, m, :], ALU.add)

    def inF(self, winF, ci, key, hn, T, ncols=128):
        kb = self.kb
        w = self.ring.fetch(winF[ci], rkeys=[(key,)]).rearrange("p (k c) -> p k c", k=8)
        p = self.psum()
        for kc in range(8):
            kb.mm(p[0:ncols, 0:T], w[:, kc, 0:ncols], hn[:, kc, :], start=(kc == 0), stop=(kc == 7))
        return p[0:ncols, 0:T]

    def even_mixer(self, G, ti, xT):
        kb, ar = self.kb, self.ar
        T = G.T; NC = T // 64
        kt = G.npast + ti
        ar.push()
        hn = ar.alloc((8, T), BF16)
        self.rmsnorm(xT, 8, T, 32 + 0, hn, D)
        mixed = ar.alloc((8, T), BF16)
        ar.push()
        pgt = self.inF(self.winF_e, 18, "we", hn, T, ncols=8)
        gT = ar.alloc(T, parts=8); kb.copy(gT, pgt, eng='act')
        ga = ar.alloc((4, T)); gb = ar.alloc((4, T))
        for g in range(8):
            p = self.psum()
            kb.mm(p[:, 0:T], self.sel[0:8, g, :], gT[0:8, :])
            h = g % 4
            if g < 4:
                kb.act(ga[:, h, :], p[:, 0:T], AF.Identity, bias=self.bi[:, h:h + 1])
            else:
                kb.act(gb[:, h, :], p[:, 0:T], AF.Exp, scale=-1.0, bias=self.nbf[:, h:h + 1])
        gl = ar.alloc((4, T))
        kb.act(gl, gb, AF.Ln, bias=self.onec[:, 0:1])
        for h in range(4):
            kb.scan(gb[:, h, :], self.cm[:, 0:T], gl[:, h, :], 0.0, ALU.mult, ALU.add)
        kb.tt(ga, ga, gb, ALU.add)
        amax = ar.alloc((4, NC)); mext = ar.alloc((4, NC + 1)); Ac = ar.alloc((4, NC)); wi = ar.alloc((4, NC))
        kb.reduce(amax.rearrange("p h c -> p (h c)"), ga.rearrange("p h (c t) -> p (h c) t", t=64), ALU.max)
        gbend = gb.rearrange("p h (c t) -> p h c t", t=64)[:, :, :, 63]
        kb.copy(mext[:, :, 0], self.mst)
        for h in range(4):
            kb.scan(mext[:, h, 1:NC + 1], amax[:, h, :], gbend[:, h, :], self.mst[:, h:h + 1], ALU.max, ALU.subtract)
        kb.tt(Ac, mext[:, :, 1:NC + 1], gbend, ALU.add)
        kb.tt(wi, mext[:, :, 0:NC], Ac, ALU.subtract)
        kb.act(wi, wi, AF.Exp)
        kb.copy(self.mst, mext[:, :, NC])
        Ab = bc(Ac.rearrange("p h c -> p (h c)").unsqueeze(2), [128, 4 * NC, 64])
        gav = ga.rearrange("p h (c t) -> p (h c) t", t=64); gbv = gb.rearrange("p h (c t) -> p (h c) t", t=64)
        kb.tt(gav, gav, Ab, ALU.subtract); kb.act(ga, ga, AF.Exp)
        kb.tt(gbv, gbv, Ab, ALU.subtract); kb.act(gb, gb, AF.Exp)
        pw = self.psum()
        for c in range(NC):
            for h in range(4):
                kb.mm(pw[0:64, c * 4 + h:c * 4 + h + 1], ga[0:1, h, c * 64:(c + 1) * 64], self.onef[0:1, 0:1])
        wacol = ar.alloc((NC, 4), parts=64)
        kb.copy(wacol.rearrange("p c h -> p (c h)"), pw[0:64, 0:NC * 4])
        qmT = ar.alloc((4, T), BF16); kmT = ar.alloc((4, T), BF16); osig = ar.alloc((4, T))
        for h in range(4):
            p = self.inF(self.winF_e, h, "we", hn, T)
            kb.copy(qmT[:, h, :], p, eng='act')
        for h in range(4):
            p = self.inF(self.winF_e, 4 + h, "we", hn, T)
            kb.stt(kmT[:, h, :], p, 128 ** -0.5, ga[:, h, :], ALU.mult, ALU.mult)
        for h in range(4):
            p = self.inF(self.winF_e, 8 + h, "we", hn, T)
            kb.act(osig[:, h, :], p, AF.Sigmoid)
        ktok = ar.alloc((NC, 4, 128), BF16, parts=64); vaug = ar.alloc((NC, 4, 129), BF16, parts=64)
        kb.memset(vaug[:, :, :, 128:129], 1.0, eng='pool')
        wk = self.ring.fetch(self.winT_e[0], rkeys=[("we",)]).rearrange("p (k n) -> p k n", k=8)
        for c in range(NC):
            p = self.psum()
            for kc in range(8):
                kb.mm(p[0:64, :], hn[:, kc, c * 64:(c + 1) * 64], wk[:, kc, :], start=(kc == 0), stop=(kc == 7))
            for h in range(4):
                kb.ts(ktok[:, c, h, :], p[0:64, h * 128:(h + 1) * 128], wacol[:, c, h:h + 1], ALU.mult, 128 ** -0.5, ALU.mult)
        wv = self.ring.fetch(self.winT_e[1], rkeys=[("we",)]).rearrange("p (k n) -> p k n", k=8)
        for c in range(NC):
            p = self.psum()
            for kc in range(8):
                kb.mm(p[0:64, :], hn[:, kc, c * 64:(c + 1) * 64], wv[:, kc, :], start=(kc == 0), stop=(kc == 7))
            self.evac(vaug[:, c, :, 0:128], p[0:64, :].rearrange("p (h d) -> p h d", h=4))
        hm = ar.alloc((4, T))
        Cs = [ar.alloc((4, 129)) for _ in range(2)]
        Csb = [ar.alloc((4, 128), BF16) for _ in range(2)]
        nrep = [ar.alloc((4, 128), BF16) for _ in range(2)]
        GT = [ar.alloc((4, 64), BF16, parts=64) for _ in range(2)]
        dab = [ar.alloc((4, 64)) for _ in range(2)]
        for c in range(NC):
            b = c % 2
            cs = slice(c * 64, (c + 1) * 64)
            pS = self.psum(); pN = self.psum(); pD = self.psum(); pC0 = self.psum(); pC1 = self.psum()
            for h in range(4):
                kb.ts(Cs[b][:, h, :], self.Cst[:, h, :], wi[:, h, c:c + 1], ALU.mult)
                kb.copy(Csb[b][:, h, :], Cs[b][:, h, 0:128], eng='act')
                kb.copy(nrep[b][:, h, :], bc(Cs[b][:, h, 128:129], [128, 128]), eng='pool')
                kb.mm(pS[0:64, h * 64:(h + 1) * 64], kmT[:, h, cs], qmT[:, h, cs])
            kb.tt(GT[b], pS[0:64, 0:256].rearrange("p (h t) -> p h t", h=4), bc(self.maskml.unsqueeze(1), [64, 4, 64]), ALU.mult)
            for h in range(4):
                kb.mm(pN[:, h * 64:(h + 1) * 64], Csb[b][:, h, :], qmT[:, h, cs], start=True, stop=False)
                kb.mm(pN[:, h * 64:(h + 1) * 64], vaug[:, c, h, 0:128], GT[b][:, h, :], start=False, stop=True)
            for h in range(4):
                kb.mm(pD[:, h * 64:(h + 1) * 64], nrep[b][:, h, :], qmT[:, h, cs], start=True, stop=False)
                kb.mm(pD[:, h * 64:(h + 1) * 64], self.ones[0:64, :], GT[b][:, h, :], start=False, stop=True)
            for h in range(4):
                pc = pC0 if h < 2 else pC1
                kb.mm(pc[:, (h % 2) * 129:(h % 2) * 129 + 129], ktok[:, c, h, :], vaug[:, c, h, :])
            d = dab[b]
            kb.act(d, pD[:, 0:256].rearrange("p (h t) -> p h t", h=4), AF.Abs)
            kb.tt(d, d, gb[:, :, cs], ALU.max)
            kb.recip(d, d)
            kb.tt(hm[:, :, cs], pN[:, 0:256].rearrange("p (h t) -> p h t", h=4), d, ALU.mult)
            kb.tt(self.Cst[:, 0:2, :], pC0[:, 0:258].rearrange("p (h d) -> p h d", h=2), Cs[b][:, 0:2, :], ALU.add)
            kb.tt(self.Cst[:, 2:4, :], pC1[:, 0:258].rearrange("p (h d) -> p h d", h=2), Cs[b][:, 2:4, :], ALU.add)
        sq = kmT
        kb.act(sq, hm, AF.Square)
        rs = gl
        for h in range(4):
            p = self.psum()
            kb.mm(p[:, 0:T], self.ones, sq[:, h, :])
            kb.act(rs[:, h, :], p[:, 0:T], AF.Sqrt, scale=1.0 / 128, bias=self.epsc[:, 0:1])
        kb.recip(rs, rs)
        for h in range(4):
            kb.stt(hm[:, h, :], hm[:, h, :], self.gains[:, 56 + h:57 + h], rs[:, h, :], ALU.mult, ALU.mult)
        kb.tt(mixed[:, 0:4, :], hm, osig, ALU.mult)
        ar.pop()
        ar.push()
        cqT = ar.alloc((3, T)); ckvT = ar.alloc((2, T)); qr5 = ar.alloc((5, T), parts=64)
        for c in range(3):
            p = self.inF(self.winF_e, 12 + c, "we", hn, T); self.evac(cqT[:, c, :], p)
        for c in range(2):
            p = self.inF(self.winF_e, 15 + c, "we", hn, T); self.evac(ckvT[:, c, :], p)
        p = self.inF(self.winF_e, 17, "we", hn, T, ncols=64); self.evac(qr5[:, 4, :], p)
        cqn = ar.alloc((3, T), BF16)
        self.rmsnorm(cqT, 3, T, 60, cqn, 384)
        qnT = ar.alloc((4, T), BF16)
        for h in range(4):
            p = self.psum()
            for kc in range(3):
                kb.mm(p[:, 0:T], self.wuq[:, kc, h * 192:h * 192 + 128], cqn[:, kc, :], start=(kc == 0), stop=(kc == 2))
            kb.copy(qnT[:, h, :], p[:, 0:T], eng='act')
            p2 = self.psum()
            for kc in range(3):
                kb.mm(p2[0:64, 0:T], self.wuq[:, kc, h * 192 + 128:h * 192 + 192], cqn[:, kc, :], start=(kc == 0), stop=(kc == 2))
            kb.copy(qr5[:, h, :], p2[0:64, 0:T], eng='dve')
        cs2 = ar.alloc(T, parts=64); sn2 = ar.alloc(T, parts=64)
        kb.dma(cs2, G.rope[0][:, ti * T:(ti + 1) * T]); kb.dma(sn2, G.rope[1][:, ti * T:(ti + 1) * T])
        qrf = ar.alloc((5, T), parts=64); qrb = ar.alloc((5, T), BF16, parts=64)
        ar.push()
        t1 = ar.alloc((5, T), parts=64); t2 = ar.alloc((5, T), parts=64)
        kb.tt(t1, qr5, bc(cs2.unsqueeze(1), [64, 5, T]), ALU.mult)
        kb.tt(t2[0:32], qr5[32:64], bc(sn2[32:64].unsqueeze(1), [32, 5, T]), ALU.mult)
        kb.tt(t2[32:64], qr5[0:32], bc(sn2[0:32].unsqueeze(1), [32, 5, T]), ALU.mult, eng='pool')
        kb.tt(qrf[0:32], t1[0:32], t2[0:32], ALU.subtract)
        kb.tt(qrf[32:64], t1[32:64], t2[32:64], ALU.add, eng='pool')
        ar.pop()
        kb.copy(qrb, qrf, eng='act')
        latf = ar.alloc((2, T)); latb = ar.alloc((2, T), BF16)
        self.rmsnorm(ckvT, 2, T, 63, latf, 256)
        kb.copy(latb, latf, eng='act')
        r0 = ti * T
        self.transpose_out([(latf[:, 0, :], 128), (latf[:, 1, :], 128)], 256, T, G.lat[r0:r0 + T, :])
        self.transpose_out([(qrf[:, 4, :], 64)], 64, T, G.kr[r0:r0 + T, :])
        self.kv_expand(G, kt, latb, T)
        kb.dma(G.KrT[kt][:, 0:T], qrb[:, 4, :], eng='pool', wkeys=[("kv" + G.name, kt)])
        self.done.add(("kv" + G.name, kt))
        self.mla_attend(G, ti, T, qnT, qrb, mixed)
        ar.pop()
        if G.name == "p" and ti == 0 and self.debug:
            self.dbg("dbg_mixed", mixed, (8, T))
        self.out_proj(xT, T, mixed, self.wout_e, "we")
        ar.pop()

    def kv_expand(self, G, kt, latb, T):
        kb, ar = self.kb, self.ar
        ar.push()
        kn = ar.alloc((4, T), BF16)
        for h in range(4):
            p = self.psum()
            for kc in range(2):
                kb.mm(p[:, 0:T], self.wukv[:, kc, h * 256:h * 256 + 128], latb[:, kc, :], start=(kc == 0), stop=(kc == 1))
            self.evac(kn[:, h, :], p[:, 0:T])
        kb.dma(G.KnT[kt].rearrange("p (h t) -> p h t", h=4)[:, :, 0:T], kn, eng='pool', wkeys=[("kv" + G.name, kt)])
        if self.debug and G.name == "p" and kt == 0:
            self.dbg("dbg_kn", kn, (4, T))
        tb = min(128, T); nb = T // tb
        vt = ar.alloc((nb, 512), BF16, parts=tb)
        wv = self.wukv.rearrange("p k (h x) -> p k h x", h=4)[:, :, :, 128:256]
        for b in range(nb):
            p = self.psum()
            for kc in range(2):
                kb.mm(p[0:tb, :], latb[:, kc, b * tb:(b + 1) * tb], wv[:, kc, :, :], start=(kc == 0), stop=(kc == 1))
            self.evac(vt[:, b, :], p[0:tb, :])
        kb.dma(G.V[kt].rearrange("p (b x) -> p b x", b=4)[0:tb, 0:nb, :], vt, eng='pool', wkeys=[("kv" + G.name, kt)])
        ar.pop()

    def mla_attend(self, G, ti, T, qnT, qrb, mixed):
        kb, ar = self.kb, self.ar
        scale = 192 ** -0.5
        ktc = G.npast + ti
        ar.push()
        Pb = [ar.alloc(T, BF16) for _ in range(3)]
        rden = ar.alloc(T)
        pi = 0
        for h in range(4):
            res = self.reserve(2)
            pO = self.ps[res[0]]; pDn = self.ps[res[1]]
            first = True
            for kt in range(0, ktc + 1):
                diag = (kt == ktc)
                nk = T if diag else 512
                key = [("kv" + G.name, kt)]
                self.ring.unhold()
                knT = self.ring.fetch(G.KnT[kt][:, h * 512:h * 512 + nk], rkeys=key, hold=True)
                krT = self.ring.fetch(G.KrT[kt][:, 0:nk], rkeys=key, hold=True)
                kbs = min(128, nk); nkb = nk // kbs
                vv = self.ring.fetch(G.V[kt].rearrange("p (b x) -> p b x", b=4)[0:kbs, 0:nkb, h * 128:(h + 1) * 128], rkeys=key)
                vv = vv.rearrange("p (b x) -> p b x", b=nkb)
                if self.debug and G.name == "p" and ti == 0 and h < 2:
                    self.dbg("dbg_knT%d" % h, knT, (512,))
                    if h == 0:
                        self.dbg("dbg_qnT", qnT, (4, T))
                for b in range(nkb):
                    ko = b * kbs
                    qs = ko if diag else 0
                    last = diag and (b == nkb - 1)
                    pS = self.psum()
                    kb.mm(pS[0:kbs, qs:T], knT[:, ko:ko + kbs], qnT[:, h, qs:T], start=True, stop=False)
                    kb.mm(pS[0:kbs, qs:T], krT[0:64, ko:ko + kbs], qrb[:, h, qs:T], start=False, stop=True)
                    P = Pb[pi % 3]; pi += 1
                    kb.act(P[0:kbs, qs:T], pS[0:kbs, qs:T], AF.Exp, scale=scale)
                    if diag and kbs == 128:
                        kb.ts(P[64:128, qs:qs + 64], P[64:128, qs:qs + 64], 0.0, ALU.mult)
                    kb.mm(pO[:, qs:T], vv[:, b, :], P[0:kbs, qs:T], start=first, stop=last)
                    kb.mm(pDn[:, qs:T], self.ones[0:kbs, :], P[0:kbs, qs:T], start=first, stop=last)
                    first = False
            kb.recip(rden, pDn[:, 0:T])
            kb.tt(mixed[:, 4 + h, :], pO[:, 0:T], rden, ALU.mult)
            self.release(res)
            self.ring.unhold()
        ar.pop()

    def odd_mixer(self, G, ti, xT):
        kb, ar = self.kb, self.ar
        T = G.T
        kt = G.npast + ti
        ar.push()
        hn = ar.alloc((8, T), BF16)
        self.rmsnorm(xT, 8, T, 32 + 8, hn, D)
        mixed = ar.alloc((8, T), BF16)
        kb.cserial = ("odd" in self.serial_stages) or ("s5" in self.serial_stages)
        ar.push()
        Ts = min(128, T); nsc = T // Ts
        Ec = ar.alloc((16, 128)); Es = ar.alloc((16, 128)); BT = ar.alloc((32, 128), BF16); Cm = ar.alloc((32, 128), BF16)
        kb.dma(Ec, self.s5tab[:, 0:2048].rearrange("p (s t) -> p s t", s=16), rkeys=[("s5",)])
        kb.dma(Es, self.s5tab[:, 4096:4096 + 2048].rearrange("p (s t) -> p s t", s=16), rkeys=[("s5",)])
        kb.dma(BT, self.s5mat[:, 0:4096].rearrange("p (s t) -> p s t", s=32), rkeys=[("s5",)])
        kb.dma(Cm, self.s5mat[:, 4096:8192].rearrange("p (s t) -> p s t", s=32), rkeys=[("s5",)])
        uf = ar.alloc((4, T)); ub = ar.alloc((4, T), BF16)
        for j in range(4):
            p = self.inF(self.winF_o, j, "wo", hn, T)
            kb.copy(uf[:, j, :], p, eng='act'); kb.copy(ub[:, j, :], p, eng='dve')
        yT = ar.alloc((4, T))
        tA = [ar.alloc(T)] * 2; tB = [ar.alloc(T)] * 2
        cR = [ar.alloc(T)] * 2; cI = [ar.alloc(T)] * 2
        wR = [ar.alloc(T)] * 2; wI = [ar.alloc(T)] * 2
        xR = [ar.alloc(T, BF16) for _ in range(4)]; xI = [ar.alloc(T, BF16) for _ in range(4)]
        tmp1 = ar.alloc(4)
        v3 = lambda a: a.rearrange("p (c t) -> p c t", t=Ts)
        for j in range(4):
            for sl in range(4):
                s = 4 * j + sl
                b = s % 2
                pr = self.psum(); pim = self.psum()
                kb.mm(pr[:, 0:T], BT[:, 2 * s, :], ub[:, j, :])
                kb.mm(pim[:, 0:T], BT[:, 2 * s + 1, :], ub[:, j, :])
                ecb = bc(Ec[:, s, 0:Ts].unsqueeze(1), [128, nsc, Ts]); esb = bc(Es[:, s, 0:Ts].unsqueeze(1), [128, nsc, Ts])
                kb.tt(v3(tA[b]), v3(pr[:, 0:T]), ecb, ALU.mult)
                kb.tt(v3(tB[b]), v3(pim[:, 0:T]), esb, ALU.mult)
                kb.tt(cR[b], tA[b], tB[b], ALU.add, eng='pool')
                kb.tt(v3(tA[b]), v3(pim[:, 0:T]), ecb, ALU.mult)
                kb.tt(v3(tB[b]), v3(pr[:, 0:T]), esb, ALU.mult)
                kb.tt(cI[b], tA[b], tB[b], ALU.subtract, eng='pool')
                rb = bc(self.rdec[:, s:s + 1], [128, Ts])
                for sc in range(nsc):
                    cs = slice(sc * Ts, (sc + 1) * Ts)
                    kb.scan(wR[b][:, cs], rb, cR[b][:, cs], self.wst[:, s, 0:1], ALU.mult, ALU.add)
                    kb.scan(wI[b][:, cs], rb, cI[b][:, cs], self.wst[:, s, 1:2], ALU.mult, ALU.add)
                    e = sc * Ts + Ts - 1
                    ecl = Ec[:, s, Ts - 1:Ts]; esl = Es[:, s, Ts - 1:Ts]
                    kb.ts(tmp1[:, 0:1], wI[b][:, e:e + 1], esl, ALU.mult)
                    kb.ts(tmp1[:, 1:2], wR[b][:, e:e + 1], esl, ALU.mult)
                    kb.stt(self.wst[:, s, 0:1], wR[b][:, e:e + 1], ecl, tmp1[:, 0:1], ALU.mult, ALU.subtract)
                    kb.stt(self.wst[:, s, 1:2], wI[b][:, e:e + 1], ecl, tmp1[:, 1:2], ALU.mult, ALU.add)
                xi = (s % 4)
                kb.tt(v3(tA[b]), v3(wR[b]), ecb, ALU.mult)
                kb.tt(v3(tB[b]), v3(wI[b]), esb, ALU.mult, eng='pool')
                kb.tt(xR[xi], tA[b], tB[b], ALU.subtract)
                kb.tt(v3(cR[b]), v3(wI[b]), ecb, ALU.mult, eng='pool')
                kb.tt(v3(cI[b]), v3(wR[b]), esb, ALU.mult)
                kb.tt(xI[xi], cR[b], cI[b], ALU.add, eng='pool')
            py = self.psum()
            for sl in range(4):
                s = 4 * j + sl
                kb.mm(py[:, 0:T], Cm[:, 2 * s, :], xR[s % 4], start=(sl == 0), stop=False)
                kb.mm(py[:, 0:T], Cm[:, 2 * s + 1, :], xI[s % 4], start=False, stop=(sl == 3))
            kb.stt(yT[:, j, :], uf[:, j, :], self.gains[:, 65 + j:66 + j], py[:, 0:T], ALU.mult, ALU.add)
        gq = ar.alloc((4, T)); gg = ar.alloc((4, T)); gbf = ar.alloc((4, T), BF16)
        kb.act(gq, yT, AF.Square)
        kb.ts(gq, gq, 0.044715, ALU.mult, 1.0, ALU.add)
        kb.tt(gq, gq, yT, ALU.mult)
        kb.act(gq, gq, AF.Sigmoid, scale=2.0 * 0.7978845608028654)
        kb.tt(gg, gq, yT, ALU.mult)
        kb.copy(gbf, gg, eng='act')
        for m in range(4):
            p = self.psum()
            for kc in range(4):
                kb.mm(p[:, 0:T], self.wglu[:, kc, m * 128:(m + 1) * 128], gbf[:, kc, :], start=(kc == 0), stop=(kc == 3))
            kb.act(gq[:, m, :], p[:, 0:T], AF.Sigmoid)
        kb.tt(mixed[:, 0:4, :], gg, gq, ALU.mult)
        ar.pop()
        kb.cserial = ("odd" in self.serial_stages) or ("sb" in self.serial_stages)
        ar.push()
        qT = ar.alloc((2, 4, T), BF16); kT = ar.alloc((4, T), BF16)
        kb.memset(qT, 0.0)
        for j in range(4):
            p = self.inF(self.winF_o, 4 + j, "wo", hn, T)
            kb.act(qT[0:64, 0, j, :], p[0:64, :], AF.Copy, scale=0.125)
            kb.act(qT[64:128, 1, j, :], p[64:128, :], AF.Copy, scale=0.125)
        for j in range(4):
            p = self.inF(self.winF_o, 8 + j, "wo", hn, T)
            self.evac(kT[:, j, :], p)
        kb.dma(G.sKT[kt].rearrange("p (j t) -> p j t", j=4)[:, :, 0:T], kT, eng='pool', wkeys=[("sb" + G.name, kt)])
        tb = min(128, T); nb = T // tb
        r0 = ti * T
        for which, (dst, oo) in enumerate(((None, G.ok), (G.sV, G.ov))):
            w = self.ring.fetch(self.winT_o[which], rkeys=[("wo",)]).rearrange("p (k n) -> p k n", k=8)
            ar.push()
            stf = ar.alloc((nb, 512), parts=tb)
            stb = ar.alloc((nb, 512), BF16, parts=tb)
            for b in range(nb):
                p = self.psum()
                for kc in range(8):
                    kb.mm(p[0:tb, :], hn[:, kc, b * tb:(b + 1) * tb], w[:, kc, :], start=(kc == 0), stop=(kc == 7))
                kb.copy(stf[:, b, :], p[0:tb, :], eng='act')
                if dst is not None:
                    kb.copy(stb[:, b, :], p[0:tb, :], eng='dve')
            kb.dma(oo[r0:r0 + T, :].rearrange("(b p) f -> p b f", p=tb), stf, eng='pool')
            if dst is not None:
                kb.dma(dst[kt].rearrange("p (b x) -> p b x", b=4)[0:tb, 0:nb, :], stb, eng='pool', wkeys=[("sb" + G.name, kt)])
            ar.pop()
        self.done.add(("sb" + G.name, kt))
        self.sb_attend(G, ti, T, qT, mixed)
        ar.pop()
        self.out_proj(xT, T, mixed, self.wout_o, "wo")
        ar.pop()

    def sb_attend(self, G, ti, T, qT, mixed):
        kb, ar = self.kb, self.ar
        ktc = G.npast + ti
        ar.push()
        Eb = [ar.alloc(T) for _ in range(2)]
        Lb = [ar.alloc(T, BF16) for _ in range(2)]
        Wb = [ar.alloc(T, BF16) for _ in range(2)]
        Racc = [ar.alloc(T) for _ in range(2)]
        for j in range(4):
            res = self.reserve(2)
            pO = [self.ps[res[0]], self.ps[res[1]]]
            for e in range(2):
                kb.memset(Racc[e], 0.0, eng='pool')
            first = True
            for kt in range(ktc, -1, -1):
                diag = (kt == ktc)
                nk = T if diag else 512
                key = [("sb" + G.name, kt)]
                self.ring.unhold()
                kT = self.ring.fetch(G.sKT[kt][:, j * 512:j * 512 + nk], rkeys=key, hold=True)
                kbs = min(128, nk); nkb = nk // kbs
                vv = self.ring.fetch(G.sV[kt].rearrange("p (b x) -> p b x", b=4)[0:kbs, 0:nkb, j * 128:(j + 1) * 128], rkeys=key)
                vv = vv.rearrange("p (b x) -> p b x", b=nkb)
                for b in range(nkb - 1, -1, -1):
                    ko = b * kbs
                    qs = ko if diag else 0
                    last = (kt == 0 and b == 0)
                    pZ = [self.psum(), self.psum()]
                    for e in range(2):
                        kb.mm(pZ[e][0:kbs, qs:T], kT[:, ko:ko + kbs], qT[:, e, j, qs:T], start=True, stop=False)
                    for e in range(2):
                        kb.act(Eb[e][0:kbs, qs:T], pZ[e][0:kbs, qs:T], AF.Exp)
                    for e in range(2):
                        kb.act(Lb[e][0:kbs, qs:T], Eb[e][0:kbs, qs:T], AF.Ln, bias=self.onec[0:kbs, 0:1])
                    if diag:
                        for e in range(2):
                            kb.tt(Lb[e][0:kbs, qs:qs + kbs], Lb[e][0:kbs, qs:qs + kbs], self.masksb[0:kbs, 0:kbs], ALU.mult, eng='pool')
                    for e in range(2):
                        kb.mm(pZ[e][0:kbs, qs:T], self.uincl[0:kbs, 0:kbs], Lb[e][0:kbs, qs:T], start=False, stop=first)
                        if not first:
                            kb.mm(pZ[e][0:kbs, qs:T], self.negonesf[:, 0:kbs], Racc[e][:, qs:T], start=False, stop=True)
                    for e in range(2):
                        kb.act(Wb[e][0:kbs, qs:T], pZ[e][0:kbs, qs:T], AF.Exp)
                    if diag:
                        for e in range(2):
                            kb.tt(Wb[e][0:kbs, qs:qs + kbs], Wb[e][0:kbs, qs:qs + kbs], self.masksb[0:kbs, 0:kbs], ALU.mult, eng='pool')
                    for e in range(2):
                        kb.mm(pO[e][0:64, qs:T], vv[:, b, 64 * e:64 * e + 64], Wb[e][0:kbs, qs:T], start=first, stop=last)
                    if not last:
                        for e in range(2):
                            kb.tt(Racc[e][0:kbs, qs:T], Racc[e][0:kbs, qs:T], Lb[e][0:kbs, qs:T], ALU.add, eng='pool')
                    first = False
            for e in range(2):
                kb.copy(mixed[64 * e:64 * e + 64, 4 + j, :], pO[e][0:64, 0:T], eng='act')
            self.release(res)
            self.ring.unhold()
        ar.pop()


RING_BYTES = 32768


def _sin_of(self, out, ang, shift, shape):
    kb, ar = self.kb, self.ar
    ar.push()
    t = ar.alloc(shape); k = ar.alloc(shape)
    kb.ts(t, ang, float(shift), ALU.add)
    kb.ts(k, t, 1.0 / TWO_PI, ALU.mult, MAGIC, ALU.add)
    kb.ts(k, k, -MAGIC, ALU.add)
    kb.stt(t, k, -TWO_PI, t, ALU.mult, ALU.add)
    kb.ts(t, t, float(np.pi), ALU.min, -float(np.pi), ALU.max)
    kb.act(out, t, AF.Sin)
    ar.pop()


def prologue_s5(self):
    kb, ar = self.kb, self.ar
    A = ar.alloc
    self.rdec = A(16)
    ar.push()
    rows = A(128); kb.memset(rows, 0.0); kb.dma(rows[0:48, :], self.s5rows)
    p = self.psum(); kb.tr(p[:, 0:128], rows, self.ident)
    prm = A(48); kb.copy(prm, p[:, 0:48])
    a_r = prm[:, 0:16]; a_i = prm[:, 16:32]; ldt = prm[:, 32:48]
    dt = A(16); dar = A(16); dai = A(16)
    kb.act(dt, ldt, AF.Exp)
    kb.tt(dar, dt, a_r, ALU.mult); kb.tt(dai, dt, a_i, ALU.mult)
    kb.act(self.rdec, dar, AF.Exp)
    cs = A(16); sn = A(16)
    _sin_of(self, sn, dai, 0.0, 16); _sin_of(self, cs, dai, np.pi / 2, 16)
    abr = A(16); abi = A(16); den = A(16); t1 = A(16); t2 = A(16); cr = A(16); ci = A(16)
    kb.tt(abr, self.rdec, cs, ALU.mult); kb.tt(abi, self.rdec, sn, ALU.mult)
    kb.tt(t1, a_r, a_r, ALU.mult); kb.tt(t2, a_i, a_i, ALU.mult); kb.tt(den, t1, t2, ALU.add); kb.recip(den, den)
    kb.ts(abr, abr, -1.0, ALU.add)
    kb.tt(t1, abr, a_r, ALU.mult); kb.tt(t2, abi, a_i, ALU.mult); kb.tt(cr, t1, t2, ALU.add); kb.tt(cr, cr, den, ALU.mult)
    kb.tt(t1, abi, a_r, ALU.mult); kb.tt(t2, abr, a_i, ALU.mult); kb.tt(ci, t1, t2, ALU.subtract); kb.tt(ci, ci, den, ALU.mult)
    Bre = A((16, 16)); Bim = A((16, 16))
    kb.dma(Bre, self.s5B[0].rearrange("(s q) c -> q s c", q=128)); kb.dma(Bim, self.s5B[1].rearrange("(s q) c -> q s c", q=128))
    crb = bc(cr.unsqueeze(2), [128, 16, 16]); cib = bc(ci.unsqueeze(2), [128, 16, 16])
    u1 = A((16, 16)); u2 = A((16, 16)); bbr = A((16, 16)); bbi = A((16, 16))
    kb.tt(u1, Bre, crb, ALU.mult); kb.tt(u2, Bim, cib, ALU.mult); kb.tt(bbr, u1, u2, ALU.subtract)
    kb.tt(u1, Bim, crb, ALU.mult); kb.tt(u2, Bre, cib, ALU.mult); kb.tt(bbi, u1, u2, ALU.add)
    bb2 = A((16, 2, 32)); kb.memset(bb2, 0.0)
    for ri, bb in enumerate((bbr, bbi)):
        kb.copy(bb2[0:64, :, ri, 0:16], bb[0:64]); kb.copy(bb2[64:128, :, ri, 16:32], bb[64:128])
    BT = A((32, 128), BF16); kb.memset(BT, 0.0)
    for s in range(16):
        p = self.psum()
        for ri in range(2):
            kb.tr(p[0:32, ri * 128:(ri + 1) * 128], bb2[:, s, ri, :], self.ident)
        a = s % 4
        kb.copy(BT[32 * a:32 * a + 32, 2 * s:2 * s + 2, :], p[0:32, 0:256].rearrange("p (r c) -> p r c", r=2))
    kb.dma(self.s5mat[:, 0:4096].rearrange("p (s t) -> p s t", s=32), BT, eng='pool', wkeys=[("s5",)])
    Cm = A((32, 128), BF16); kb.memset(Cm, 0.0)
    for ri in range(2):
        Cin = A((4, 64)); kb.dma(Cin, self.s5C[ri].rearrange("(j q) p -> q j p", q=128))
        CT = A((4, 128), parts=64)
        for j in range(4):
            p = self.psum()
            kb.tr(p[0:64, 0:128], Cin[:, j, :], self.ident)
            kb.act(CT[:, j, :], p[0:64, 0:128], AF.Copy, scale=(1.0 if ri == 0 else -1.0))
        for e in range(2):
            for sl in range(4):
                c0 = (2 * sl + e) * 16
                kb.copy(Cm[64 * e:64 * e + 64, ri + 2 * sl:32:8, c0:c0 + 16], CT[:, :, c0:c0 + 16])
    kb.dma(self.s5mat[:, 4096:8192].rearrange("p (s t) -> p s t", s=32), Cm, eng='pool', wkeys=[("s5",)])
    io = A(128); kb.dma(io, self.c_iota)
    ang = A((16, 128)); tb = A((16, 128))
    for s in range(16):
        kb.ts(ang[:, s, :], io, dai[:, s:s + 1], ALU.mult)
    _sin_of(self, tb, ang, np.pi / 2, (16, 128))
    kb.dma(self.s5tab[:, 0:2048].rearrange("p (s t) -> p s t", s=16), tb, eng='pool', wkeys=[("s5",)])
    tb2 = A((16, 128))
    _sin_of(self, tb2, ang, 0.0, (16, 128))
    kb.dma(self.s5tab[:, 4096:4096 + 2048].rearrange("p (s t) -> p s t", s=16), tb2, eng='pool', wkeys=[("s5",)])
    ar.pop()


def sample_prep(self, G):
    kb, ar = self.kb, self.ar
    A = ar.alloc
    for kt in range(G.npast):
        ar.push()
        rs = slice(kt * 512, (kt + 1) * 512)
        stg = A((4, 256)); kb.dma(stg, self.c_lat[rs, :].rearrange("(b p) f -> p b f", p=128))
        latb = A((2, 512), BF16)
        for c in range(2):
            p = self.psum()
            for b in range(4):
                kb.tr(p[:, b * 128:(b + 1) * 128], stg[:, b, c * 128:(c + 1) * 128], self.ident)
            self.evac(latb[:, c, :], p[:, 0:512])
        self.kv_expand(G, kt, latb, 512)
        stg2 = A((4, 64)); kb.dma(stg2, self.c_kr[rs, :].rearrange("(b p) f -> p b f", p=128))
        p = self.psum()
        for b in range(4):
            kb.tr(p[0:64, b * 128:(b + 1) * 128], stg2[:, b, :], self.ident)
        krb = A(512, BF16, parts=64); self.evac(krb, p[0:64, 0:512])
        kb.dma(G.KrT[kt], krb, eng='pool', wkeys=[("kv" + G.name, kt)])
        stk = A((4, 512)); kb.dma(stk, self.c_sbk[rs, :].rearrange("(b p) f -> p b f", p=128))
        kT = A((4, 512), BF16)
        for j in range(4):
            p = self.psum()
            for b in range(4):
                kb.tr(p[:, b * 128:(b + 1) * 128], stk[:, b, j * 128:(j + 1) * 128], self.ident)
            self.evac(kT[:, j, :], p[:, 0:512])
        kb.dma(G.sKT[kt].rearrange("p (j t) -> p j t", j=4), kT, eng='pool', wkeys=[("sb" + G.name, kt)])
        stv = A((4, 512)); kb.dma(stv, self.c_sbv[rs, :].rearrange("(b p) f -> p b f", p=128))
        vb = A((4, 512), BF16); kb.copy(vb, stv, eng='pool')
        kb.dma(G.sV[kt].rearrange("p (b x) -> p b x", b=4), vb, eng='pool', wkeys=[("sb" + G.name, kt)])
        self.done.add(("kv" + G.name, kt)); self.done.add(("sb" + G.name, kt))
        ar.pop()
    ar.push()
    stC = A((4, 128)); kb.dma(stC, self.st_C.rearrange("h v k -> v h k"))
    for h in range(4):
        p = self.psum(); kb.tr(p[:, 0:128], stC[:, h, :], self.ident)
        self.evac(self.Cst[:, h, 0:128], p[:, 0:128])
    rows = A(128); kb.memset(rows, 0.0); kb.dma(rows[0:4, :], self.st_n)
    p = self.psum(); kb.tr(p[:, 0:128], rows, self.ident)
    kb.copy(self.Cst[:, :, 128], p[:, 0:4])
    kb.dma(self.mst, bc(self.st_m[0:1, :], [128, 4]))
    rows2 = A(128); kb.memset(rows2, 0.0); kb.dma(rows2[0:16, :], self.st_re); kb.dma(rows2[16:32, :], self.st_im)
    p = self.psum(); kb.tr(p[:, 0:128], rows2, self.ident)
    kb.copy(self.wst[:, :, 0], p[:, 0:16]); kb.copy(self.wst[:, :, 1], p[:, 16:32])
    ar.pop()


def write_states(self, G):
    kb, ar = self.kb, self.ar
    A = ar.alloc
    ar.push()
    stg = A((4, 128))
    for h in range(4):
        p = self.psum(); kb.tr(p[:, 0:128], self.Cst[:, h, 0:128], self.ident)
        self.evac(stg[:, h, :], p[:, 0:128])
    kb.dma(G.oC.rearrange("h v k -> v h k"), stg, eng='pool')
    tmp = A(4); kb.copy(tmp, self.Cst[:, :, 128])
    p = self.psum(); kb.tr(p[0:4, 0:128], tmp, self.ident)
    st2 = A(128, parts=4); kb.copy(st2, p[0:4, 0:128]); kb.dma(G.on, st2, eng='pool')
    kb.dma(G.om, self.mst[0:1, :], eng='pool')
    tmp2 = A(32); kb.copy(tmp2[:, 0:16], self.wst[:, :, 0]); kb.copy(tmp2[:, 16:32], self.wst[:, :, 1])
    p = self.psum(); kb.tr(p[0:32, 0:128], tmp2, self.ident)
    st3 = A(128, parts=32); kb.copy(st3, p[0:32, 0:128])
    kb.dma(G.ore, st3[0:16, :], eng='pool'); kb.dma(G.oim, st3[16:32, :], eng='pool')
    ar.pop()


def final_out(self, G, ti, xT):
    kb, ar = self.kb, self.ar
    T = G.T
    ar.push()
    yT = ar.alloc((8, T))
    self.rmsnorm(xT, 8, T, 48, yT, D)
    r0 = ti * T
    self.transpose_out([(yT[:, c, :], 128) for c in range(8)], D, T, G.y[r0:r0 + T, :])
    ar.pop()


def build(self, stages=("ffn0", "even", "ffn1", "ffn2", "odd", "ffn3")):
    kb = self.kb
    import os as _os
    self.serial_stages = tuple(_os.environ.get("KSERIAL", "odd").split(","))
    self.declare()
    plan = None
    for pass_ in (0, 1):
        kb.dry = (pass_ == 0)
        self.plan = plan
        self.ar = ar = Arena(self.arena)
        self.ps_i = 0; self.ps_res = set(); self._evac_i = 0
        self.done = set()
        self.load_consts()
        self.epsc = ar.alloc(1); kb.memset(self.epsc, EPS)
        self.onec = ar.alloc(1); kb.memset(self.onec, 1.0)
        self.ring = Ring(self, RING_BYTES)
        xTfull = ar.alloc((8, 512))
        self.prologue_weights()
        prologue_s5(self)
        self.done.update([("wgu", f) for f in range(4)] + [("wdn", f) for f in range(4)] + [("we",), ("wo",), ("s5",)])
        for G in self.groups:
            if G.name == "p":
                kb.memset(self.Cst, 0.0); kb.memset(self.mst, 0.0); kb.memset(self.wst, 0.0)
            else:
                sample_prep(self, G)
            for ti in range(G.ntile):
                xT = xTfull[:, :, 0:G.T]
                self.load_xT(G, ti, xT)
                SER = self.serial_stages
                kb.cserial = "ffn" in SER
                if "ffn0" in stages: self.ffn(xT, G.T, 0)
                kb.cserial = "even" in SER
                if "even" in stages: self.even_mixer(G, ti, xT)
                kb.cserial = "ffn" in SER
                if "ffn1" in stages: self.ffn(xT, G.T, 1)
                if "ffn2" in stages: self.ffn(xT, G.T, 2)
                kb.cserial = "odd" in SER
                if "odd" in stages: self.odd_mixer(G, ti, xT)
                kb.cserial = "ffn" in SER
                if "ffn3" in stages: self.ffn(xT, G.T, 3)
                kb.cserial = True
                final_out(self, G, ti, xT)
            write_states(self, G)
        if pass_ == 0:
            plan = self.ring.newplan
    print("arena peak", self.ar.peak, "ops", len(kb.ops), "plan", len(plan))
    kb.analyze()
    print(kb.stats())
    kb.emit(self.st)
    self.st.close()
    return self.nc


def rope_tables(pos):
    half = 32
    inv = (np.float32(10000.0) ** (-np.arange(half, dtype=np.float32) / np.float32(half))).astype(np.float32)
    ang = pos.astype(np.float32)[None, :] * inv[:, None]
    c = np.cos(ang).astype(np.float32); s = np.sin(ang).astype(np.float32)
    return np.concatenate([c, c], 0), np.concatenate([s, s], 0)


def host_consts(S, PAST):
    k = np.arange(128)
    c = {}
    c["c_ident"] = np.eye(128, dtype=np.float32)
    c["c_masksb"] = (k[:, None] < k[None, :]).astype(np.float32)
    c["c_maskml"] = (k[:64, None] <= k[None, :64]).astype(np.float32)
    c["c_uincl"] = -(k[:, None] >= k[None, :]).astype(np.float32)
    sel = np.zeros((8, 8, 128), np.float32)
    for g in range(8):
        sel[g, g, :] = 1.0
    c["c_sel"] = sel.reshape(8, 1024)
    cm = np.ones((128, 512), np.float32); cm[:, ::64] = 0.0
    c["c_cm"] = cm
    c["c_iota"] = np.tile(np.arange(1, 129, dtype=np.float32)[None, :], (128, 1))
    c["c_cosp"], c["c_sinp"] = rope_tables(np.arange(S))
    c["c_coss"], c["c_sins"] = rope_tables(PAST + np.arange(64))
    return c


def make_inmaps(inputs, ncores, S, PAST):
    f = lambda a: np.ascontiguousarray(np.asarray(a), dtype=np.float32)
    I = {k: np.asarray(v) for k, v in inputs.items()}
    shared = dict(
        w_gate=f(I["ffn_w_gate"].reshape(4, D, DFF)), w_up=f(I["ffn_w_up"].reshape(4, D, DFF)),
        w_down=f(I["ffn_w_down"].reshape(4, DFF, D)),
        e_win=f(I["even_w_in"][0]), e_wout=f(I["even_w_out"][0]), wuq=f(I["mla_w_uq"][0]), wukv=f(I["mla_w_ukv"][0]),
        o_win=f(I["odd_w_in"][0]), o_wout=f(I["odd_w_out"][0]), wglu=f(I["s5_w_glu"][0]),
        grows=f(np.concatenate([I["norm_ffn"].reshape(32, 128), I["norm_mix"].reshape(16, 128), I["norm_final"].reshape(8, 128),
                                I["mlstm_out_norm"][0].reshape(4, 128), I["mla_q_norm"][0].reshape(3, 128),
                                I["mla_kv_norm"][0].reshape(2, 128), I["s5_D"][0].reshape(4, 128)], 0)),
        bi=f(I["mlstm_b_igate"][0][None]), bf=f(I["mlstm_b_fgate"][0][None]),
        s5rows=f(np.concatenate([I["s5_A_re"][0].reshape(16, 128), I["s5_A_im"][0].reshape(16, 128),
                                 np.repeat(I["s5_log_dt"][0][:, None], 64, axis=1).reshape(16, 128)], 0)),
        s5B_re=f(I["s5_B_re"][0].reshape(2048, 16)), s5B_im=f(I["s5_B_im"][0].reshape(2048, 16)),
        s5C_re=f(I["s5_C_re"][0].reshape(512, 64)), s5C_im=f(I["s5_C_im"][0].reshape(512, 64)),
    )
    shared.update(host_consts(S, PAST))
    maps = []
    for b in range(ncores):
        m = dict(shared)
        m.update(xp=f(I["x_prompt"][b]), xs=f(I["x_sample"][b]), c_lat=f(I["cache_mla_latent"][0, b]),
                 c_kr=f(I["cache_mla_krope"][0, b]), st_C=f(I["state_mlstm_C"][0, b]), st_n=f(I["state_mlstm_n"][0, b]),
                 st_m=f(I["state_mlstm_m"][0, b][None]), st_re=f(I["state_s5_re"][0, b].reshape(16, 128)),
                 st_im=f(I["state_s5_im"][0, b].reshape(16, 128)), c_sbk=f(I["cache_sb_k"][0, b].reshape(PAST, 512)),
                 c_sbv=f(I["cache_sb_v"][0, b].reshape(PAST, 512)))
        maps.append(m)
    return maps


def assemble(results, S):
    def st(name, shape):
        return np.stack([np.asarray(r[name], dtype=np.float32).reshape(shape) for r in results], 0)
    outs = [st("y_p", (S, D)), st("y_s", (64, D))]
    for nm, n in (("p", S), ("s", 64)):
        outs += [st("lat_" + nm, (n, 256))[None], st("kr_" + nm, (n, 64))[None], st("C_" + nm, (4, 128, 128))[None],
                 st("n_" + nm, (4, 128))[None], st("m_" + nm, (4,))[None], st("re_" + nm, (32, 64))[None],
                 st("im_" + nm, (32, 64))[None], st("k_" + nm, (n, 8, 64))[None], st("v_" + nm, (n, 8, 64))[None]]
    return tuple(outs)


_S, _PAST = 8192, 2048


def kernel(**inputs):
    ncores = 8
    b = Builder(_S, _PAST)
    nc = build(b)
    maps = make_inmaps(inputs, ncores, _S, _PAST)
    res = run_bass_kernel_spmd(nc, maps, core_ids=list(range(ncores)))
    return assemble(res.results, _S)
```
